# Optimizing a Trainium2 kernel written in Bass

```python
import math
import jax, jax.numpy as jnp
from jax import lax
import numpy as np

D_MODEL = 2048
BATCH = 16
SEQ = 2048
DEPTH = 4

N_BRANCHES = 4
BRANCH_WIDTH = D_MODEL // 4
FOX_HEAD_DIM = 128
FOX_HEADS = BRANCH_WIDTH // FOX_HEAD_DIM
DIFF_V_DIM = 128
DIFF_HEADS = BRANCH_WIDTH // DIFF_V_DIM
DIFF_QK_DIM = DIFF_V_DIM // 2
SSD_HEAD_DIM = 64
SSD_HEADS = BRANCH_WIDTH // SSD_HEAD_DIM
SSD_GROUPS = 2
SSD_STATE = 128
SSD_CONV = 4
SSD_CHUNK = 128
SSD_CONV_DIM = BRANCH_WIDTH + 2 * SSD_GROUPS * SSD_STATE
S5_GROUP_CH = 16
S5_GROUPS = BRANCH_WIDTH // S5_GROUP_CH
S5_STATE = 64

Q_BLOCK = 128
ROPE_THETA = 10000.0
NORM_EPS = 1e-6

SEGMENT_SIZES = (
    BRANCH_WIDTH, BRANCH_WIDTH, BRANCH_WIDTH, FOX_HEADS, BRANCH_WIDTH,
    BRANCH_WIDTH, BRANCH_WIDTH, BRANCH_WIDTH, BRANCH_WIDTH,
    BRANCH_WIDTH, SSD_CONV_DIM, SSD_HEADS,
    BRANCH_WIDTH, BRANCH_WIDTH,
    N_BRANCHES * D_MODEL,
)
IN_COLS = sum(SEGMENT_SIZES)

kernel_name = 'hybrid_fox_diff_ssd_s5_gated_merge'

F32 = jnp.float32


def _rmsnorm(x, g):
    xf = x.astype(F32)
    y = xf * lax.rsqrt(jnp.mean(xf * xf, axis=-1, keepdims=True) + NORM_EPS)
    return (y * g.astype(F32)).astype(x.dtype)


def _combined_projection(xn, w):
    outs = []
    off = 0
    for size in SEGMENT_SIZES:
        outs.append(xn @ w[:, off:off + size])
        off += size
    return outs


def _rope(t, pos):
    d = t.shape[-1]
    half = d // 2
    inv = ROPE_THETA ** (-jnp.arange(half, dtype=F32) / half)
    ang = pos[:, None] * inv[None, :]
    shp = (1, t.shape[1]) + (1,) * (t.ndim - 3) + (half,)
    cos = jnp.cos(ang).reshape(shp)
    sin = jnp.sin(ang).reshape(shp)
    tf = t.astype(F32)
    t1, t2 = tf[..., :half], tf[..., half:]
    return jnp.concatenate([t1 * cos - t2 * sin, t1 * sin + t2 * cos], axis=-1).astype(t.dtype)


def _causal_block_probs(q_blk, k_pre, q_start, bias=None):
    scale = q_blk.shape[-1] ** -0.5
    s = jnp.einsum('bhqd,bhkd->bhqk', q_blk.astype(F32), k_pre.astype(F32)) * scale
    if bias is not None:
        s = s + bias
    qpos = q_start + jnp.arange(q_blk.shape[2])
    kpos = jnp.arange(k_pre.shape[2])
    s = jnp.where(qpos[:, None] >= kpos[None, :], s, -jnp.inf)
    return jax.nn.softmax(s, axis=-1)


def _forgetting_attention(q, k, v, f_logit):
    Bsz, S, H, d = q.shape
    F = jnp.cumsum(jax.nn.log_sigmoid(f_logit.astype(F32)), axis=1).transpose(0, 2, 1)
    qh, kh, vh = (t.transpose(0, 2, 1, 3) for t in (q, k, v))
    outs = []
    for start in range(0, S, Q_BLOCK):
        end = start + Q_BLOCK
        bias = F[:, :, start:end, None] - F[:, :, None, :end]
        p = _causal_block_probs(qh[:, :, start:end], kh[:, :, :end], start, bias)
        outs.append(jnp.einsum('bhqk,bhkd->bhqd', p, vh[:, :, :end].astype(F32)))
    o = jnp.concatenate(outs, axis=2).transpose(0, 2, 1, 3)
    return o.reshape(Bsz, S, H * d).astype(q.dtype)


def _differential_attention(q, k, v, lam, sub_gain, lambda_init):
    Bsz, S, H, _, _ = q.shape
    q1 = q[:, :, :, 0].transpose(0, 2, 1, 3)
    q2 = q[:, :, :, 1].transpose(0, 2, 1, 3)
    k1 = k[:, :, :, 0].transpose(0, 2, 1, 3)
    k2 = k[:, :, :, 1].transpose(0, 2, 1, 3)
    vh = v.transpose(0, 2, 1, 3)
    outs = []
    for start in range(0, S, Q_BLOCK):
        end = start + Q_BLOCK
        p1 = _causal_block_probs(q1[:, :, start:end], k1[:, :, :end], start)
        p2 = _causal_block_probs(q2[:, :, start:end], k2[:, :, :end], start)
        outs.append(jnp.einsum('bhqk,bhkd->bhqd', p1 - lam * p2, vh[:, :, :end].astype(F32)))
    o = jnp.concatenate(outs, axis=2).transpose(0, 2, 1, 3)
    o = _rmsnorm(o, sub_gain) * (1.0 - lambda_init)
    return o.reshape(Bsz, S, H * DIFF_V_DIM).astype(v.dtype)


def _segsum(x):
    T = x.shape[-1]
    xe = jnp.broadcast_to(x[..., None], x.shape + (T,))
    xe = jnp.where(jnp.tril(jnp.ones((T, T), bool), -1), xe, 0.0)
    cs = jnp.cumsum(xe, axis=-2)
    return jnp.where(jnp.tril(jnp.ones((T, T), bool), 0), cs, -jnp.inf)


def _ssd_chunked(X, A, Bh, Ch):
    b, l, h, p = X.shape
    n = Bh.shape[-1]
    c = l // SSD_CHUNK
    X = X.reshape(b, c, SSD_CHUNK, h, p)
    Bh = Bh.reshape(b, c, SSD_CHUNK, h, n)
    Ch = Ch.reshape(b, c, SSD_CHUNK, h, n)
    A = A.reshape(b, c, SSD_CHUNK, h).transpose(0, 3, 1, 2)
    A_cum = jnp.cumsum(A, axis=-1)
    L = jnp.exp(_segsum(A))
    y_diag = jnp.einsum('bclhn,bcshn,bhcls,bcshp->bclhp', Ch, Bh, L, X)
    decay_states = jnp.exp(A_cum[..., -1:] - A_cum)
    states = jnp.einsum('bclhn,bhcl,bclhp->bchpn', Bh, decay_states, X)
    states = jnp.concatenate([jnp.zeros_like(states[:, :1]), states], axis=1)
    chunk_tot = jnp.pad(A_cum[..., -1], ((0, 0), (0, 0), (1, 0)))
    decay_chunk = jnp.exp(_segsum(chunk_tot))
    states = jnp.einsum('bhzc,bchpn->bzhpn', decay_chunk, states)[:, :-1]
    y_off = jnp.einsum('bclhn,bchpn,bhcl->bclhp', Ch, states, jnp.exp(A_cum))
    return (y_diag + y_off).reshape(b, l, h, p)


def _ssd_branch(z, xbc, dt_raw, conv_w, conv_b, dt_bias, a_log, d_skip, norm_g):
    Bsz, S, _ = xbc.shape
    conv = lax.conv_general_dilated(
        xbc.astype(F32), conv_w.astype(F32)[:, None, :], window_strides=(1,),
        padding=[(SSD_CONV - 1, 0)], dimension_numbers=('NWC', 'WIO', 'NWC'),
        feature_group_count=SSD_CONV_DIM)
    xbc = jax.nn.silu(conv + conv_b.astype(F32))
    gn = SSD_GROUPS * SSD_STATE
    xs = xbc[..., :BRANCH_WIDTH].reshape(Bsz, S, SSD_HEADS, SSD_HEAD_DIM)
    rep = SSD_HEADS // SSD_GROUPS
    bm = jnp.repeat(xbc[..., BRANCH_WIDTH:BRANCH_WIDTH + gn].reshape(Bsz, S, SSD_GROUPS, SSD_STATE), rep, axis=2)
    cm = jnp.repeat(xbc[..., BRANCH_WIDTH + gn:].reshape(Bsz, S, SSD_GROUPS, SSD_STATE), rep, axis=2)
    dt = jax.nn.softplus(dt_raw.astype(F32) + dt_bias.astype(F32))
    a = -jnp.exp(a_log.astype(F32))
    y = _ssd_chunked(xs * dt[..., None], a * dt, bm, cm)
    y = y + d_skip.astype(F32)[:, None] * xs
    y = y.reshape(Bsz, S, BRANCH_WIDTH) * jax.nn.silu(z.astype(F32))
    return _rmsnorm(y, norm_g).astype(z.dtype)


def _s5_combine(e1, e2):
    a1, b1 = e1
    a2, b2 = e2
    return a1 * a2, a2 * b1 + b2


def _s5_branch(u, a_re, a_im, b_re, b_im, c_re, c_im, d_skip, log_dt, w_glu):
    Bsz, S, _ = u.shape
    uf = u.astype(F32).reshape(Bsz, S, S5_GROUPS, S5_GROUP_CH)
    lam = lax.complex(a_re.astype(F32), a_im.astype(F32))
    dt = jnp.exp(log_dt.astype(F32))[:, None]
    a_bar = jnp.exp(lam * dt)
    b_bar = ((a_bar - 1.0) / lam)[..., None] * lax.complex(b_re.astype(F32), b_im.astype(F32))
    bu = lax.complex(jnp.einsum('gph,blgh->blgp', jnp.real(b_bar), uf),
                     jnp.einsum('gph,blgh->blgp', jnp.imag(b_bar), uf))
    a_seq = jnp.broadcast_to(a_bar, (1, S) + a_bar.shape)
    _, states = lax.associative_scan(_s5_combine, (a_seq, bu), axis=1)
    y = (jnp.einsum('ghp,blgp->blgh', c_re.astype(F32), jnp.real(states))
         - jnp.einsum('ghp,blgp->blgh', c_im.astype(F32), jnp.imag(states))
         + d_skip.astype(F32) * uf)
    y = jax.nn.gelu(y.reshape(Bsz, S, BRANCH_WIDTH))
    ga = y @ w_glu.astype(F32)
    out = ga[..., :BRANCH_WIDTH] * jax.nn.sigmoid(ga[..., BRANCH_WIDTH:])
    return out.astype(u.dtype)


def setup_inputs(seed: int = 0) -> dict:
    key = jax.random.key(seed)
    ks = jax.random.split(key, 24)
    nrm = jax.random.normal
    L, D, W = DEPTH, D_MODEL, BRANCH_WIDTH
    x = nrm(ks[0], (BATCH, SEQ, D), F32)
    w_in = nrm(ks[1], (L, D, IN_COLS), F32) * D ** -0.5
    b_fox_f = jnp.linspace(1.0, 4.0, FOX_HEADS, dtype=F32)[None, :] + 0.1 * nrm(ks[2], (L, FOX_HEADS), F32)
    pre_norm_g = 1.0 + 0.02 * nrm(ks[3], (L, D), F32)
    post_norm_g = 1.0 + 0.02 * nrm(ks[4], (L, D), F32)
    diff_lambda = 0.1 * nrm(ks[5], (L, 4, DIFF_QK_DIM), F32)
    diff_subln_g = 1.0 + 0.02 * nrm(ks[6], (L, DIFF_V_DIM), F32)
    ssd_conv_w = nrm(ks[7], (L, SSD_CONV, SSD_CONV_DIM), F32) * SSD_CONV ** -0.5
    ssd_conv_b = 0.02 * nrm(ks[8], (L, SSD_CONV_DIM), F32)
    dt0 = jnp.exp(jax.random.uniform(ks[9], (L, SSD_HEADS), F32, math.log(1e-3), math.log(1e-1)))
    ssd_dt_bias = dt0 + jnp.log(-jnp.expm1(-dt0))
    ssd_a_log = jnp.log(jax.random.uniform(ks[10], (L, SSD_HEADS), F32, 1.0, 16.0))
    ssd_d = 1.0 + 0.02 * nrm(ks[11], (L, SSD_HEADS), F32)
    ssd_norm_g = 1.0 + 0.02 * nrm(ks[12], (L, W), F32)
    s5_a_re = -0.5 + 0.01 * nrm(ks[13], (L, S5_GROUPS, S5_STATE), F32)
    s5_a_im = (jnp.pi * jnp.arange(S5_STATE, dtype=F32))[None, None, :] + 0.01 * nrm(ks[14], (L, S5_GROUPS, S5_STATE), F32)
    s5_b_re = nrm(ks[15], (L, S5_GROUPS, S5_STATE, S5_GROUP_CH), F32) * (2 * S5_GROUP_CH) ** -0.5
    s5_b_im = nrm(ks[16], (L, S5_GROUPS, S5_STATE, S5_GROUP_CH), F32) * (2 * S5_GROUP_CH) ** -0.5
    s5_c_re = nrm(ks[17], (L, S5_GROUPS, S5_GROUP_CH, S5_STATE), F32) * (2 * S5_STATE) ** -0.5
    s5_c_im = nrm(ks[18], (L, S5_GROUPS, S5_GROUP_CH, S5_STATE), F32) * (2 * S5_STATE) ** -0.5
    s5_d = nrm(ks[19], (L, S5_GROUPS, S5_GROUP_CH), F32)
    s5_log_dt = jax.random.uniform(ks[20], (L, S5_GROUPS), F32, math.log(1e-3), math.log(1e-1))
    s5_w_glu = nrm(ks[21], (L, W, 2 * W), F32) * W ** -0.5
    w_branch = nrm(ks[22], (L, N_BRANCHES, W, D), F32) * W ** -0.5
    w_out = nrm(ks[23], (L, D, D), F32) * D ** -0.5
    return {'x': x, 'w_in': w_in, 'b_fox_f': b_fox_f, 'pre_norm_g': pre_norm_g,
            'post_norm_g': post_norm_g, 'diff_lambda': diff_lambda, 'diff_subln_g': diff_subln_g,
            'ssd_conv_w': ssd_conv_w, 'ssd_conv_b': ssd_conv_b, 'ssd_dt_bias': ssd_dt_bias,
            'ssd_a_log': ssd_a_log, 'ssd_d': ssd_d, 'ssd_norm_g': ssd_norm_g,
            's5_a_re': s5_a_re, 's5_a_im': s5_a_im, 's5_b_re': s5_b_re, 's5_b_im': s5_b_im,
            's5_c_re': s5_c_re, 's5_c_im': s5_c_im, 's5_d': s5_d, 's5_log_dt': s5_log_dt,
            's5_w_glu': s5_w_glu, 'w_branch': w_branch, 'w_out': w_out}


def reference(x, w_in, b_fox_f, pre_norm_g, post_norm_g, diff_lambda, diff_subln_g,
              ssd_conv_w, ssd_conv_b, ssd_dt_bias, ssd_a_log, ssd_d, ssd_norm_g,
              s5_a_re, s5_a_im, s5_b_re, s5_b_im, s5_c_re, s5_c_im, s5_d, s5_log_dt,
              s5_w_glu, w_branch, w_out):
    Bsz, S, _ = x.shape
    pos = jnp.arange(S, dtype=F32)
    h = x
    for l in range(DEPTH):
        xn = _rmsnorm(h, pre_norm_g[l])
        (fq, fk, fv, ff, fg, dq, dk, dv, dg, sz, sxbc, sdt, su, sg, mg) = _combined_projection(xn, w_in[l])

        oa = _forgetting_attention(
            fq.reshape(Bsz, S, FOX_HEADS, FOX_HEAD_DIM), fk.reshape(Bsz, S, FOX_HEADS, FOX_HEAD_DIM),
            fv.reshape(Bsz, S, FOX_HEADS, FOX_HEAD_DIM), ff + b_fox_f[l]) * jax.nn.silu(fg)

        lambda_init = 0.8 - 0.6 * math.exp(-0.3 * l)
        lp = diff_lambda[l].astype(F32)
        lam = jnp.exp(jnp.sum(lp[0] * lp[1])) - jnp.exp(jnp.sum(lp[2] * lp[3])) + lambda_init
        dq5 = _rope(dq.reshape(Bsz, S, DIFF_HEADS, 2, DIFF_QK_DIM), pos)
        dk5 = _rope(dk.reshape(Bsz, S, DIFF_HEADS, 2, DIFF_QK_DIM), pos)
        ob = _differential_attention(dq5, dk5, dv.reshape(Bsz, S, DIFF_HEADS, DIFF_V_DIM),
                                     lam, diff_subln_g[l], lambda_init) * jax.nn.silu(dg)

        oc = _ssd_branch(sz, sxbc, sdt, ssd_conv_w[l], ssd_conv_b[l], ssd_dt_bias[l],
                         ssd_a_log[l], ssd_d[l], ssd_norm_g[l])

        od = _s5_branch(su, s5_a_re[l], s5_a_im[l], s5_b_re[l], s5_b_im[l], s5_c_re[l],
                        s5_c_im[l], s5_d[l], s5_log_dt[l], s5_w_glu[l]) * jax.nn.silu(sg)

        merged = None
        for n, o in enumerate((oa, ob, oc, od)):
            gate = jax.nn.sigmoid(mg[..., n * D_MODEL:(n + 1) * D_MODEL])
            term = gate * (o.astype(h.dtype) @ w_branch[l, n])
            merged = term if merged is None else merged + term
        y = merged @ w_out[l]
        h = h + _rmsnorm(y, post_norm_g[l])
    return h
```

```python
import contextlib
import math
import numpy as np
import ml_dtypes
import concourse.bass as bass
import concourse.mybir as mybir
from concourse.bass_utils import run_bass_kernel_spmd

F32 = mybir.dt.float32
BF16 = mybir.dt.bfloat16
AF = mybir.ActivationFunctionType
ALU = mybir.AluOpType
AX = mybir.AxisListType

D = 2048
S = 2048
NT = S // 128
DEPTH = 4
NSEQ = 2
INC = 14860
EPS = 1e-6
O_FQ, O_FK, O_FV, O_FF, O_FG = 0, 512, 1024, 1536, 1540
O_DQ, O_DK, O_DV, O_DG = 2052, 2564, 3076, 3588
O_SZ, O_SX, O_SDT, O_SU, O_SG, O_MG = 4100, 4612, 5636, 5644, 6156, 6668
FM_SEGS = [(O_FQ, 512), (O_FK, 512), (O_DQ, 512), (O_DK, 512), (O_SX, 1024), (O_SU, 512)]
R_FQ, R_FK, R_DQ, R_DK, R_SX, R_SU = 0, 512, 1024, 1536, 2048, 3072
FM_ROWS = 3584
TM_SEGS = [(O_FV, 512), (O_FG, 512), (O_DV, 512), (O_DG, 512), (O_SZ, 512), (O_SG, 512), (O_MG, 8192)]
C_FV, C_FG, C_DV, C_DG, C_SZ, C_SG, C_MG = 0, 512, 1024, 1536, 2048, 2560, 3072
TM_COLS = 11264


class Buf:
    def __init__(self, kb, name, t):
        self.kb = kb
        self.name = name
        self.t = t
        self.lw = []
        self.rd = []
        self.dsem = None

    def __getitem__(self, k):
        return self.t[k]


class KB:
    ENG = ("pe", "act", "dve", "pool", "sp")

    def __init__(self, nc, es):
        self.nc = nc
        self.es = es
        self.e = {"pe": nc.tensor, "act": nc.scalar, "dve": nc.vector, "pool": nc.gpsimd, "sp": nc.sync}
        self.sem = {}
        self.cnt = {}
        self.seen = {k: {} for k in self.ENG}
        self.semval = {}
        self.semobj = {}
        self.epoch = 0
        self.new_epoch()
        self.dma_free = []
        self.dma_all = []
        for i in range(64):
            s = es.enter_context(nc.semaphore(f"dq{i}"))
            nm = f"dq{i}"
            self.semobj[nm] = s
            self.semval[nm] = 0
            self.dma_free.append(nm)
            self.dma_all.append(nm)
        self.bar = es.enter_context(nc.semaphore("bar"))
        self.barv = 0
        self.phase_bufs = []
        self.uid = 0

    def new_epoch(self):
        self.epoch += 1
        for k in self.ENG:
            nm = f"e{self.epoch}_{k}"
            s = self.es.enter_context(self.nc.semaphore(nm))
            self.semobj[nm] = s
            self.semval[nm] = 0
            self.sem[k] = nm

    def sb(self, shape, dt, name=None):
        self.uid += 1
        name = f"{name or 'sb'}_{self.uid}"
        t = self.pes.enter_context(self.nc.sbuf_tensor(name, list(shape), dt))
        b = Buf(self, name, t)
        self.phase_bufs.append(b)
        return b

    def ps(self, shape, dt=F32, name=None):
        self.uid += 1
        name = f"{name or 'ps'}_{self.uid}"
        t = self.pes.enter_context(self.nc.psum_tensor(name, list(shape), dt))
        b = Buf(self, name, t)
        self.phase_bufs.append(b)
        return b

    def dram(self, name, t):
        return Buf(self, name, t)

    @contextlib.contextmanager
    def phase(self):
        self.pes = contextlib.ExitStack()
        self.phase_bufs = []
        try:
            yield
            self.barrier()
        finally:
            for b in self.phase_bufs:
                if b.dsem is not None:
                    self.dma_free.append(b.dsem)
                    b.dsem = None
            self.pes.close()
            self.pes = None

    def _wait(self, eng, toks):
        need = {}
        for (s, v) in toks:
            if need.get(s, 0) < v:
                need[s] = v
        for s, v in need.items():
            if self.seen[eng].get(s, 0) >= v:
                continue
            self.e[eng].wait_ge(self.semobj[s], v)
            self.seen[eng][s] = v

    def _deps(self, reads, writes):
        toks = []
        for b in reads:
            toks += b.lw
        for b in writes:
            toks += b.lw
            toks += b.rd
        return toks

    def op(self, eng, fn, reads=(), writes=()):
        self._wait(eng, self._deps(reads, writes))
        ins = fn(self.e[eng])
        s = self.sem[eng]
        self.semval[s] += 1
        ins.then_inc(self.semobj[s], 1)
        tok = (s, self.semval[s])
        self._record(tok, reads, writes)
        return tok

    def _record(self, tok, reads, writes):
        for b in reads:
            b.rd.append(tok)
            if len(b.rd) > 64:
                b.rd = self._compact(b.rd)
        for b in writes:
            b.lw = [tok]
            b.rd = []

    @staticmethod
    def _compact(toks):
        need = {}
        for (s, v) in toks:
            if need.get(s, 0) < v:
                need[s] = v
        return list(need.items())

    def mm(self, out_ap, pairs, reads, writes, start=True, stop=True):
        eng = "pe"
        self._wait(eng, self._deps(reads, writes))
        n = len(pairs)
        ins = None
        for i, (l, r) in enumerate(pairs):
            ins = self.nc.tensor.matmul(out_ap, l, r, start=(start and i == 0), stop=(stop and i == n - 1))
        s = self.sem[eng]
        self.semval[s] += 1
        ins.then_inc(self.semobj[s], 1)
        tok = (s, self.semval[s])
        self._record(tok, reads, writes)
        return tok

    def dma(self, q, out_ap, in_ap, reads=(), writes=(), sembuf=None, **kw):
        b = sembuf
        if b.dsem is None:
            b.dsem = self.dma_free.pop(0)
        self._wait(q, self._deps(reads, writes))
        ins = self.e[q].dma_start(out=out_ap, in_=in_ap, **kw)
        s = b.dsem
        self.semval[s] += 16
        ins.then_inc(self.semobj[s], 16)
        tok = (s, self.semval[s])
        for r in reads:
            r.rd.append(tok)
        for w in writes:
            if w.lw and all(t[0] == s for t in w.lw):
                w.lw = [tok]
            else:
                w.lw = [tok]
            w.rd = []
        return tok

    def barrier(self):
        toks = [(s, v) for s, v in self.semval.items() if v > 0 and not s.startswith("bar")]
        cur = set(self.sem.values()) | set(self.dma_all)
        toks = [(s, v) for (s, v) in toks if s in cur]
        self._wait("sp", toks)
        self.barv += 1
        self.nc.sync.sem_inc(self.bar, 1)
        for k in self.ENG:
            if k == "sp":
                continue
            self.e[k].wait_ge(self.bar, self.barv)
            for (s, v) in toks:
                self.seen[k][s] = max(self.seen[k].get(s, 0), v)


def bcast_ap(t_ap, shape):
    return t_ap.broadcast_to(list(shape))


class Ctx:
    pass


def load_const(kb, cx, name, shape, dt):
    b = kb.sb(shape, dt, name=name)
    kb.dma("sp", b[:], cx.cst[name].t[:], reads=[cx.cst[name]], writes=[b], sembuf=b)
    return b


def phase_A(kb, cx, l, s):
    nc = kb.nc
    with kb.phase():
        ident = load_const(kb, cx, "ident_bf", [128, 128], BF16)
        xnT = kb.sb([128, 16, S], BF16, "xnT")
        gb = kb.sb([128, D], F32, "gb")
        kb.dma("sp", gb[:], cx.pre_g.t[l:l + 1, :].broadcast_to([128, D]), reads=[cx.pre_g], writes=[gb], sembuf=gb)
        hts = [kb.sb([128, D], F32, "ht") for _ in range(2)]
        junk = kb.sb([128, D], BF16, "junk")
        xss = [kb.sb([128, D], BF16, "xs") for _ in range(2)]
        ss = kb.sb([128, NT], F32, "ss")
        rs = kb.sb([128, NT], F32, "rs")
        ptr = [kb.ps([128, 512], BF16, "ptr") for _ in range(2)]
        pmm = [kb.ps([128, 512], F32, "pmm") for _ in range(4)]
        hin = cx.h_in
        for t in range(NT if cx.stop >= 2 else 0):
            ht = hts[t % 2]
            xs = xss[t % 2]
            r0 = s * S + t * 128
            kb.dma("sp", ht[:], hin.t[r0:r0 + 128, :], reads=[hin], writes=[ht], sembuf=ht)
            kb.op("act", lambda e: e.activation(out=junk[:], in_=ht[:], func=AF.Square, accum_out=ss[:, t:t + 1]),
                  reads=[ht], writes=[junk, ss])
            kb.op("dve", lambda e: e.tensor_scalar(out=rs[:, t:t + 1], in0=ss[:, t:t + 1], scalar1=1.0 / D, scalar2=EPS,
                                                   op0=ALU.mult, op1=ALU.add), reads=[ss], writes=[rs])
            kb.op("act", lambda e: e.activation(out=rs[:, t:t + 1], in_=rs[:, t:t + 1], func=AF.Sqrt), reads=[rs], writes=[rs])
            kb.op("dve", lambda e: e.reciprocal(out=rs[:, t:t + 1], in_=rs[:, t:t + 1]), reads=[rs], writes=[rs])
            kb.op("dve", lambda e: e.scalar_tensor_tensor(out=xs[:], in0=ht[:], scalar=rs[:, t:t + 1], in1=gb[:],
                                                          op0=ALU.mult, op1=ALU.mult), reads=[ht, rs, gb], writes=[xs])
            for g4 in range(4):
                pt = ptr[g4 % 2]
                for j in range(4):
                    c = g4 * 4 + j
                    kb.op("pe", lambda e: e.transpose(out=pt[:, j * 128:(j + 1) * 128], in_=xs[:, c * 128:(c + 1) * 128],
                                                      identity=ident[:]), reads=[xs, ident], writes=[pt])
                dst = xnT[:, g4 * 4:(g4 + 1) * 4, t * 128:(t + 1) * 128]
                src = pt[:].rearrange("p (a b) -> p a b", a=4)
                if g4 % 2 == 0:
                    kb.op("dve", lambda e: e.tensor_copy(out=dst, in_=src), reads=[pt], writes=[xnT])
                else:
                    kb.op("act", lambda e: e.activation(out=dst, in_=src, func=AF.Copy), reads=[pt], writes=[xnT])

        W = cx.w_in
        wst = [kb.sb([128, 4, 512], F32, "wst") for _ in range(2)]
        wbfs = [kb.sb([128, 16, 512], BF16, "wbf") for _ in range(2)]
        stfm = [kb.sb([128, S], BF16, "stfm") for _ in range(2)]
        sttm = [kb.sb([128, 512], BF16, "sttm") for _ in range(3)]
        blocks = []
        for (wc, n), r in zip(FM_SEGS, [R_FQ, R_FK, R_DQ, R_DK, R_SX, R_SU]):
            for k in range(n // 512):
                blocks.append(("fm", wc + k * 512, r + k * 512))
        col = 0
        for (wc, n) in TM_SEGS:
            for k in range(n // 512):
                blocks.append(("tm", wc + k * 512, col))
                col += 512
        if cx.dbg_nblocks is not None:
            blocks = blocks[:cx.dbg_nblocks[0]] + [b for b in blocks if b[0] == "tm"][:cx.dbg_nblocks[1]]
        fmT, tm, sm = cx.fmT[s], cx.tm[s], cx.sm[s]
        state = {"k": 0, "pm": 0, "sf": 0, "st": 0}

        def load_block(i):
            kind, wc, dst = blocks[i]
            wbf = wbfs[i % 2]
            for piece in range(4):
                st = wst[state["k"] % 2]
                state["k"] += 1
                src = W.t[l, piece * 512:(piece + 1) * 512, wc:wc + 512].rearrange("(c p) n -> p c n", p=128)
                kb.dma("sp", st[:], src, reads=[W], writes=[st], sembuf=st)
                kb.op("pool", lambda e: e.tensor_copy(out=wbf[:, piece * 4:(piece + 1) * 4, :], in_=st[:]),
                      reads=[st], writes=[wbf])

        if cx.stop < 3:
            return
        wsm_f = kb.sb([128, 16, 12], F32, "wsmf")
        wsm = kb.sb([128, 16, 12], BF16, "wsm")
        smst = kb.sb([128, NT, 12], F32, "smst")
        for (wc, n, o) in ((O_FF, 4, 0), (O_SDT, 8, 4)):
            src = W.t[l, :, wc:wc + n].rearrange("(c p) n -> p c n", p=128)
            kb.dma("sp", wsm_f[:, :, o:o + n], src, reads=[W], writes=[wsm_f], sembuf=wsm_f, allow_slow_non_contiguous=True)
        kb.op("pool", lambda e: e.tensor_copy(out=wsm[:], in_=wsm_f[:]), reads=[wsm_f], writes=[wsm])
        for t in range(NT):
            pm = pmm[state["pm"] % 4]
            state["pm"] += 1
            kb.mm(pm[:, 0:12], [(xnT[:, c, t * 128:(t + 1) * 128], wsm[:, c, :]) for c in range(16)],
                  reads=[xnT, wsm], writes=[pm])
            kb.op("act", lambda e: e.activation(out=smst[:, t, :], in_=pm[:, 0:12], func=AF.Copy), reads=[pm], writes=[smst])
        kb.dma("act", sm.t[:, 0:12].rearrange("(t p) n -> p t n", p=128), smst[:], reads=[smst], writes=[sm], sembuf=smst,
               allow_slow_non_contiguous=True)

        if cx.stop < 4:
            return
        load_block(0)
        for i, (kind, wc, dst) in enumerate(blocks):
            if i + 1 < len(blocks):
                load_block(i + 1)
            wbf = wbfs[i % 2]
            if kind == "fm":
                for j in range(4):
                    sf = stfm[state["sf"] % 2]
                    state["sf"] += 1
                    for tb in range(4):
                        pm = pmm[state["pm"] % 4]
                        state["pm"] += 1
                        kb.mm(pm[:], [(wbf[:, c, j * 128:(j + 1) * 128], xnT[:, c, tb * 512:(tb + 1) * 512]) for c in range(16)],
                              reads=[xnT, wbf], writes=[pm])
                        kb.op("act", lambda e: e.activation(out=sf[:, tb * 512:(tb + 1) * 512], in_=pm[:], func=AF.Copy),
                              reads=[pm], writes=[sf])
                    kb.dma("act", fmT.t[dst + j * 128:dst + (j + 1) * 128, :], sf[:], reads=[sf], writes=[fmT], sembuf=sf)
            else:
                for t in range(NT):
                    pm = pmm[state["pm"] % 4]
                    state["pm"] += 1
                    stt = sttm[state["st"] % 3]
                    state["st"] += 1
                    kb.mm(pm[:], [(xnT[:, c, t * 128:(t + 1) * 128], wbf[:, c, :]) for c in range(16)],
                          reads=[xnT, wbf], writes=[pm])
                    if t % 2 == 0:
                        kb.op("act", lambda e: e.activation(out=stt[:], in_=pm[:], func=AF.Copy), reads=[pm], writes=[stt])
                    else:
                        kb.op("dve", lambda e: e.tensor_copy(out=stt[:], in_=pm[:]), reads=[pm], writes=[stt])
                    kb.dma("act", tm.t[t * 128:(t + 1) * 128, dst:dst + 512], stt[:], reads=[stt], writes=[tm], sembuf=stt)


BF = ml_dtypes.bfloat16


def make_consts():
    c = {}
    c["ident_bf"] = np.eye(128, dtype=np.float32).astype(BF)
    c["ident_f"] = np.eye(128, dtype=np.float32)
    k = np.arange(128)[:, None]
    q = np.arange(128)[None, :]
    c["mask_le_bf"] = (k <= q).astype(np.float32).astype(BF)
    c["tri_le_f"] = (k <= q).astype(np.float32)
    c["gt_f"] = (k > q).astype(np.float32)
    c["negmask_f"] = np.where(q < k, -30000.0, 0.0).astype(np.float32)
    c["ones_f"] = np.ones((128, 128), np.float32)
    d = np.arange(128) % 32
    inv = (10000.0 ** (-d.astype(np.float64) / 32.0))
    ang = inv[:, None] * np.arange(S, dtype=np.float64)[None, :]
    c["rope_cos"] = np.cos(ang).astype(np.float32)
    c["rope_sin"] = np.sin(ang).astype(np.float32)
    RT = np.zeros((128, 128), np.float32)
    for p in range(128):
        dd = p % 64
        if dd < 32:
            RT[p + 32, p] = -1.0
        else:
            RT[p - 32, p] = 1.0
    c["rope_rt"] = RT.astype(BF)
    gm = np.zeros((128, 8), np.float32)
    for r in range(128):
        gm[r, r // 16] = 1.0
    c["gmask"] = gm
    return c


CONST_DT = {"ident_bf": BF16, "mask_le_bf": BF16, "rope_rt": BF16}

PARAMS = ["w_in", "b_fox_f", "pre_norm_g", "post_norm_g", "diff_lambda", "diff_subln_g", "ssd_conv_w", "ssd_conv_b",
          "ssd_dt_bias", "ssd_a_log", "ssd_d", "ssd_norm_g", "s5_a_re", "s5_a_im", "s5_b_re", "s5_b_im", "s5_c_re",
          "s5_c_im", "s5_d", "s5_log_dt", "s5_w_glu", "w_branch", "w_out"]


def build(shapes, layers=range(DEPTH), phases="ABCDEF", dbg=None, dbg_nblocks=None, stop=99, stop2=0):
    dbg = dbg or {}
    nc = bass.Bass("TRN2", target_bir_lowering=False)
    es = contextlib.ExitStack()
    kb = KB(nc, es)
    cx = Ctx()
    cx.dbg_nblocks = dbg_nblocks
    cx.stop = stop
    cx.stop2 = stop2
    x = nc.dram_tensor("x", [NSEQ * S, D], F32, kind="ExternalInput")
    out = nc.dram_tensor("out", [NSEQ * S, D], F32, kind="ExternalOutput")
    cx.cst = {}
    for k, v in make_consts().items():
        cx.cst[k] = kb.dram(k, nc.dram_tensor("c_" + k, list(v.shape), CONST_DT.get(k, F32), kind="ExternalInput"))
    for p in PARAMS:
        setattr(cx, p, kb.dram(p, nc.dram_tensor(p, list(shapes[p]), F32, kind="ExternalInput")))
    cx.pre_g = cx.pre_norm_g

    def scratch(name, shape, dt):
        kind = {"in": "ExternalInput", "out": "ExternalOutput"}.get(dbg.get(name), "Internal")
        return kb.dram(name, nc.dram_tensor(name, shape, dt, kind=kind))

    cx.fmT = [scratch(f"fmT{s}", [FM_ROWS, S], BF16) for s in range(NSEQ)]
    cx.tm = [scratch(f"tm{s}", [S, TM_COLS], BF16) for s in range(NSEQ)]
    cx.sm = [scratch(f"sm{s}", [S, 16], F32) for s in range(NSEQ)]
    cx.oT = [scratch(f"oT{s}", [4 * 512, S], BF16) for s in range(NSEQ)]
    cx.mTs = scratch("mTs", [D, NSEQ * S], BF16)
    cx.Etab = scratch("Etab", [16, 2, 128, S], F32)
    hA = scratch("hA", [NSEQ * S, D], F32)
    hB = scratch("hB", [NSEQ * S, D], F32)
    xb = kb.dram("x", x)
    ob = kb.dram("out", out)
    layers = list(layers)
    if "E" in phases:
        s5_alloc(kb, cx)
    for li, l in enumerate(layers):
        cx.h_in = xb if li == 0 else (hA if li % 2 == 1 else hB)
        cx.h_out = ob if li == len(layers) - 1 else (hA if li % 2 == 0 else hB)
        cx.l = l
        if "E" in phases:
            phase_s5_prep(kb, cx, l)
        for s in range(NSEQ):
            if "A" in phases:
                phase_A(kb, cx, l, s)
            if "B" in phases:
                phase_fox(kb, cx, l, s)
            if "C" in phases:
                phase_diff(kb, cx, l, s)
            if "D" in phases:
                phase_ssd(kb, cx, l, s)
            if "E" in phases:
                phase_s5(kb, cx, l, s)
        if "F" in phases:
            phase_out(kb, cx, l)
        if li + 1 < len(layers):
            kb.new_epoch()
    es.close()
    return nc


_CACHE = {}


def kernel(**inputs):
    x = np.ascontiguousarray(inputs["x"], dtype=np.float32)
    shapes = {p: inputs[p].shape for p in PARAMS}
    key = "full"
    if key not in _CACHE:
        _CACHE[key] = build(shapes)
    nc = _CACHE[key]
    consts = make_consts()
    base = {p: np.ascontiguousarray(inputs[p], dtype=np.float32) for p in PARAMS}
    for k, v in consts.items():
        base["c_" + k] = v
    in_maps = []
    for c in range(8):
        m = dict(base)
        m["x"] = x[c * NSEQ:(c + 1) * NSEQ].reshape(NSEQ * S, D)
        in_maps.append(m)
    res = run_bass_kernel_spmd(nc, in_maps, core_ids=list(range(8)))
    outs = [np.asarray(r["out"]).reshape(NSEQ, S, D) for r in res.results]
    return np.concatenate(outs, axis=0).astype(np.float32)


def attn_core(kb, cx, maps, nheads, vaug, finalize, scale, pools):
    sc_ps, pt_sb, po_ps, maskT = pools
    st = {"sc": 0, "po": 0}
    for h in range(nheads):
        for i in range(NT):
            pos = []
            for mi, mp in enumerate(maps):
                po = po_ps[st["po"] % len(po_ps)]
                st["po"] += 1
                pos.append(po)
                groups = [list(range(j0, min(j0 + 4, i + 1))) for j0 in range(0, i + 1, 4)]
                pend = None
                for gi, js in enumerate(groups):
                    sc = sc_ps[st["sc"] % len(sc_ps)]
                    pt = pt_sb[st["sc"] % len(pt_sb)]
                    st["sc"] += 1
                    for jj, j in enumerate(js):
                        kb.mm(sc[:, jj * 128:(jj + 1) * 128], [(mp["kT"](h, j), mp["qT"](h, i))], reads=mp["reads"], writes=[sc])
                    if mp["bias"] is None:
                        n = len(js) * 128
                        kb.op("act", lambda e: e.activation(out=pt[:, 0:n], in_=sc[:, 0:n], func=AF.Exp, scale=scale),
                              reads=[sc], writes=[pt])
                    else:
                        for jj, j in enumerate(js):
                            kb.op("act", lambda e: e.activation(out=pt[:, jj * 128:(jj + 1) * 128], in_=sc[:, jj * 128:(jj + 1) * 128],
                                                                func=AF.Exp, scale=scale, bias=mp["bias"](h, i, j)),
                                  reads=[sc] + mp["bias_reads"], writes=[pt])
                    if js[-1] == i:
                        jj = len(js) - 1
                        kb.op("dve", lambda e: e.tensor_tensor(out=pt[:, jj * 128:(jj + 1) * 128], in0=pt[:, jj * 128:(jj + 1) * 128],
                                                               in1=maskT[:], op=ALU.mult), reads=[pt, maskT], writes=[pt])
                    if pend is not None:
                        _pv(kb, pend, po, vaug, h, i)
                    pend = (js, pt)
                _pv(kb, pend, po, vaug, h, i)
            finalize(h, i, pos)


def _pv(kb, pend, po, vaug, h, i):
    js, pt = pend
    for jj, j in enumerate(js):
        kb.nc.tensor
        kb._wait("pe", kb._deps([pt, vaug], [po] if j == 0 else []))
        ins = kb.nc.tensor.matmul(po[:, 0:129], pt[:, jj * 128:(jj + 1) * 128], vaug[:, j, h, :], start=(j == 0), stop=(j == i))
        s = kb.sem["pe"]
        kb.semval[s] += 1
        ins.then_inc(kb.semobj[s], 1)
        tok = (s, kb.semval[s])
        pt.rd.append(tok)
        vaug.rd.append(tok)
        if j == 0:
            po.rd = []
        po.lw = [tok]


def load_vaug(kb, cx, s, col0, name):
    vaug = kb.sb([128, NT, 4, 129], BF16, name)
    kb.op("pool", lambda e: e.memset(vaug[:, :, :, 128:129], 1.0), reads=[], writes=[vaug])
    tm = cx.tm[s]
    for t in range(NT):
        src = tm.t[t * 128:(t + 1) * 128, col0:col0 + 512].rearrange("p (h d) -> p h d", h=4)
        kb.dma("sp", vaug[:, t, :, 0:128], src, reads=[tm], writes=[vaug], sembuf=vaug)
    return vaug


def transpose_out(kb, cx, o_tm, ident, ptr, oTs, i):
    for c in range(4):
        kb.op("pe", lambda e: e.transpose(out=ptr[:, c * 128:(c + 1) * 128], in_=o_tm[:, c * 128:(c + 1) * 128], identity=ident[:]),
              reads=[o_tm, ident], writes=[ptr])
    kb.op("act", lambda e: e.activation(out=oTs[:, :, i * 128:(i + 1) * 128], in_=ptr[:].rearrange("p (a b) -> p a b", a=4), func=AF.Copy),
          reads=[ptr], writes=[oTs])


def phase_fox(kb, cx, l, s):
    nc = kb.nc
    with kb.phase():
        ident = load_const(kb, cx, "ident_bf", [128, 128], BF16)
        maskT = load_const(kb, cx, "mask_le_bf", [128, 128], BF16)
        tri = load_const(kb, cx, "tri_le_f", [128, 128], F32)
        ones = load_const(kb, cx, "ones_f", [128, 128], F32)
        fmT, tm, sm = cx.fmT[s], cx.tm[s], cx.sm[s]
        qT = kb.sb([128, 4, S], BF16, "qT")
        kT = kb.sb([128, 4, S], BF16, "kT")
        kb.dma("sp", qT[:], fmT.t[R_FQ:R_FQ + 512, :].rearrange("(h p) t -> p h t", p=128), reads=[fmT], writes=[qT], sembuf=qT)
        kb.dma("sp", kT[:], fmT.t[R_FK:R_FK + 512, :].rearrange("(h p) t -> p h t", p=128), reads=[fmT], writes=[kT], sembuf=kT)
        vaug = load_vaug(kb, cx, s, C_FV, "vaugf")
        ff = kb.sb([128, NT, 4], F32, "ff")
        kb.dma("sp", ff[:], sm.t[:, 0:4].rearrange("(t p) n -> p t n", p=128), reads=[sm], writes=[ff], sembuf=ff,
               allow_slow_non_contiguous=True)
        bb = kb.sb([128, 4], F32, "bb")
        kb.dma("sp", bb[:], cx.b_fox_f.t[l:l + 1, :].broadcast_to([128, 4]), reads=[cx.b_fox_f], writes=[bb], sembuf=bb)
        lf = kb.sb([128, NT, 4], F32, "lf")
        kb.op("dve", lambda e: e.tensor_tensor(out=lf[:], in0=ff[:], in1=bb[:].unsqueeze(1).broadcast_to([128, NT, 4]), op=ALU.add),
              reads=[ff, bb], writes=[lf])
        kb.op("act", lambda e: e.activation(out=lf[:], in_=lf[:], func=AF.Exp, scale=-1.0), reads=[lf], writes=[lf])
        kb.op("act", lambda e: e.activation(out=lf[:], in_=lf[:], func=AF.Ln, bias=1.0), reads=[lf], writes=[lf])
        sc_ps = [kb.ps([128, 512], F32, "sc") for _ in range(3)]
        pF, pT = sc_ps[0], sc_ps[1]
        lf2 = lf[:].rearrange("p t h -> p (t h)")
        kb.mm(pF[:, 0:64], [(tri[:], lf2)], reads=[tri, lf], writes=[pF])
        kb.mm(pT[:, 0:64], [(ones[:], lf2)], reads=[ones, lf], writes=[pT])
        Lloc = kb.sb([128, NT, 4], F32, "Lloc")
        tot = kb.sb([128, NT, 4], F32, "tot")
        Lend = kb.sb([128, NT, 4], F32, "Lend")
        Lc = kb.sb([128, NT, 4], F32, "Lc")
        kb.op("dve", lambda e: e.tensor_copy(out=Lloc[:].rearrange("p t h -> p (t h)"), in_=pF[:, 0:64]), reads=[pF], writes=[Lloc])
        kb.op("dve", lambda e: e.tensor_copy(out=tot[:].rearrange("p t h -> p (t h)"), in_=pT[:, 0:64]), reads=[pT], writes=[tot])
        kb.op("dve", lambda e: e.tensor_copy(out=Lend[:, 0, :], in_=tot[:, 0, :]), reads=[tot], writes=[Lend])
        for t in range(1, NT):
            kb.op("dve", lambda e: e.tensor_tensor(out=Lend[:, t, :], in0=Lend[:, t - 1, :], in1=tot[:, t, :], op=ALU.add),
                  reads=[Lend, tot], writes=[Lend])
        kb.op("dve", lambda e: e.tensor_tensor(out=Lc[:], in0=Lend[:], in1=tot[:], op=ALU.subtract), reads=[Lend, tot], writes=[Lc])
        kb.op("dve", lambda e: e.tensor_tensor(out=Lc[:], in0=Lc[:], in1=Lloc[:], op=ALU.add), reads=[Lc, Lloc], writes=[Lc])
        bm = kb.sb([128, 4, NT, NT], F32, "bm")
        for h in range(4):
            for i in range(NT):
                kb.op("dve", lambda e: e.tensor_scalar(out=bm[:, h, i, 0:i + 1], in0=Lc[:, 0:i + 1, h], scalar1=Lend[:, i, h:h + 1],
                                                       scalar2=None, op0=ALU.subtract), reads=[Lc, Lend], writes=[bm])
        pt_sb = [kb.sb([128, 512], BF16, "pt") for _ in range(3)]
        po_ps = [kb.ps([128, 512], F32, "po") for _ in range(4)]
        ptr = kb.ps([128, 512], BF16, "ptr")
        oTs = kb.sb([128, 4, S], BF16, "oTs")
        otm = kb.sb([128, NT, 512], BF16, "otm")
        sg = kb.sb([128, NT, 512], F32, "sg")
        gin = kb.sb([128, NT, 512], BF16, "gin")
        kb.dma("sp", gin[:], tm.t[:, C_FG:C_FG + 512].rearrange("(t p) n -> p t n", p=128), reads=[tm], writes=[gin], sembuf=gin)
        kb.op("act", lambda e: e.activation(out=sg[:], in_=gin[:], func=AF.Silu), reads=[gin], writes=[sg])
        rec = kb.sb([128, 4 * NT], F32, "rec")

        def fin(h, i, pos):
            po = pos[0]
            r = rec[:, h * NT + i:h * NT + i + 1]
            kb.op("dve", lambda e: e.reciprocal(out=r, in_=po[:, 128:129]), reads=[po], writes=[rec])
            kb.op("dve", lambda e: e.scalar_tensor_tensor(out=otm[:, i, h * 128:(h + 1) * 128], in0=po[:, 0:128], scalar=r,
                                                          in1=sg[:, i, h * 128:(h + 1) * 128], op0=ALU.mult, op1=ALU.mult),
                  reads=[po, rec, sg], writes=[otm])

        maps = [dict(kT=lambda h, j: kT[:, h, j * 128:(j + 1) * 128], qT=lambda h, i: qT[:, h, i * 128:(i + 1) * 128],
                     bias=lambda h, i, j: bm[:, h, i, j:j + 1], bias_reads=[bm], reads=[kT, qT])]
        attn_core(kb, cx, maps, 4, vaug, fin, 128 ** -0.5, (sc_ps, pt_sb, po_ps, maskT))
        for i in range(NT):
            transpose_out(kb, cx, otm[:, i, :], ident, ptr, oTs, i) if False else None
        for i in range(NT):
            for c in range(4):
                kb.op("pe", lambda e: e.transpose(out=ptr[:, c * 128:(c + 1) * 128], in_=otm[:, i, c * 128:(c + 1) * 128], identity=ident[:]),
                      reads=[otm, ident], writes=[ptr])
            kb.op("act", lambda e: e.activation(out=oTs[:, :, i * 128:(i + 1) * 128], in_=ptr[:].rearrange("p (a b) -> p a b", a=4),
                                                func=AF.Copy), reads=[ptr], writes=[oTs])
        oT = cx.oT[s]
        kb.dma("sp", oT.t[0:512, :].rearrange("(c p) t -> p c t", p=128), oTs[:], reads=[oTs], writes=[oT], sembuf=oTs)


def rstd_from_ss(kb, ss_ap, out_ap, n, ssb, outb):
    kb.op("dve", lambda e: e.tensor_scalar(out=out_ap, in0=ss_ap, scalar1=1.0 / n, scalar2=EPS, op0=ALU.mult, op1=ALU.add),
          reads=[ssb], writes=[outb])
    kb.op("act", lambda e: e.activation(out=out_ap, in_=out_ap, func=AF.Ln), reads=[outb], writes=[outb])
    kb.op("act", lambda e: e.activation(out=out_ap, in_=out_ap, func=AF.Exp, scale=-0.5), reads=[outb], writes=[outb])


def phase_diff(kb, cx, l, s):
    lambda_init = 0.8 - 0.6 * math.exp(-0.3 * l)
    with kb.phase():
        ident = load_const(kb, cx, "ident_bf", [128, 128], BF16)
        maskT = load_const(kb, cx, "mask_le_bf", [128, 128], BF16)
        rt = load_const(kb, cx, "rope_rt", [128, 128], BF16)
        rcos = load_const(kb, cx, "rope_cos", [128, S], F32)
        rsin = load_const(kb, cx, "rope_sin", [128, S], F32)
        fmT, tm = cx.fmT[s], cx.tm[s]
        sc_ps = [kb.ps([128, 512], F32, "sc") for _ in range(3)]
        pt_sb = [kb.sb([128, 512], BF16, "pt") for _ in range(3)]
        po_ps = [kb.ps([128, 512], F32, "po") for _ in range(4)]
        ptr = kb.ps([128, 512], BF16, "ptr")
        raw = kb.sb([128, 4, S], BF16, "raw")
        qr = kb.sb([128, 4, S], BF16, "qr")
        kr = kb.sb([128, 4, S], BF16, "kr")
        t1s = [kb.sb([128, 512], F32, "t1") for _ in range(2)]
        t2s = [kb.sb([128, 512], F32, "t2") for _ in range(2)]
        n = 0
        for (r0, dst) in ((R_DQ, qr), (R_DK, kr)):
            kb.dma("sp", raw[:], fmT.t[r0:r0 + 512, :].rearrange("(h p) t -> p h t", p=128), reads=[fmT], writes=[raw], sembuf=raw)
            for h in range(4):
                for tb in range(4):
                    ps = sc_ps[n % 2]
                    t1 = t1s[n % 2]
                    t2 = t2s[n % 2]
                    n += 1
                    cs = slice(tb * 512, (tb + 1) * 512)
                    kb.mm(ps[:], [(rt[:], raw[:, h, cs])], reads=[rt, raw], writes=[ps])
                    kb.op("dve", lambda e: e.tensor_tensor(out=t1[:], in0=ps[:], in1=rsin[:, cs], op=ALU.mult), reads=[ps, rsin], writes=[t1])
                    kb.op("pool", lambda e: e.tensor_tensor(out=t2[:], in0=raw[:, h, cs], in1=rcos[:, cs], op=ALU.mult),
                          reads=[raw, rcos], writes=[t2])
                    kb.op("dve", lambda e: e.tensor_tensor(out=dst[:, h, cs], in0=t1[:], in1=t2[:], op=ALU.add), reads=[t1, t2], writes=[dst])
        vaug = load_vaug(kb, cx, s, C_DV, "vaugd")
        lpb = kb.sb([128, 256], F32, "lpb")
        kb.dma("sp", lpb[:], cx.diff_lambda.t[l:l + 1].rearrange("o a b -> o (a b)").broadcast_to([128, 256]), reads=[cx.diff_lambda],
               writes=[lpb], sembuf=lpb)
        lt = kb.sb([128, 8], F32, "lt")
        lj = kb.sb([128, 64], F32, "lj")
        for k in range(2):
            kb.op("dve", lambda e: e.scalar_tensor_tensor(out=lj[:], in0=lpb[:, k * 128:k * 128 + 64], scalar=1.0, in1=lpb[:, k * 128 + 64:k * 128 + 128],
                                                          op0=ALU.mult, op1=ALU.mult, accum_out=lt[:, k:k + 1]), reads=[lpb], writes=[lj, lt])
        kb.op("act", lambda e: e.activation(out=lt[:, 2:4], in_=lt[:, 0:2], func=AF.Exp), reads=[lt], writes=[lt])
        kb.op("dve", lambda e: e.tensor_tensor(out=lt[:, 4:5], in0=lt[:, 3:4], in1=lt[:, 2:3], op=ALU.subtract), reads=[lt], writes=[lt])
        kb.op("dve", lambda e: e.tensor_scalar(out=lt[:, 5:6], in0=lt[:, 4:5], scalar1=-lambda_init, scalar2=None, op0=ALU.add), reads=[lt], writes=[lt])
        nlam = lt[:, 5:6]
        gsb = kb.sb([128, 128], F32, "gsb")
        kb.dma("sp", gsb[:], cx.diff_subln_g.t[l:l + 1, :].broadcast_to([128, 128]), reads=[cx.diff_subln_g], writes=[gsb], sembuf=gsb)
        kb.op("dve", lambda e: e.tensor_scalar(out=gsb[:], in0=gsb[:], scalar1=1.0 - lambda_init, scalar2=None, op0=ALU.mult), reads=[gsb], writes=[gsb])
        oTs = kb.sb([128, 4, S], BF16, "oTs")
        otm = kb.sb([128, NT, 512], BF16, "otm")
        sg = kb.sb([128, NT, 512], F32, "sg")
        gin = kb.sb([128, NT, 512], BF16, "gin")
        kb.dma("sp", gin[:], tm.t[:, C_DG:C_DG + 512].rearrange("(t p) n -> p t n", p=128), reads=[tm], writes=[gin], sembuf=gin)
        kb.op("act", lambda e: e.activation(out=sg[:], in_=gin[:], func=AF.Silu), reads=[gin], writes=[sg])
        sm4 = [kb.sb([128, 8], F32, "sm4") for _ in range(2)]
        o2s = [kb.sb([128, 128], F32, "o2s") for _ in range(2)]
        ob = [kb.sb([128, 128], F32, "ob") for _ in range(2)]
        jk = kb.sb([128, 128], F32, "jk")
        cnt = {"n": 0}

        def fin(h, i, pos):
            k = cnt["n"] % 2
            cnt["n"] += 1
            po1, po2 = pos
            v = sm4[k]
            kb.op("dve", lambda e: e.reciprocal(out=v[:, 0:1], in_=po1[:, 128:129]), reads=[po1], writes=[v])
            kb.op("dve", lambda e: e.reciprocal(out=v[:, 1:2], in_=po2[:, 128:129]), reads=[po2], writes=[v])
            kb.op("dve", lambda e: e.tensor_tensor(out=v[:, 2:3], in0=v[:, 1:2], in1=nlam, op=ALU.mult), reads=[v, lt], writes=[v])
            kb.op("dve", lambda e: e.tensor_scalar(out=o2s[k][:], in0=po2[:, 0:128], scalar1=v[:, 2:3], scalar2=None, op0=ALU.mult),
                  reads=[po2, v], writes=[o2s[k]])
            kb.op("dve", lambda e: e.scalar_tensor_tensor(out=ob[k][:], in0=po1[:, 0:128], scalar=v[:, 0:1], in1=o2s[k][:],
                                                          op0=ALU.mult, op1=ALU.add), reads=[po1, v, o2s[k]], writes=[ob[k]])
            kb.op("dve", lambda e: e.scalar_tensor_tensor(out=jk[:], in0=ob[k][:], scalar=1.0, in1=ob[k][:], op0=ALU.mult, op1=ALU.mult,
                                                          accum_out=v[:, 3:4]), reads=[ob[k]], writes=[jk, v])
            rstd_from_ss(kb, v[:, 3:4], v[:, 4:5], 128, v, v)
            kb.op("dve", lambda e: e.scalar_tensor_tensor(out=ob[k][:], in0=ob[k][:], scalar=v[:, 4:5], in1=gsb[:], op0=ALU.mult, op1=ALU.mult),
                  reads=[ob[k], v, gsb], writes=[ob[k]])
            kb.op("dve", lambda e: e.tensor_tensor(out=otm[:, i, h * 128:(h + 1) * 128], in0=ob[k][:], in1=sg[:, i, h * 128:(h + 1) * 128],
                                                   op=ALU.mult), reads=[ob[k], sg], writes=[otm])

        maps = []
        for m in range(2):
            maps.append(dict(kT=(lambda h, j, m=m: kr[m * 64:(m + 1) * 64, h, j * 128:(j + 1) * 128]),
                             qT=(lambda h, i, m=m: qr[m * 64:(m + 1) * 64, h, i * 128:(i + 1) * 128]),
                             bias=None, reads=[kr, qr]))
        attn_core(kb, cx, maps, 4, vaug, fin, 64 ** -0.5, (sc_ps, pt_sb, po_ps, maskT))
        for i in range(NT):
            for c in range(4):
                kb.op("pe", lambda e: e.transpose(out=ptr[:, c * 128:(c + 1) * 128], in_=otm[:, i, c * 128:(c + 1) * 128], identity=ident[:]),
                      reads=[otm, ident], writes=[ptr])
            kb.op("act", lambda e: e.activation(out=oTs[:, :, i * 128:(i + 1) * 128], in_=ptr[:].rearrange("p (a b) -> p a b", a=4),
                                                func=AF.Copy), reads=[ptr], writes=[oTs])
        oT = cx.oT[s]
        kb.dma("sp", oT.t[512:1024, :].rearrange("(c p) t -> p c t", p=128), oTs[:], reads=[oTs], writes=[oT], sembuf=oTs)


def bc3(ap2, n):
    a = ap2.shape[1]
    return ap2.unsqueeze(2).broadcast_to([ap2.shape[0], a, n])


def phase_ssd(kb, cx, l, s):
    with kb.phase():
        ident = load_const(kb, cx, "ident_bf", [128, 128], BF16)
        identf = load_const(kb, cx, "ident_f", [128, 128], F32)
        tri = load_const(kb, cx, "tri_le_f", [128, 128], F32)
        gtf = load_const(kb, cx, "gt_f", [128, 128], F32)
        negm = load_const(kb, cx, "negmask_f", [128, 128], F32)
        ones = load_const(kb, cx, "ones_f", [128, 128], F32)
        fmT, tm, sm = cx.fmT[s], cx.tm[s], cx.sm[s]
        pG = kb.ps([128, 2, 128], F32, "pG")
        pD = [kb.ps([128, 128], F32, "pD") for _ in range(2)]
        pY = kb.ps([128, 512], F32, "pY")
        pY2 = kb.ps([128, 512], F32, "pY2")
        pS = kb.ps([128, 512], F32, "pS")
        ptr = kb.ps([128, 512], BF16, "ptr")
        ptr2 = kb.ps([128, 512], BF16, "ptr2")
        cw4 = kb.sb([4, 1024], F32, "cw4")
        kb.dma("sp", cw4[:], cx.ssd_conv_w.t[l], reads=[cx.ssd_conv_w], writes=[cw4], sembuf=cw4)
        cb8 = kb.sb([8, 128], F32, "cb8")
        kb.dma("sp", cb8[:], cx.ssd_conv_b.t[l:l + 1, :].rearrange("o (c p) -> (o c) p", p=128), reads=[cx.ssd_conv_b], writes=[cb8], sembuf=cb8)
        par = kb.sb([128, 40], F32, "par")
        for c in range(8):
            kb.op("pe", lambda e: e.transpose(out=pD[0][:, c * 4:(c + 1) * 4], in_=cw4[0:4, c * 128:(c + 1) * 128], identity=identf[0:4, 0:4]),
                  reads=[cw4, identf], writes=[pD[0]])
        kb.op("pe", lambda e: e.transpose(out=pD[0][:, 32:40], in_=cb8[0:8, :], identity=identf[0:8, 0:8]), reads=[cb8, identf], writes=[pD[0]])
        kb.op("dve", lambda e: e.tensor_copy(out=par[:], in_=pD[0][:, 0:40]), reads=[pD[0]], writes=[par])
        if cx.stop < 11:
            return
        dtb = kb.sb([128, 8], F32, "dtb")
        kb.dma("sp", dtb[:], cx.ssd_dt_bias.t[l:l + 1, :].broadcast_to([128, 8]), reads=[cx.ssd_dt_bias], writes=[dtb], sembuf=dtb)
        negA = kb.sb([128, 8], F32, "negA")
        kb.dma("sp", negA[:], cx.ssd_a_log.t[l:l + 1, :].broadcast_to([128, 8]), reads=[cx.ssd_a_log], writes=[negA], sembuf=negA)
        kb.op("act", lambda e: e.activation(out=negA[:], in_=negA[:], func=AF.Exp), reads=[negA], writes=[negA])
        kb.op("dve", lambda e: e.tensor_scalar(out=negA[:], in0=negA[:], scalar1=-1.0, scalar2=None, op0=ALU.mult), reads=[negA], writes=[negA])
        dsk = kb.sb([128, 8], F32, "dsk")
        kb.dma("sp", dsk[:], cx.ssd_d.t[l:l + 1, :].broadcast_to([128, 8]), reads=[cx.ssd_d], writes=[dsk], sembuf=dsk)
        ng = kb.sb([128, 512], F32, "ng")
        kb.dma("sp", ng[:], cx.ssd_norm_g.t[l:l + 1, :].broadcast_to([128, 512]), reads=[cx.ssd_norm_g], writes=[ng], sembuf=ng)
        xbc = kb.sb([128, 8, S], BF16, "xbc")
        kb.dma("sp", xbc[:], fmT.t[R_SX:R_SX + 1024, :].rearrange("(c p) t -> p c t", p=128), reads=[fmT], writes=[xbc], sembuf=xbc)
        xc = kb.sb([128, 8, S], BF16, "xc")
        accs = [kb.sb([128, S], F32, "acc") for _ in range(2)]
        for c in range(8):
            acc = accs[c % 2]
            w = lambda j: par[:, c * 4 + j:c * 4 + j + 1]
            kb.op("dve", lambda e: e.tensor_scalar(out=acc[:], in0=xbc[:, c, :], scalar1=w(3), scalar2=None, op0=ALU.mult), reads=[xbc, par], writes=[acc])
            for sh in (1, 2, 3):
                kb.op("dve", lambda e: e.scalar_tensor_tensor(out=acc[:, sh:], in0=xbc[:, c, 0:S - sh], scalar=w(3 - sh), in1=acc[:, sh:],
                                                              op0=ALU.mult, op1=ALU.add), reads=[xbc, par, acc], writes=[acc])
            kb.op("act", lambda e: e.activation(out=xc[:, c, :], in_=acc[:], func=AF.Silu, bias=par[:, 32 + c:33 + c]), reads=[acc, par], writes=[xc])
        if cx.stop < 12:
            return
        dt = kb.sb([128, NT, 8], F32, "dt")
        kb.dma("sp", dt[:], sm.t[:, 4:12].rearrange("(t p) n -> p t n", p=128), reads=[sm], writes=[dt], sembuf=dt, allow_slow_non_contiguous=True)
        kb.op("dve", lambda e: e.tensor_tensor(out=dt[:], in0=dt[:], in1=dtb[:].unsqueeze(1).broadcast_to([128, NT, 8]), op=ALU.add), reads=[dt, dtb], writes=[dt])
        if cx.stop2 == 1:
            return
        kb.op("act", lambda e: e.activation(out=dt[:], in_=dt[:], func=AF.Exp), reads=[dt], writes=[dt])
        kb.op("act", lambda e: e.activation(out=dt[:], in_=dt[:], func=AF.Ln, bias=1.0), reads=[dt], writes=[dt])
        if cx.stop2 == 2:
            return
        av = kb.sb([128, NT, 8], F32, "av")
        kb.op("dve", lambda e: e.tensor_tensor(out=av[:], in0=dt[:], in1=negA[:].unsqueeze(1).broadcast_to([128, NT, 8]), op=ALU.mult), reads=[dt, negA], writes=[av])
        if cx.stop2 == 3:
            return
        av2 = av[:].rearrange("p t h -> p (t h)")
        kb.mm(pY[:, 0:128], [(tri[:], av2)], reads=[tri, av], writes=[pY])
        kb.mm(pY2[:, 0:128], [(ones[:], av2)], reads=[ones, av], writes=[pY2])
        if cx.stop2 == 4:
            return
        Acum = kb.sb([128, NT, 8], F32, "Acum")
        eA = kb.sb([128, NT, 8], F32, "eA")
        eAt = kb.sb([128, NT, 8], F32, "eAt")
        dsv = kb.sb([128, NT, 8], F32, "dsv")
        f2 = lambda b: b[:].rearrange("p t h -> p (t h)")
        kb.op("dve", lambda e: e.tensor_copy(out=f2(Acum), in_=pY[:, 0:128]), reads=[pY], writes=[Acum])
        kb.op("dve", lambda e: e.tensor_tensor(out=f2(dsv), in0=pY2[:, 0:128], in1=f2(Acum), op=ALU.subtract), reads=[pY2, Acum], writes=[dsv])
        if cx.stop2 == 5:
            return
        kb.op("dve", lambda e: e.tensor_copy(out=f2(eAt), in_=pY2[:, 0:128]), reads=[pY2], writes=[eAt])
        kb.op("act", lambda e: e.activation(out=f2(eAt), in_=f2(eAt), func=AF.Exp), reads=[eAt], writes=[eAt])
        if cx.stop2 == 6:
            return
        kb.op("act", lambda e: e.activation(out=f2(eA), in_=f2(Acum), func=AF.Exp), reads=[Acum], writes=[eA])
        if cx.stop2 == 7:
            return
        dsv0 = dsv
        dsv = kb.sb([128, NT, 8], F32, "dsv2")
        kb.op("act", lambda e: e.activation(out=f2(dsv), in_=f2(dsv0), func=AF.Exp), reads=[dsv0], writes=[dsv])
        if cx.stop < 13:
            return
        xs_tm = kb.sb([128, NT, 512], BF16, "xs_tm")
        B_tm = kb.sb([128, NT, 256], BF16, "B_tm")
        for t in range(NT):
            ts_ = slice(t * 128, (t + 1) * 128)
            for c in range(4):
                kb.op("pe", lambda e: e.transpose(out=ptr[:, c * 128:(c + 1) * 128], in_=xc[:, c, ts_], identity=ident[:]), reads=[xc, ident], writes=[ptr])
            kb.op("act", lambda e: e.activation(out=xs_tm[:, t, :], in_=ptr[:, 0:512], func=AF.Copy), reads=[ptr], writes=[xs_tm])
            for c in range(2):
                kb.op("pe", lambda e: e.transpose(out=ptr2[:, c * 128:(c + 1) * 128], in_=xc[:, 4 + c, ts_], identity=ident[:]), reads=[xc, ident], writes=[ptr2])
            kb.op("dve", lambda e: e.tensor_copy(out=B_tm[:, t, :], in_=ptr2[:, 0:256]), reads=[ptr2], writes=[B_tm])
        if cx.stop < 14:
            return
        zin = kb.sb([128, NT, 512], BF16, "zin")
        kb.dma("sp", zin[:], tm.t[:, C_SZ:C_SZ + 512].rearrange("(t p) n -> p t n", p=128), reads=[tm], writes=[zin], sembuf=zin)
        oTs = kb.sb([128, 4, S], BF16, "oTs")
        ST = kb.sb([128, 512], F32, "ST")
        STb = kb.sb([128, 512], BF16, "STb")
        tmpS = kb.sb([128, 512], F32, "tmpS")
        kb.op("dve", lambda e: e.memset(ST[:], 0.0), reads=[], writes=[ST])
        Xd = [kb.sb([128, 512], BF16, "Xd") for _ in range(2)]
        Xds = [kb.sb([128, 512], BF16, "Xds") for _ in range(2)]
        adg = [kb.sb([128, 128], F32, "adg") for _ in range(2)]
        Eb = [kb.sb([128, 128], F32, "Eb") for _ in range(2)]
        MT = [kb.sb([128, 128], BF16, "MT") for _ in range(2)]
        ysb = kb.sb([128, 512], F32, "ysb")
        y2 = kb.sb([128, 512], F32, "y2")
        szt = kb.sb([128, 512], F32, "szt")
        jk = kb.sb([128, 512], F32, "jk")
        otm = [kb.sb([128, 512], BF16, "otm") for _ in range(2)]
        sv = kb.sb([128, NT, 4], F32, "sv")
        v3 = lambda b: b[:].rearrange("p (h d) -> p h d", h=8)
        n = 0
        for t in range(NT if cx.stop >= 20 else max(0, cx.stop - 14)):
            ts_ = slice(t * 128, (t + 1) * 128)
            xd, xds = Xd[t % 2], Xds[t % 2]
            kb.op("dve", lambda e: e.tensor_tensor(out=v3(xd), in0=xs_tm[:, t, :].rearrange("p (h d) -> p h d", h=8), in1=bc3(dt[:, t, :], 64),
                                                   op=ALU.mult), reads=[xs_tm, dt], writes=[xd])
            kb.op("dve", lambda e: e.tensor_tensor(out=v3(xds), in0=v3(xd), in1=bc3(dsv[:, t, :], 64), op=ALU.mult), reads=[xd, dsv], writes=[xds])
            for g in range(2):
                kb.mm(pG[:, g, :], [(xc[:, 4 + g, ts_], xc[:, 6 + g, ts_])], reads=[xc], writes=[pG])
            for h in range(8):
                g = h // 4
                a_, e_, m_, pd = adg[n % 2], Eb[n % 2], MT[n % 2], pD[n % 2]
                n += 1
                kb.op("dve", lambda e: e.tensor_scalar(out=a_[:], in0=tri[:], scalar1=av[:, t, h:h + 1], scalar2=None, op0=ALU.mult),
                      reads=[tri, av], writes=[a_])
                kb.mm(pd[:], [(gtf[:], a_[:]), (identf[:], negm[:])], reads=[gtf, a_, identf, negm], writes=[pd])
                kb.op("act", lambda e: e.activation(out=e_[:], in_=pd[:], func=AF.Exp), reads=[pd], writes=[e_])
                kb.op("dve", lambda e: e.tensor_tensor(out=m_[:], in0=pG[:, g, :], in1=e_[:], op=ALU.mult), reads=[pG, e_], writes=[m_])
                hs = slice(h * 64, (h + 1) * 64)
                kb.mm(pY[:, hs], [(m_[:], xd[:, hs])], reads=[m_, xd], writes=[pY])
                if t > 0:
                    kb.mm(pY2[:, hs], [(xc[:, 6 + g, ts_], STb[:, hs])], reads=[xc, STb], writes=[pY2])
                kb.mm(pS[:, hs], [(B_tm[:, t, g * 128:(g + 1) * 128], xds[:, hs])], reads=[B_tm, xds], writes=[pS])
            kb.op("act", lambda e: e.activation(out=ysb[:], in_=pY[:], func=AF.Copy), reads=[pY], writes=[ysb])
            if t > 0:
                kb.op("dve", lambda e: e.tensor_tensor(out=v3(y2), in0=pY2[:].rearrange("p (h d) -> p h d", h=8), in1=bc3(eA[:, t, :], 64), op=ALU.mult),
                      reads=[pY2, eA], writes=[y2])
                kb.op("dve", lambda e: e.tensor_tensor(out=ysb[:], in0=ysb[:], in1=y2[:], op=ALU.add), reads=[ysb, y2], writes=[ysb])
            if t < NT - 1:
                kb.op("dve", lambda e: e.tensor_tensor(out=v3(tmpS), in0=v3(ST), in1=bc3(eAt[:, t, :], 64), op=ALU.mult), reads=[ST, eAt], writes=[tmpS])
                kb.op("dve", lambda e: e.tensor_tensor(out=ST[:], in0=tmpS[:], in1=pS[:], op=ALU.add), reads=[tmpS, pS], writes=[ST])
                kb.op("act", lambda e: e.activation(out=STb[:], in_=ST[:], func=AF.Copy), reads=[ST], writes=[STb])
            kb.op("dve", lambda e: e.tensor_tensor(out=v3(y2), in0=xs_tm[:, t, :].rearrange("p (h d) -> p h d", h=8), in1=bc3(dsk[:], 64), op=ALU.mult),
                  reads=[xs_tm, dsk], writes=[y2])
            kb.op("dve", lambda e: e.tensor_tensor(out=ysb[:], in0=ysb[:], in1=y2[:], op=ALU.add), reads=[ysb, y2], writes=[ysb])
            kb.op("act", lambda e: e.activation(out=szt[:], in_=zin[:, t, :], func=AF.Silu), reads=[zin], writes=[szt])
            kb.op("dve", lambda e: e.tensor_tensor(out=ysb[:], in0=ysb[:], in1=szt[:], op=ALU.mult), reads=[ysb, szt], writes=[ysb])
            kb.op("dve", lambda e: e.scalar_tensor_tensor(out=jk[:], in0=ysb[:], scalar=1.0, in1=ysb[:], op0=ALU.mult, op1=ALU.mult,
                                                          accum_out=sv[:, t, 0:1]), reads=[ysb], writes=[jk, sv])
            rstd_from_ss(kb, sv[:, t, 0:1], sv[:, t, 1:2], 512, sv, sv)
            om = otm[t % 2]
            kb.op("dve", lambda e: e.scalar_tensor_tensor(out=om[:], in0=ysb[:], scalar=sv[:, t, 1:2], in1=ng[:], op0=ALU.mult, op1=ALU.mult),
                  reads=[ysb, sv, ng], writes=[om])
            for c in range(4):
                kb.op("pe", lambda e: e.transpose(out=ptr[:, c * 128:(c + 1) * 128], in_=om[:, c * 128:(c + 1) * 128], identity=ident[:]),
                      reads=[om, ident], writes=[ptr])
            kb.op("act", lambda e: e.activation(out=oTs[:, :, ts_], in_=ptr[:, 0:512].rearrange("p (a b) -> p a b", a=4), func=AF.Copy),
                  reads=[ptr], writes=[oTs])
        oT = cx.oT[s]
        kb.dma("sp", oT.t[1024:1536, :].rearrange("(c p) t -> p c t", p=128), oTs[:], reads=[oTs], writes=[oT], sembuf=oTs)


def s5_alloc(kb, cx):
    def P(shape, dt, name):
        t = kb.es.enter_context(kb.nc.sbuf_tensor("s5p_" + name, list(shape), dt))
        return Buf(kb, name, t)
    cx.s5 = dict(BT=P([128, 16, 2, 128], BF16, "BT"), CP=P([128, 16, 2, 128], BF16, "CP"),
                 PWr=P([128, 11, 16], F32, "PWr"), PWi=P([128, 11, 16], F32, "PWi"), NPWi=P([128, 11, 16], F32, "NPWi"),
                 dsk=P([128, 4], F32, "dsk"), Ur=P([128, 11, 16], F32, "Ur"), Ui=P([128, 11, 16], F32, "Ui"),
                 NUi=P([128, 11, 16], F32, "NUi"), R=P([128, 16], F32, "R"))


def phase_s5_prep(kb, cx, l):
    TWO_PI = 2.0 * math.pi
    s5 = cx.s5
    with kb.phase():
        identf = load_const(kb, cx, "ident_f", [128, 128], F32)
        ones = load_const(kb, cx, "ones_f", [128, 128], F32)
        gmask = load_const(kb, cx, "gmask", [128, 8], F32)
        pt = kb.ps([128, 128], F32, "pt")
        pts = [kb.ps([128, 128], F32, "pts") for _ in range(2)]
        p3 = kb.sb([16, 3, 128], F32, "p3")
        kb.dma("sp", p3[:, 0, :], cx.s5_a_re.t[l].rearrange("(st g) p -> st (g p)", g=2), reads=[cx.s5_a_re], writes=[p3], sembuf=p3)
        kb.dma("sp", p3[:, 1, :], cx.s5_a_im.t[l].rearrange("(st g) p -> st (g p)", g=2), reads=[cx.s5_a_im], writes=[p3], sembuf=p3)
        ld = kb.sb([16, 2], F32, "ld")
        kb.dma("sp", ld[:], cx.s5_log_dt.t[l:l + 1, :].rearrange("o (st g) -> (o st) g", g=2), reads=[cx.s5_log_dt], writes=[ld], sembuf=ld)
        for g in range(2):
            kb.op("dve", lambda e: e.tensor_scalar(out=p3[:, 2, g * 64:(g + 1) * 64], in0=ones[0:16, 0:64], scalar1=ld[:, g:g + 1], scalar2=None,
                                                   op0=ALU.mult), reads=[ones, ld], writes=[p3])
        d4 = kb.sb([4, 128], F32, "d4")
        kb.dma("sp", d4[:], cx.s5_d.t[l].rearrange("(cc gl) h -> cc (gl h)", gl=8), reads=[cx.s5_d], writes=[d4], sembuf=d4)
        for k in range(3):
            kb.op("pe", lambda e: e.transpose(out=pt[:, k * 16:(k + 1) * 16], in_=p3[0:16, k, :], identity=identf[0:16, 0:16]),
                  reads=[p3, identf], writes=[pt])
        kb.op("pe", lambda e: e.transpose(out=pt[:, 48:52], in_=d4[0:4, :], identity=identf[0:4, 0:4]), reads=[d4, identf], writes=[pt])
        prm = kb.sb([128, 3, 16], F32, "prm")
        kb.op("dve", lambda e: e.tensor_copy(out=prm[:].rearrange("p a b -> p (a b)"), in_=pt[:, 0:48]), reads=[pt], writes=[prm])
        kb.op("dve", lambda e: e.tensor_copy(out=s5["dsk"][:], in_=pt[:, 48:52]), reads=[pt], writes=[s5["dsk"]])
        aR, aI, LD = prm[:, 0, :], prm[:, 1, :], prm[:, 2, :]
        w = kb.sb([128, 24, 16], F32, "w")
        W = lambda i: w[:, i, :]
        dve = lambda fn, rd=(), wr=(): kb.op("dve", fn, reads=list(rd) or [w, prm], writes=list(wr) or [w])
        act = lambda fn, rd=(), wr=(): kb.op("act", fn, reads=list(rd) or [w, prm], writes=list(wr) or [w])
        TT = lambda o, a, b, op: dve(lambda e: e.tensor_tensor(out=o, in0=a, in1=b, op=op))
        act(lambda e: e.activation(out=W(0), in_=LD, func=AF.Exp))
        TT(W(1), aR, W(0), ALU.mult)
        act(lambda e: e.activation(out=W(1), in_=W(1), func=AF.Exp))
        TT(W(2), aI, W(0), ALU.mult)
        dve(lambda e: e.tensor_scalar(out=W(2), in0=W(2), scalar1=1.0 / TWO_PI, scalar2=None, op0=ALU.mult))
        wi = kb.sb([128, 16], mybir.dt.int32, "wi")
        dve(lambda e: e.tensor_copy(out=wi[:], in_=W(2)), wr=[wi])
        dve(lambda e: e.tensor_copy(out=W(3), in_=wi[:]), rd=[wi])
        TT(W(3), W(2), W(3), ALU.subtract)
        dve(lambda e: e.tensor_scalar(out=W(4), in0=W(3), scalar1=0.25, scalar2=None, op0=ALU.add))

        def wrap(src, dst, t1, t2):
            dve(lambda e: e.tensor_single_scalar(out=t1, in_=src, scalar=0.5, op=ALU.is_gt))
            dve(lambda e: e.tensor_single_scalar(out=t2, in_=src, scalar=-0.5, op=ALU.is_lt))
            TT(dst, src, t1, ALU.subtract)
            TT(dst, dst, t2, ALU.add)
        wrap(W(3), W(5), W(6), W(7))
        wrap(W(4), W(8), W(6), W(7))
        act(lambda e: e.activation(out=W(9), in_=W(5), func=AF.Sin, scale=TWO_PI))
        act(lambda e: e.activation(out=W(10), in_=W(8), func=AF.Sin, scale=TWO_PI))
        PWr, PWi, NPWi = s5["PWr"], s5["PWi"], s5["NPWi"]
        kb.op("dve", lambda e: e.tensor_tensor(out=PWr[:, 0, :], in0=W(1), in1=W(10), op=ALU.mult), reads=[w], writes=[PWr])
        kb.op("dve", lambda e: e.tensor_tensor(out=PWi[:, 0, :], in0=W(1), in1=W(9), op=ALU.mult), reads=[w], writes=[PWi])
        for k in range(10):
            kb.op("dve", lambda e: e.tensor_tensor(out=W(11), in0=PWr[:, k, :], in1=PWr[:, k, :], op=ALU.mult), reads=[PWr], writes=[w])
            kb.op("dve", lambda e: e.tensor_tensor(out=W(12), in0=PWi[:, k, :], in1=PWi[:, k, :], op=ALU.mult), reads=[PWi], writes=[w])
            kb.op("dve", lambda e: e.tensor_tensor(out=PWr[:, k + 1, :], in0=W(11), in1=W(12), op=ALU.subtract), reads=[w], writes=[PWr])
            kb.op("dve", lambda e: e.scalar_tensor_tensor(out=PWi[:, k + 1, :], in0=PWr[:, k, :], scalar=2.0, in1=PWi[:, k, :], op0=ALU.mult, op1=ALU.mult),
                  reads=[PWr, PWi], writes=[PWi])
        kb.op("dve", lambda e: e.tensor_scalar(out=NPWi[:], in0=PWi[:], scalar1=-1.0, scalar2=None, op0=ALU.mult), reads=[PWi], writes=[NPWi])
        Ur, Ui, NUi, Rm = s5["Ur"], s5["Ui"], s5["NUi"], s5["R"]
        kb.op("dve", lambda e: e.tensor_copy(out=Ur[:, 0, :], in_=W(10)), reads=[w], writes=[Ur])
        kb.op("dve", lambda e: e.tensor_copy(out=Ui[:, 0, :], in_=W(9)), reads=[w], writes=[Ui])
        kb.op("dve", lambda e: e.tensor_copy(out=Rm[:], in_=W(1)), reads=[w], writes=[Rm])
        for k in range(10):
            kb.op("dve", lambda e: e.tensor_tensor(out=W(19), in0=Ur[:, k, :], in1=Ur[:, k, :], op=ALU.mult), reads=[Ur], writes=[w])
            kb.op("dve", lambda e: e.tensor_tensor(out=W(20), in0=Ui[:, k, :], in1=Ui[:, k, :], op=ALU.mult), reads=[Ui], writes=[w])
            kb.op("dve", lambda e: e.tensor_tensor(out=Ur[:, k + 1, :], in0=W(19), in1=W(20), op=ALU.subtract), reads=[w], writes=[Ur])
            kb.op("dve", lambda e: e.scalar_tensor_tensor(out=Ui[:, k + 1, :], in0=Ur[:, k, :], scalar=2.0, in1=Ui[:, k, :], op0=ALU.mult, op1=ALU.mult),
                  reads=[Ur, Ui], writes=[Ui])
        kb.op("dve", lambda e: e.tensor_scalar(out=NUi[:], in0=Ui[:], scalar1=-1.0, scalar2=None, op0=ALU.mult), reads=[Ui], writes=[NUi])
        Ets = [kb.sb([128, 2, S], F32, "Et") for _ in range(2)]
        for st in range(16):
            E = Ets[st % 2]
            kb.op("dve", lambda e: e.memset(E[:, 0, 0:1], 1.0), reads=[], writes=[E])
            kb.op("dve", lambda e: e.memset(E[:, 1, 0:1], 0.0), reads=[], writes=[E])
            for k in range(11):
                sh = 1 << k
                ur, ui, nui = Ur[:, k, st:st + 1], Ui[:, k, st:st + 1], NUi[:, k, st:st + 1]
                kb.op("dve", lambda e: e.tensor_scalar(out=E[:, 0, sh:2 * sh], in0=E[:, 0, 0:sh], scalar1=ur, scalar2=None, op0=ALU.mult),
                      reads=[E, Ur], writes=[E])
                kb.op("dve", lambda e: e.scalar_tensor_tensor(out=E[:, 0, sh:2 * sh], in0=E[:, 1, 0:sh], scalar=nui, in1=E[:, 0, sh:2 * sh],
                                                              op0=ALU.mult, op1=ALU.add), reads=[E, NUi], writes=[E])
                kb.op("dve", lambda e: e.tensor_scalar(out=E[:, 1, sh:2 * sh], in0=E[:, 0, 0:sh], scalar1=ui, scalar2=None, op0=ALU.mult),
                      reads=[E, Ui], writes=[E])
                kb.op("dve", lambda e: e.scalar_tensor_tensor(out=E[:, 1, sh:2 * sh], in0=E[:, 1, 0:sh], scalar=ur, in1=E[:, 1, sh:2 * sh],
                                                              op0=ALU.mult, op1=ALU.add), reads=[E, Ur], writes=[E])
            kb.dma("sp", cx.Etab.t[st].rearrange("a p t -> p a t"), E[:], reads=[E], writes=[cx.Etab], sembuf=E)
        ar, ai = PWr[:, 0, :], PWi[:, 0, :]
        rdP = [w, prm, PWr, PWi]
        dveP = lambda fn: kb.op("dve", fn, reads=rdP, writes=[w])
        dveP(lambda e: e.tensor_scalar(out=W(13), in0=ar, scalar1=-1.0, scalar2=None, op0=ALU.add))
        dveP(lambda e: e.tensor_tensor(out=W(14), in0=aR, in1=aR, op=ALU.mult))
        dveP(lambda e: e.tensor_tensor(out=W(15), in0=aI, in1=aI, op=ALU.mult))
        dveP(lambda e: e.tensor_tensor(out=W(14), in0=W(14), in1=W(15), op=ALU.add))
        dveP(lambda e: e.reciprocal(out=W(14), in_=W(14)))
        dveP(lambda e: e.tensor_tensor(out=W(15), in0=W(13), in1=aR, op=ALU.mult))
        dveP(lambda e: e.tensor_tensor(out=W(16), in0=ai, in1=aI, op=ALU.mult))
        dveP(lambda e: e.tensor_tensor(out=W(15), in0=W(15), in1=W(16), op=ALU.add))
        dveP(lambda e: e.tensor_tensor(out=W(17), in0=W(15), in1=W(14), op=ALU.mult))
        dveP(lambda e: e.tensor_tensor(out=W(15), in0=ai, in1=aR, op=ALU.mult))
        dveP(lambda e: e.tensor_tensor(out=W(16), in0=W(13), in1=aI, op=ALU.mult))
        dveP(lambda e: e.tensor_tensor(out=W(15), in0=W(15), in1=W(16), op=ALU.subtract))
        dveP(lambda e: e.tensor_tensor(out=W(18), in0=W(15), in1=W(14), op=ALU.mult))
        bq = kb.sb([128, 2, 16, 16], F32, "bq")
        for ri, src in enumerate((cx.s5_b_re, cx.s5_b_im)):
            v = src.t[l].rearrange("(st g) p h -> g p st h", g=2)
            for g in range(2):
                kb.dma("sp", bq[g * 64:(g + 1) * 64, ri, :, :], v[g], reads=[src], writes=[bq], sembuf=bq, allow_slow_non_contiguous=True)
        bb = kb.sb([128, 2, 16, 16], F32, "bb")
        tq = kb.sb([128, 2, 16, 16], F32, "tq")
        gr, gi = bc3(W(17), 16), bc3(W(18), 16)
        rdB = [bq, w, tq, bb]
        kb.op("dve", lambda e: e.tensor_tensor(out=tq[:, 0], in0=bq[:, 0], in1=gr, op=ALU.mult), reads=rdB, writes=[tq])
        kb.op("dve", lambda e: e.tensor_tensor(out=tq[:, 1], in0=bq[:, 1], in1=gi, op=ALU.mult), reads=rdB, writes=[tq])
        kb.op("dve", lambda e: e.tensor_tensor(out=bb[:, 0], in0=tq[:, 0], in1=tq[:, 1], op=ALU.subtract), reads=rdB, writes=[bb])
        kb.op("dve", lambda e: e.tensor_tensor(out=tq[:, 0], in0=bq[:, 1], in1=gr, op=ALU.mult), reads=rdB, writes=[tq])
        kb.op("dve", lambda e: e.tensor_tensor(out=tq[:, 1], in0=bq[:, 0], in1=gi, op=ALU.mult), reads=rdB, writes=[tq])
        kb.op("dve", lambda e: e.tensor_tensor(out=bb[:, 1], in0=tq[:, 0], in1=tq[:, 1], op=ALU.add), reads=rdB, writes=[bb])
        bpad = kb.sb([128, 16, 2, 128], F32, "bpad")
        kb.op("pool", lambda e: e.memset(bpad[:], 0.0), reads=[], writes=[bpad])
        for st4 in range(4):
            for g in range(2):
                c0 = (st4 * 2 + g) * 16
                for ri in range(2):
                    kb.op("dve", lambda e: e.tensor_copy(out=bpad[g * 64:(g + 1) * 64, st4::4, ri, c0:c0 + 16], in_=bb[g * 64:(g + 1) * 64, ri, st4::4, :]),
                          reads=[bb], writes=[bpad])
        BT, CP = s5["BT"], s5["CP"]
        n = 0
        for st in range(16):
            for ri in range(2):
                p = pts[n % 2]
                n += 1
                kb.op("pe", lambda e: e.transpose(out=p[:], in_=bpad[:, st, ri, :], identity=identf[:]), reads=[bpad, identf], writes=[p])
                kb.op("act", lambda e: e.activation(out=BT[:, st, ri, :], in_=p[:], func=AF.Copy), reads=[p], writes=[BT])
        cn = kb.sb([128, 2, 4, 64], F32, "cn")
        for ri, src in enumerate((cx.s5_c_re, cx.s5_c_im)):
            for cc in range(4):
                kb.dma("sp", cn[:, ri, cc, :], src.t[l, cc * 8:(cc + 1) * 8].rearrange("g h p -> (g h) p"), reads=[src], writes=[cn], sembuf=cn)
        mts = [kb.sb([128, 128], F32, "mt") for _ in range(2)]
        for st in range(16):
            cc, gl0 = st // 4, (st % 4) * 2
            for ri in range(2):
                mt = mts[n % 2]
                p = pts[n % 2]
                n += 1
                for g in range(2):
                    kb.op("dve", lambda e: e.tensor_scalar(out=mt[:, g * 64:(g + 1) * 64], in0=cn[:, ri, cc, :], scalar1=gmask[:, gl0 + g:gl0 + g + 1],
                                                           scalar2=None, op0=ALU.mult), reads=[cn, gmask], writes=[mt])
                kb.op("pe", lambda e: e.transpose(out=p[:], in_=mt[:], identity=identf[:]), reads=[mt, identf], writes=[p])
                sgn = 1.0 if ri == 0 else -1.0
                kb.op("dve", lambda e: e.tensor_scalar(out=CP[:, st, ri, :], in0=p[:], scalar1=sgn, scalar2=None, op0=ALU.mult), reads=[p], writes=[CP])


def phase_s5(kb, cx, l, s):
    s5 = cx.s5
    BT, CP, PWr, PWi, NPWi, dsk = s5["BT"], s5["CP"], s5["PWr"], s5["PWi"], s5["NPWi"], s5["dsk"]
    with kb.phase():
        ident = load_const(kb, cx, "ident_bf", [128, 128], BF16)
        fmT, tm = cx.fmT[s], cx.tm[s]
        pYo = [kb.ps([128, 512], F32, "pYo") for _ in range(4)]
        pB = [kb.ps([128, 512], F32, "pB") for _ in range(2)]
        ptr = kb.ps([128, 512], BF16, "ptr")
        uT = kb.sb([128, 4, S], BF16, "uT")
        kb.dma("sp", uT[:], fmT.t[R_SU:R_SU + 512, :].rearrange("(c p) t -> p c t", p=128), reads=[fmT], writes=[uT], sembuf=uT)
        bp = [kb.sb([128, S], F32, f"bp{b}") for b in range(2)]
        zz = [kb.sb([128, S], F32, f"zz{b}") for b in range(2)]
        xb = [kb.sb([128, S], BF16, f"xb{b}") for b in range(2)]
        Etl = [kb.sb([128, 2, S], F32, "Etl") for _ in range(2)]
        tA = [kb.sb([128, 512], F32, "tA") for _ in range(2)]
        tB = [kb.sb([128, 512], F32, "tB") for _ in range(2)]
        gT = kb.sb([128, 4, S], BF16, "gT")
        wss = [kb.sb([128, 1024], F32, "ws") for _ in range(2)]
        wg = kb.sb([128, 4, 1024], BF16, "wg")
        wv = cx.s5_w_glu.t[l].rearrange("(c p) n -> p c n", p=128)
        for c in range(4):
            kb.dma("sp", wss[c % 2][:], wv[:, c, :], reads=[cx.s5_w_glu], writes=[wss[c % 2]], sembuf=wss[c % 2])
            kb.op("pool", lambda e: e.tensor_copy(out=wg[:, c, :], in_=wss[c % 2][:]), reads=[wss[c % 2]], writes=[wg])
        sgin = [kb.sb([128, 512], BF16, "sgin") for _ in range(2)]
        yb = [kb.sb([128, 512], F32, "yb") for _ in range(2)]
        q1 = [kb.sb([128, 512], F32, "q1") for _ in range(2)]
        Rm = s5["R"]
        nt = 0
        for st in range(16):
            cc = st // 4
            Et = Etl[st % 2]
            kb.dma("sp", Et[:], cx.Etab.t[st].rearrange("a p t -> p a t"), reads=[cx.Etab], writes=[Et], sembuf=Et)
            for tb in range(4):
                cs = slice(tb * 512, (tb + 1) * 512)
                a_, b_ = tA[nt % 2], tB[nt % 2]
                nt += 1
                kb.mm(pB[0][:], [(BT[:, st, 0, :], uT[:, cc, cs])], reads=[BT, uT], writes=[pB[0]])
                kb.mm(pB[1][:], [(BT[:, st, 1, :], uT[:, cc, cs])], reads=[BT, uT], writes=[pB[1]])
                kb.op("act", lambda e: e.activation(out=zz[0][:, cs], in_=pB[0][:], func=AF.Copy), reads=[pB[0]], writes=[zz[0]])
                kb.op("act", lambda e: e.activation(out=zz[1][:, cs], in_=pB[1][:], func=AF.Copy), reads=[pB[1]], writes=[zz[1]])
                kb.op("dve", lambda e: e.tensor_tensor(out=a_[:], in0=zz[0][:, cs], in1=Et[:, 0, cs], op=ALU.mult), reads=[zz[0], Et], writes=[a_])
                kb.op("pool", lambda e: e.tensor_tensor(out=b_[:], in0=zz[1][:, cs], in1=Et[:, 1, cs], op=ALU.mult), reads=[zz[1], Et], writes=[b_])
                kb.op("dve", lambda e: e.tensor_tensor(out=bp[0][:, cs], in0=a_[:], in1=b_[:], op=ALU.add), reads=[a_, b_], writes=[bp[0]])
                kb.op("dve", lambda e: e.tensor_tensor(out=a_[:], in0=zz[1][:, cs], in1=Et[:, 0, cs], op=ALU.mult), reads=[zz[1], Et], writes=[a_])
                kb.op("pool", lambda e: e.tensor_tensor(out=b_[:], in0=zz[0][:, cs], in1=Et[:, 1, cs], op=ALU.mult), reads=[zz[0], Et], writes=[b_])
                kb.op("dve", lambda e: e.tensor_tensor(out=bp[1][:, cs], in0=a_[:], in1=b_[:], op=ALU.subtract), reads=[a_, b_], writes=[bp[1]])
            rb = Rm[:, st:st + 1].broadcast_to([128, S])
            for ri in range(2):
                kb.op("dve", lambda e: e.tensor_tensor_scan(out=zz[ri][:], data0=rb, data1=bp[ri][:], initial=0.0, op0=ALU.mult, op1=ALU.add),
                      reads=[Rm, bp[ri], zz[ri]], writes=[zz[ri]])
            for tb in range(4):
                cs = slice(tb * 512, (tb + 1) * 512)
                a_, b_ = tA[nt % 2], tB[nt % 2]
                nt += 1
                kb.op("dve", lambda e: e.tensor_tensor(out=a_[:], in0=zz[0][:, cs], in1=Et[:, 0, cs], op=ALU.mult), reads=[zz[0], Et], writes=[a_])
                kb.op("pool", lambda e: e.tensor_tensor(out=b_[:], in0=zz[1][:, cs], in1=Et[:, 1, cs], op=ALU.mult), reads=[zz[1], Et], writes=[b_])
                kb.op("dve", lambda e: e.tensor_tensor(out=xb[0][:, cs], in0=a_[:], in1=b_[:], op=ALU.subtract), reads=[a_, b_], writes=[xb[0]])
                kb.op("dve", lambda e: e.tensor_tensor(out=a_[:], in0=zz[0][:, cs], in1=Et[:, 1, cs], op=ALU.mult), reads=[zz[0], Et], writes=[a_])
                kb.op("pool", lambda e: e.tensor_tensor(out=b_[:], in0=zz[1][:, cs], in1=Et[:, 0, cs], op=ALU.mult), reads=[zz[1], Et], writes=[b_])
                kb.op("dve", lambda e: e.tensor_tensor(out=xb[1][:, cs], in0=a_[:], in1=b_[:], op=ALU.add), reads=[a_, b_], writes=[xb[1]])
            for tb in range(4):
                cs = slice(tb * 512, (tb + 1) * 512)
                py = pYo[tb]
                first, last = (st % 4 == 0), (st % 4 == 3)
                kb._wait("pe", kb._deps([CP, xb[0], xb[1]], [py] if first else []))
                kb.nc.tensor.matmul(py[:], CP[:, st, 0, :], xb[0][:, cs], start=first, stop=False)
                ins = kb.nc.tensor.matmul(py[:], CP[:, st, 1, :], xb[1][:, cs], start=False, stop=last)
                sname = kb.sem["pe"]
                kb.semval[sname] += 1
                ins.then_inc(kb.semobj[sname], 1)
                tok = (sname, kb.semval[sname])
                xb[0].rd.append(tok)
                xb[1].rd.append(tok)
                if first:
                    py.rd = []
                py.lw = [tok]
            if st % 4 == 3:
                for tb in range(4):
                    cs = slice(tb * 512, (tb + 1) * 512)
                    y, q = yb[tb % 2], q1[tb % 2]
                    kb.op("dve", lambda e: e.scalar_tensor_tensor(out=y[:], in0=uT[:, cc, cs], scalar=dsk[:, cc:cc + 1], in1=pYo[tb][:], op0=ALU.mult, op1=ALU.add),
                          reads=[uT, dsk, pYo[tb]], writes=[y])
                    kb.op("dve", lambda e: e.tensor_tensor(out=q[:], in0=y[:], in1=y[:], op=ALU.mult), reads=[y], writes=[q])
                    kb.op("dve", lambda e: e.tensor_scalar(out=q[:], in0=q[:], scalar1=0.044715, scalar2=1.0, op0=ALU.mult, op1=ALU.add), reads=[q], writes=[q])
                    kb.op("dve", lambda e: e.tensor_tensor(out=q[:], in0=q[:], in1=y[:], op=ALU.mult), reads=[q, y], writes=[q])
                    kb.op("act", lambda e: e.activation(out=q[:], in_=q[:], func=AF.Sigmoid, scale=1.5957691216057308), reads=[q], writes=[q])
                    kb.op("dve", lambda e: e.tensor_tensor(out=gT[:, cc, cs], in0=q[:], in1=y[:], op=ALU.mult), reads=[q, y], writes=[gT])
        oTs = kb.sb([128, 4, S], BF16, "oTs")
        sgs = [kb.sb([128, 512], F32, "sgs") for _ in range(2)]
        sig = [kb.sb([128, 512], F32, "sig") for _ in range(2)]
        od = [kb.sb([128, 512], F32, "od") for _ in range(2)]
        om = [kb.sb([128, 512], BF16, "om") for _ in range(2)]
        for t in range(NT):
            ts_ = slice(t * 128, (t + 1) * 128)
            k = t % 2
            kb.mm(pB[0][:], [(gT[:, c, ts_], wg[:, c, 0:512]) for c in range(4)], reads=[gT, wg], writes=[pB[0]])
            kb.mm(pB[1][:], [(gT[:, c, ts_], wg[:, c, 512:1024]) for c in range(4)], reads=[gT, wg], writes=[pB[1]])
            kb.op("act", lambda e: e.activation(out=sig[k][:], in_=pB[1][:], func=AF.Sigmoid), reads=[pB[1]], writes=[sig[k]])
            kb.dma("sp", sgin[k][:], tm.t[ts_, C_SG:C_SG + 512], reads=[tm], writes=[sgin[k]], sembuf=sgin[k])
            kb.op("act", lambda e: e.activation(out=sgs[k][:], in_=sgin[k][:], func=AF.Silu), reads=[sgin[k]], writes=[sgs[k]])
            kb.op("dve", lambda e: e.tensor_tensor(out=od[k][:], in0=pB[0][:], in1=sig[k][:], op=ALU.mult), reads=[pB[0], sig[k]], writes=[od[k]])
            kb.op("dve", lambda e: e.tensor_tensor(out=om[k][:], in0=od[k][:], in1=sgs[k][:], op=ALU.mult), reads=[od[k], sgs[k]], writes=[om[k]])
            for c in range(4):
                kb.op("pe", lambda e: e.transpose(out=ptr[:, c * 128:(c + 1) * 128], in_=om[k][:, c * 128:(c + 1) * 128], identity=ident[:]),
                      reads=[om[k], ident], writes=[ptr])
            kb.op("act", lambda e: e.activation(out=oTs[:, :, ts_], in_=ptr[:].rearrange("p (a b) -> p a b", a=4), func=AF.Copy), reads=[ptr], writes=[oTs])
        oT = cx.oT[s]
        kb.dma("sp", oT.t[1536:2048, :].rearrange("(c p) t -> p c t", p=128), oTs[:], reads=[oTs], writes=[oT], sembuf=oTs)


def load_w_bf16(kb, dst, src_ap2d, srcbuf, stg):
    v = src_ap2d.rearrange("(c p) n -> p c n", p=128)
    for c in range(16):
        st = stg[c % 2]
        kb.dma("sp", st[:], v[:, c, :], reads=[srcbuf], writes=[st], sembuf=st)
        kb.op("pool", lambda e: e.tensor_copy(out=dst[:, c, :], in_=st[:]), reads=[st], writes=[dst])


def phase_out(kb, cx, l):
    mTs = cx.mTs
    with kb.phase():
        ident = load_const(kb, cx, "ident_bf", [128, 128], BF16)
        Wb = kb.sb([128, 16, D], BF16, "Wb")
        stg = [kb.sb([128, D], F32, "stg") for _ in range(2)]
        load_w_bf16(kb, Wb, cx.w_branch.t[l].rearrange("n k d -> (n k) d"), cx.w_branch, stg)
        oTt = [kb.sb([128, 16, 128], BF16, "oTt") for _ in range(2)]
        mgt = [kb.sb([128, 4 * D], BF16, "mgt") for _ in range(2)]
        sgm = [kb.sb([128, D], F32, "sgm") for _ in range(2)]
        mergedb = [[kb.sb([128, 512], F32, "merged") for _ in range(4)] for _ in range(2)]
        tmp = [kb.sb([128, 512], F32, "tmp") for _ in range(4)]
        mbf = kb.sb([128, D], BF16, "mbf")
        mst = [kb.sb([128, 16, 128], BF16, "mst") for _ in range(2)]
        pm = [kb.ps([128, 512], F32, "pm") for _ in range(4)]
        ptr = [kb.ps([128, 512], BF16, "ptr") for _ in range(2)]
        n_pm = 0
        n_sg = 0
        it = 0
        for s in range(NSEQ):
            oT, tm = cx.oT[s], cx.tm[s]
            for t in range(NT):
                k = it % 2
                it += 1
                ts_ = slice(t * 128, (t + 1) * 128)
                kb.dma("sp", oTt[k][:], oT.t[:, ts_].rearrange("(c p) t -> p c t", p=128), reads=[oT], writes=[oTt[k]], sembuf=oTt[k])
                kb.dma("sp", mgt[k][:], tm.t[ts_, C_MG:C_MG + 4 * D], reads=[tm], writes=[mgt[k]], sembuf=mgt[k])
                for n in range(4):
                    sg = sgm[n_sg % 2]
                    n_sg += 1
                    kb.op("act", lambda e: e.activation(out=sg[:], in_=mgt[k][:, n * D:(n + 1) * D], func=AF.Sigmoid), reads=[mgt[k]], writes=[sg])
                    for cb in range(4):
                        cs = slice(cb * 512, (cb + 1) * 512)
                        p = pm[n_pm % 4]
                        tp = tmp[n_pm % 4]
                        n_pm += 1
                        kb.mm(p[:], [(oTt[k][:, n * 4 + c, :], Wb[:, n * 4 + c, cs]) for c in range(4)], reads=[oTt[k], Wb], writes=[p])
                        mg_ = mergedb[k][cb]
                        if n == 0:
                            kb.op("dve", lambda e: e.tensor_tensor(out=mg_[:], in0=p[:], in1=sg[:, cs], op=ALU.mult), reads=[p, sg], writes=[mg_])
                        else:
                            kb.op("dve", lambda e: e.tensor_tensor(out=tp[:], in0=p[:], in1=sg[:, cs], op=ALU.mult), reads=[p, sg], writes=[tp])
                            kb.op("pool", lambda e: e.tensor_tensor(out=mg_[:], in0=mg_[:], in1=tp[:], op=ALU.add), reads=[mg_, tp], writes=[mg_])
                for cb in range(4):
                    kb.op("act", lambda e: e.activation(out=mbf[:, cb * 512:(cb + 1) * 512], in_=mergedb[k][cb][:], func=AF.Copy),
                          reads=[mergedb[k][cb]], writes=[mbf])
                ms = mst[k]
                for g4 in range(4):
                    pt = ptr[g4 % 2]
                    for j in range(4):
                        c = g4 * 4 + j
                        kb.op("pe", lambda e: e.transpose(out=pt[:, j * 128:(j + 1) * 128], in_=mbf[:, c * 128:(c + 1) * 128], identity=ident[:]),
                              reads=[mbf, ident], writes=[pt])
                    kb.op("act", lambda e: e.activation(out=ms[:, g4 * 4:(g4 + 1) * 4, :], in_=pt[:].rearrange("p (a b) -> p a b", a=4), func=AF.Copy),
                          reads=[pt], writes=[ms])
                c0 = s * S + t * 128
                kb.dma("act", mTs.t[:, c0:c0 + 128].rearrange("(c p) t -> p c t", p=128), ms[:], reads=[ms], writes=[mTs], sembuf=ms)
    with kb.phase():
        Wo = kb.sb([128, 16, D], BF16, "Wo")
        stg = [kb.sb([128, D], F32, "stg") for _ in range(2)]
        load_w_bf16(kb, Wo, cx.w_out.t[l], cx.w_out, stg)
        gp = kb.sb([128, D], F32, "gp")
        kb.dma("sp", gp[:], cx.post_norm_g.t[l:l + 1, :].broadcast_to([128, D]), reads=[cx.post_norm_g], writes=[gp], sembuf=gp)
        mT = [kb.sb([128, 16, 128], BF16, "mT") for _ in range(2)]
        ht = [kb.sb([128, D], F32, "ht") for _ in range(2)]
        hn = [kb.sb([128, D], F32, "hn") for _ in range(2)]
        t1 = [kb.sb([128, 512], F32, "t1") for _ in range(2)]
        jk = kb.sb([128, 512], BF16, "jk")
        sq = kb.sb([128, 2 * NT * NSEQ, 8], F32, "sq")
        py = [kb.ps([128, 512], F32, "py") for _ in range(8)]
        hin, hout = cx.h_in, cx.h_out
        it = 0
        for s in range(NSEQ):
            for t in range(NT):
                k = it % 2
                c0 = s * S + t * 128
                kb.dma("sp", mT[k][:], mTs.t[:, c0:c0 + 128].rearrange("(c p) t -> p c t", p=128), reads=[mTs], writes=[mT[k]], sembuf=mT[k])
                kb.dma("sp", ht[k][:], hin.t[c0:c0 + 128, :], reads=[hin], writes=[ht[k]], sembuf=ht[k])
                pys = py[k * 4:(k + 1) * 4]
                for cb in range(4):
                    cs = slice(cb * 512, (cb + 1) * 512)
                    kb.mm(pys[cb][:], [(mT[k][:, c, :], Wo[:, c, cs]) for c in range(16)], reads=[mT[k], Wo], writes=[pys[cb]])
                    kb.op("act", lambda e: e.activation(out=jk[:], in_=pys[cb][:], func=AF.Square, accum_out=sq[:, it, cb:cb + 1]),
                          reads=[pys[cb]], writes=[jk, sq])
                kb.op("dve", lambda e: e.tensor_reduce(out=sq[:, it, 4:5], in_=sq[:, it, 0:4], axis=AX.X, op=ALU.add), reads=[sq], writes=[sq])
                rstd_from_ss(kb, sq[:, it, 4:5], sq[:, it, 5:6], D, sq, sq)
                for cb in range(4):
                    cs = slice(cb * 512, (cb + 1) * 512)
                    tt = t1[cb % 2]
                    kb.op("dve", lambda e: e.scalar_tensor_tensor(out=tt[:], in0=pys[cb][:], scalar=sq[:, it, 5:6], in1=gp[:, cs], op0=ALU.mult, op1=ALU.mult),
                          reads=[pys[cb], sq, gp], writes=[tt])
                    kb.op("pool", lambda e: e.tensor_tensor(out=hn[k][:, cs], in0=tt[:], in1=ht[k][:, cs], op=ALU.add), reads=[tt, ht[k]], writes=[hn[k]])
                kb.dma("act", hout.t[c0:c0 + 128, :], hn[k][:], reads=[hn[k]], writes=[hout], sembuf=hn[k])
                it += 1
```

```python
import contextlib
import math
import numpy as np
import ml_dtypes
import concourse.bass as bass
import concourse.mybir as mybir
from concourse.bass_utils import run_bass_kernel_spmd

F32 = mybir.dt.float32
BF16 = mybir.dt.bfloat16
AF = mybir.ActivationFunctionType
ALU = mybir.AluOpType
AX = mybir.AxisListType

D = 2048
S = 2048
NT = S // 128
DEPTH = 4
NSEQ = 2
INC = 14860
EPS = 1e-6
O_FQ, O_FK, O_FV, O_FF, O_FG = 0, 512, 1024, 1536, 1540
O_DQ, O_DK, O_DV, O_DG = 2052, 2564, 3076, 3588
O_SZ, O_SX, O_SDT, O_SU, O_SG, O_MG = 4100, 4612, 5636, 5644, 6156, 6668
FM_SEGS = [(O_FQ, 512), (O_FK, 512), (O_DQ, 512), (O_DK, 512), (O_SX, 1024), (O_SU, 512)]
R_FQ, R_FK, R_DQ, R_DK, R_SX, R_SU = 0, 512, 1024, 1536, 2048, 3072
FM_ROWS = 3584
TM_SEGS = [(O_FV, 512), (O_FG, 512), (O_DV, 512), (O_DG, 512), (O_SZ, 512), (O_SG, 512), (O_MG, 8192)]
C_FV, C_FG, C_DV, C_DG, C_SZ, C_SG, C_MG = 0, 512, 1024, 1536, 2048, 2560, 3072
TM_COLS = 11264


class Buf:
    def __init__(self, kb, name, t):
        self.kb = kb
        self.name = name
        self.t = t
        self.lw = []
        self.rd = []
        self.dsem = None

    def __getitem__(self, k):
        return self.t[k]


class KB:
    ENG = ("pe", "act", "dve", "pool", "sp")

    def __init__(self, nc, es):
        self.nc = nc
        self.es = es
        self.e = {"pe": nc.tensor, "act": nc.scalar, "dve": nc.vector, "pool": nc.gpsimd, "sp": nc.sync}
        self.sem = {}
        self.cnt = {}
        self.seen = {k: {} for k in self.ENG}
        self.semval = {}
        self.semobj = {}
        self.epoch = 0
        self.new_epoch()
        self.dma_free = []
        self.dma_all = []
        for i in range(64):
            s = es.enter_context(nc.semaphore(f"dq{i}"))
            nm = f"dq{i}"
            self.semobj[nm] = s
            self.semval[nm] = 0
            self.dma_free.append(nm)
            self.dma_all.append(nm)
        self.bar = es.enter_context(nc.semaphore("bar"))
        self.barv = 0
        self.phase_bufs = []
        self.uid = 0

    def new_epoch(self):
        self.epoch += 1
        for k in self.ENG:
            nm = f"e{self.epoch}_{k}"
            s = self.es.enter_context(self.nc.semaphore(nm))
            self.semobj[nm] = s
            self.semval[nm] = 0
            self.sem[k] = nm

    def sb(self, shape, dt, name=None):
        self.uid += 1
        name = f"{name or 'sb'}_{self.uid}"
        t = self.pes.enter_context(self.nc.sbuf_tensor(name, list(shape), dt))
        b = Buf(self, name, t)
        self.phase_bufs.append(b)
        return b

    def ps(self, shape, dt=F32, name=None):
        self.uid += 1
        name = f"{name or 'ps'}_{self.uid}"
        t = self.pes.enter_context(self.nc.psum_tensor(name, list(shape), dt))
        b = Buf(self, name, t)
        self.phase_bufs.append(b)
        return b

    def dram(self, name, t):
        return Buf(self, name, t)

    @contextlib.contextmanager
    def phase(self):
        self.pes = contextlib.ExitStack()
        self.phase_bufs = []
        try:
            yield
            self.barrier()
        finally:
            for b in self.phase_bufs:
                if b.dsem is not None:
                    self.dma_free.append(b.dsem)
                    b.dsem = None
            self.pes.close()
            self.pes = None

    def _wait(self, eng, toks):
        need = {}
        for (s, v) in toks:
            if need.get(s, 0) < v:
                need[s] = v
        for s, v in need.items():
            if self.seen[eng].get(s, 0) >= v:
                continue
            self.e[eng].wait_ge(self.semobj[s], v)
            self.seen[eng][s] = v

    def _deps(self, reads, writes):
        toks = []
        for b in reads:
            toks += b.lw
        for b in writes:
            toks += b.lw
            toks += b.rd
        return toks

    def op(self, eng, fn, reads=(), writes=()):
        self._wait(eng, self._deps(reads, writes))
        ins = fn(self.e[eng])
        s = self.sem[eng]
        self.semval[s] += 1
        ins.then_inc(self.semobj[s], 1)
        tok = (s, self.semval[s])
        self._record(tok, reads, writes)
        return tok

    def _record(self, tok, reads, writes):
        for b in reads:
            b.rd.append(tok)
            if len(b.rd) > 64:
                b.rd = self._compact(b.rd)
        for b in writes:
            b.lw = [tok]
            b.rd = []

    @staticmethod
    def _compact(toks):
        need = {}
        for (s, v) in toks:
            if need.get(s, 0) < v:
                need[s] = v
        return list(need.items())

    def mm(self, out_ap, pairs, reads, writes, start=True, stop=True):
        eng = "pe"
        self._wait(eng, self._deps(reads, writes))
        n = len(pairs)
        ins = None
        for i, (l, r) in enumerate(pairs):
            ins = self.nc.tensor.matmul(out_ap, l, r, start=(start and i == 0), stop=(stop and i == n - 1))
        s = self.sem[eng]
        self.semval[s] += 1
        ins.then_inc(self.semobj[s], 1)
        tok = (s, self.semval[s])
        self._record(tok, reads, writes)
        return tok

    def dma(self, q, out_ap, in_ap, reads=(), writes=(), sembuf=None, **kw):
        b = sembuf
        if b.dsem is None:
            b.dsem = self.dma_free.pop(0)
        self._wait(q, self._deps(reads, writes))
        ins = self.e[q].dma_start(out=out_ap, in_=in_ap, **kw)
        s = b.dsem
        self.semval[s] += 16
        ins.then_inc(self.semobj[s], 16)
        tok = (s, self.semval[s])
        for r in reads:
            r.rd.append(tok)
        for w in writes:
            if w.lw and all(t[0] == s for t in w.lw):
                w.lw = [tok]
            else:
                w.lw = [tok]
            w.rd = []
        return tok

    def barrier(self):
        toks = [(s, v) for s, v in self.semval.items() if v > 0 and not s.startswith("bar")]
        cur = set(self.sem.values()) | set(self.dma_all)
        toks = [(s, v) for (s, v) in toks if s in cur]
        self._wait("sp", toks)
        self.barv += 1
        self.nc.sync.sem_inc(self.bar, 1)
        for k in self.ENG:
            if k == "sp":
                continue
            self.e[k].wait_ge(self.bar, self.barv)
            for (s, v) in toks:
                self.seen[k][s] = max(self.seen[k].get(s, 0), v)


def bcast_ap(t_ap, shape):
    return t_ap.broadcast_to(list(shape))


class Ctx:
    pass


def load_const(kb, cx, name, shape, dt):
    b = kb.sb(shape, dt, name=name)
    kb.dma("sp", b[:], cx.cst[name].t[:], reads=[cx.cst[name]], writes=[b], sembuf=b)
    return b


def phase_A(kb, cx, l, s):
    nc = kb.nc
    with kb.phase():
        ident = load_const(kb, cx, "ident_bf", [128, 128], BF16)
        xnT = kb.sb([128, 16, S], BF16, "xnT")
        gb = kb.sb([128, D], F32, "gb")
        kb.dma("sp", gb[:], cx.pre_g.t[l:l + 1, :].broadcast_to([128, D]), reads=[cx.pre_g], writes=[gb], sembuf=gb)
        hts = [kb.sb([128, D], F32, "ht") for _ in range(2)]
        junk = kb.sb([128, D], BF16, "junk")
        xss = [kb.sb([128, D], BF16, "xs") for _ in range(2)]
        ss = kb.sb([128, NT], F32, "ss")
        rs = kb.sb([128, NT], F32, "rs")
        ptr = [kb.ps([128, 512], BF16, "ptr") for _ in range(2)]
        pmm = [kb.ps([128, 512], F32, "pmm") for _ in range(4)]
        hin = cx.h_in
        for t in range(NT if cx.stop >= 2 else 0):
            ht = hts[t % 2]
            xs = xss[t % 2]
            r0 = s * S + t * 128
            kb.dma("sp", ht[:], hin.t[r0:r0 + 128, :], reads=[hin], writes=[ht], sembuf=ht)
            kb.op("act", lambda e: e.activation(out=junk[:], in_=ht[:], func=AF.Square, accum_out=ss[:, t:t + 1]),
                  reads=[ht], writes=[junk, ss])
            kb.op("dve", lambda e: e.tensor_scalar(out=rs[:, t:t + 1], in0=ss[:, t:t + 1], scalar1=1.0 / D, scalar2=EPS,
                                                   op0=ALU.mult, op1=ALU.add), reads=[ss], writes=[rs])
            kb.op("act", lambda e: e.activation(out=rs[:, t:t + 1], in_=rs[:, t:t + 1], func=AF.Sqrt), reads=[rs], writes=[rs])
            kb.op("dve", lambda e: e.reciprocal(out=rs[:, t:t + 1], in_=rs[:, t:t + 1]), reads=[rs], writes=[rs])
            kb.op("dve", lambda e: e.scalar_tensor_tensor(out=xs[:], in0=ht[:], scalar=rs[:, t:t + 1], in1=gb[:],
                                                          op0=ALU.mult, op1=ALU.mult), reads=[ht, rs, gb], writes=[xs])
            for g4 in range(4):
                pt = ptr[g4 % 2]
                for j in range(4):
                    c = g4 * 4 + j
                    kb.op("pe", lambda e: e.transpose(out=pt[:, j * 128:(j + 1) * 128], in_=xs[:, c * 128:(c + 1) * 128],
                                                      identity=ident[:]), reads=[xs, ident], writes=[pt])
                dst = xnT[:, g4 * 4:(g4 + 1) * 4, t * 128:(t + 1) * 128]
                src = pt[:].rearrange("p (a b) -> p a b", a=4)
                if g4 % 2 == 0:
                    kb.op("dve", lambda e: e.tensor_copy(out=dst, in_=src), reads=[pt], writes=[xnT])
                else:
                    kb.op("act", lambda e: e.activation(out=dst, in_=src, func=AF.Copy), reads=[pt], writes=[xnT])

        W = cx.w_in
        wst = [kb.sb([128, 4, 512], F32, "wst") for _ in range(2)]
        wbfs = [kb.sb([128, 16, 512], BF16, "wbf") for _ in range(2)]
        stfm = [kb.sb([128, S], BF16, "stfm") for _ in range(2)]
        sttm = [kb.sb([128, 512], BF16, "sttm") for _ in range(3)]
        blocks = []
        for (wc, n), r in zip(FM_SEGS, [R_FQ, R_FK, R_DQ, R_DK, R_SX, R_SU]):
            for k in range(n // 512):
                blocks.append(("fm", wc + k * 512, r + k * 512))
        col = 0
        for (wc, n) in TM_SEGS:
            for k in range(n // 512):
                blocks.append(("tm", wc + k * 512, col))
                col += 512
        if cx.dbg_nblocks is not None:
            blocks = blocks[:cx.dbg_nblocks[0]] + [b for b in blocks if b[0] == "tm"][:cx.dbg_nblocks[1]]
        fmT, tm, sm = cx.fmT[s], cx.tm[s], cx.sm[s]
        state = {"k": 0, "pm": 0, "sf": 0, "st": 0}

        def load_block(i):
            kind, wc, dst = blocks[i]
            wbf = wbfs[i % 2]
            for piece in range(4):
                st = wst[state["k"] % 2]
                state["k"] += 1
                src = W.t[l, piece * 512:(piece + 1) * 512, wc:wc + 512].rearrange("(c p) n -> p c n", p=128)
                kb.dma("sp", st[:], src, reads=[W], writes=[st], sembuf=st)
                kb.op("pool", lambda e: e.tensor_copy(out=wbf[:, piece * 4:(piece + 1) * 4, :], in_=st[:]),
                      reads=[st], writes=[wbf])

        if cx.stop < 3:
            return
        wsm_f = kb.sb([128, 16, 12], F32, "wsmf")
        wsm = kb.sb([128, 16, 12], BF16, "wsm")
        smst = kb.sb([128, NT, 12], F32, "smst")
        for (wc, n, o) in ((O_FF, 4, 0), (O_SDT, 8, 4)):
            src = W.t[l, :, wc:wc + n].rearrange("(c p) n -> p c n", p=128)
            kb.dma("sp", wsm_f[:, :, o:o + n], src, reads=[W], writes=[wsm_f], sembuf=wsm_f, allow_slow_non_contiguous=True)
        kb.op("pool", lambda e: e.tensor_copy(out=wsm[:], in_=wsm_f[:]), reads=[wsm_f], writes=[wsm])
        for t in range(NT):
            pm = pmm[state["pm"] % 4]
            state["pm"] += 1
            kb.mm(pm[:, 0:12], [(xnT[:, c, t * 128:(t + 1) * 128], wsm[:, c, :]) for c in range(16)],
                  reads=[xnT, wsm], writes=[pm])
            kb.op("act", lambda e: e.activation(out=smst[:, t, :], in_=pm[:, 0:12], func=AF.Copy), reads=[pm], writes=[smst])
        kb.dma("act", sm.t[:, 0:12].rearrange("(t p) n -> p t n", p=128), smst[:], reads=[smst], writes=[sm], sembuf=smst,
               allow_slow_non_contiguous=True)

        if cx.stop < 4:
            return
        load_block(0)
        for i, (kind, wc, dst) in enumerate(blocks):
            if i + 1 < len(blocks):
                load_block(i + 1)
            wbf = wbfs[i % 2]
            if kind == "fm":
                for j in range(4):
                    sf = stfm[state["sf"] % 2]
                    state["sf"] += 1
                    for tb in range(4):
                        pm = pmm[state["pm"] % 4]
                        state["pm"] += 1
                        kb.mm(pm[:], [(wbf[:, c, j * 128:(j + 1) * 128], xnT[:, c, tb * 512:(tb + 1) * 512]) for c in range(16)],
                              reads=[xnT, wbf], writes=[pm])
                        kb.op("act", lambda e: e.activation(out=sf[:, tb * 512:(tb + 1) * 512], in_=pm[:], func=AF.Copy),
                              reads=[pm], writes=[sf])
                    kb.dma("act", fmT.t[dst + j * 128:dst + (j + 1) * 128, :], sf[:], reads=[sf], writes=[fmT], sembuf=sf)
            else:
                for t in range(NT):
                    pm = pmm[state["pm"] % 4]
                    state["pm"] += 1
                    stt = sttm[state["st"] % 3]
                    state["st"] += 1
                    kb.mm(pm[:], [(xnT[:, c, t * 128:(t + 1) * 128], wbf[:, c, :]) for c in range(16)],
                          reads=[xnT, wbf], writes=[pm])
                    if t % 2 == 0:
                        kb.op("act", lambda e: e.activation(out=stt[:], in_=pm[:], func=AF.Copy), reads=[pm], writes=[stt])
                    else:
                        kb.op("dve", lambda e: e.tensor_copy(out=stt[:], in_=pm[:]), reads=[pm], writes=[stt])
                    kb.dma("act", tm.t[t * 128:(t + 1) * 128, dst:dst + 512], stt[:], reads=[stt], writes=[tm], sembuf=stt)


BF = ml_dtypes.bfloat16


def make_consts():
    c = {}
    c["ident_bf"] = np.eye(128, dtype=np.float32).astype(BF)
    c["ident_f"] = np.eye(128, dtype=np.float32)
    k = np.arange(128)[:, None]
    q = np.arange(128)[None, :]
    c["mask_le_bf"] = (k <= q).astype(np.float32).astype(BF)
    c["tri_le_f"] = (k <= q).astype(np.float32)
    c["gt_f"] = (k > q).astype(np.float32)
    c["negmask_f"] = np.where(q < k, -30000.0, 0.0).astype(np.float32)
    c["ones_f"] = np.ones((128, 128), np.float32)
    d = np.arange(128) % 32
    inv = (10000.0 ** (-d.astype(np.float64) / 32.0))
    ang = inv[:, None] * np.arange(S, dtype=np.float64)[None, :]
    c["rope_cos"] = np.cos(ang).astype(np.float32)
    c["rope_sin"] = np.sin(ang).astype(np.float32)
    RT = np.zeros((128, 128), np.float32)
    for p in range(128):
        dd = p % 64
        if dd < 32:
            RT[p + 32, p] = -1.0
        else:
            RT[p - 32, p] = 1.0
    c["rope_rt"] = RT.astype(BF)
    gm = np.zeros((128, 8), np.float32)
    for r in range(128):
        gm[r, r // 16] = 1.0
    c["gmask"] = gm
    return c


CONST_DT = {"ident_bf": BF16, "mask_le_bf": BF16, "rope_rt": BF16}

PARAMS = ["w_in", "b_fox_f", "pre_norm_g", "post_norm_g", "diff_lambda", "diff_subln_g", "ssd_conv_w", "ssd_conv_b",
          "ssd_dt_bias", "ssd_a_log", "ssd_d", "ssd_norm_g", "s5_a_re", "s5_a_im", "s5_b_re", "s5_b_im", "s5_c_re",
          "s5_c_im", "s5_d", "s5_log_dt", "s5_w_glu", "w_branch", "w_out"]


def build(shapes, layers=range(DEPTH), phases="ABCDEF", dbg=None, dbg_nblocks=None, stop=99, stop2=0):
    dbg = dbg or {}
    nc = bass.Bass("TRN2", target_bir_lowering=False)
    es = contextlib.ExitStack()
    kb = KB(nc, es)
    cx = Ctx()
    cx.dbg_nblocks = dbg_nblocks
    cx.stop = stop
    cx.stop2 = stop2
    x = nc.dram_tensor("x", [NSEQ * S, D], F32, kind="ExternalInput")
    out = nc.dram_tensor("out", [NSEQ * S, D], F32, kind="ExternalOutput")
    cx.cst = {}
    for k, v in make_consts().items():
        cx.cst[k] = kb.dram(k, nc.dram_tensor("c_" + k, list(v.shape), CONST_DT.get(k, F32), kind="ExternalInput"))
    for p in PARAMS:
        setattr(cx, p, kb.dram(p, nc.dram_tensor(p, list(shapes[p]), F32, kind="ExternalInput")))
    cx.pre_g = cx.pre_norm_g

    def scratch(name, shape, dt):
        kind = {"in": "ExternalInput", "out": "ExternalOutput"}.get(dbg.get(name), "Internal")
        return kb.dram(name, nc.dram_tensor(name, shape, dt, kind=kind))

    cx.fmT = [scratch(f"fmT{s}", [FM_ROWS, S], BF16) for s in range(NSEQ)]
    cx.tm = [scratch(f"tm{s}", [S, TM_COLS], BF16) for s in range(NSEQ)]
    cx.sm = [scratch(f"sm{s}", [S, 16], F32) for s in range(NSEQ)]
    cx.oT = [scratch(f"oT{s}", [4 * 512, S], BF16) for s in range(NSEQ)]
    cx.mTs = scratch("mTs", [D, NSEQ * S], BF16)
    cx.Etab = scratch("Etab", [16, 2, 128, S], F32)
    hA = scratch("hA", [NSEQ * S, D], F32)
    hB = scratch("hB", [NSEQ * S, D], F32)
    xb = kb.dram("x", x)
    ob = kb.dram("out", out)
    layers = list(layers)
    if "E" in phases:
        s5_alloc(kb, cx)
    for li, l in enumerate(layers):
        cx.h_in = xb if li == 0 else (hA if li % 2 == 1 else hB)
        cx.h_out = ob if li == len(layers) - 1 else (hA if li % 2 == 0 else hB)
        cx.l = l
        if "E" in phases:
            phase_s5_prep(kb, cx, l)
        for s in range(NSEQ):
            if "A" in phases:
                phase_A(kb, cx, l, s)
            if "B" in phases:
                phase_fox(kb, cx, l, s)
            if "C" in phases:
                phase_diff(kb, cx, l, s)
            if "D" in phases:
                phase_ssd(kb, cx, l, s)
            if "E" in phases:
                phase_s5(kb, cx, l, s)
        if "F" in phases:
            phase_out(kb, cx, l)
        if li + 1 < len(layers):
            kb.new_epoch()
    es.close()
    return nc


_CACHE = {}


def kernel(**inputs):
    x = np.ascontiguousarray(inputs["x"], dtype=np.float32)
    shapes = {p: inputs[p].shape for p in PARAMS}
    key = "full"
    if key not in _CACHE:
        _CACHE[key] = build(shapes)
    nc = _CACHE[key]
    consts = make_consts()
    base = {p: np.ascontiguousarray(inputs[p], dtype=np.float32) for p in PARAMS}
    for k, v in consts.items():
        base["c_" + k] = v
    in_maps = []
    for c in range(8):
        m = dict(base)
        m["x"] = x[c * NSEQ:(c + 1) * NSEQ].reshape(NSEQ * S, D)
        in_maps.append(m)
    res = run_bass_kernel_spmd(nc, in_maps, core_ids=list(range(8)))
    outs = [np.asarray(r["out"]).reshape(NSEQ, S, D) for r in res.results]
    return np.concatenate(outs, axis=0).astype(np.float32)


def attn_core(kb, cx, maps, nheads, vaug, finalize, scale, pools):
    sc_ps, pt_sb, po_ps, maskT = pools
    st = {"sc": 0, "po": 0}
    for h in range(nheads):
        for i in range(NT):
            pos = []
            for mi, mp in enumerate(maps):
                po = po_ps[st["po"] % len(po_ps)]
                st["po"] += 1
                pos.append(po)
                groups = [list(range(j0, min(j0 + 4, i + 1))) for j0 in range(0, i + 1, 4)]
                pend = None
                for gi, js in enumerate(groups):
                    sc = sc_ps[st["sc"] % len(sc_ps)]
                    pt = pt_sb[st["sc"] % len(pt_sb)]
                    st["sc"] += 1
                    for jj, j in enumerate(js):
                        kb.mm(sc[:, jj * 128:(jj + 1) * 128], [(mp["kT"](h, j), mp["qT"](h, i))], reads=mp["reads"], writes=[sc])
                    if mp["bias"] is None:
                        n = len(js) * 128
                        kb.op("act", lambda e: e.activation(out=pt[:, 0:n], in_=sc[:, 0:n], func=AF.Exp, scale=scale),
                              reads=[sc], writes=[pt])
                    else:
                        for jj, j in enumerate(js):
                            kb.op("act", lambda e: e.activation(out=pt[:, jj * 128:(jj + 1) * 128], in_=sc[:, jj * 128:(jj + 1) * 128],
                                                                func=AF.Exp, scale=scale, bias=mp["bias"](h, i, j)),
                                  reads=[sc] + mp["bias_reads"], writes=[pt])
                    if js[-1] == i:
                        jj = len(js) - 1
                        kb.op("dve", lambda e: e.tensor_tensor(out=pt[:, jj * 128:(jj + 1) * 128], in0=pt[:, jj * 128:(jj + 1) * 128],
                                                               in1=maskT[:], op=ALU.mult), reads=[pt, maskT], writes=[pt])
                    if pend is not None:
                        _pv(kb, pend, po, vaug, h, i)
                    pend = (js, pt)
                _pv(kb, pend, po, vaug, h, i)
            finalize(h, i, pos)


def _pv(kb, pend, po, vaug, h, i):
    js, pt = pend
    for jj, j in enumerate(js):
        kb.nc.tensor
        kb._wait("pe", kb._deps([pt, vaug], [po] if j == 0 else []))
        ins = kb.nc.tensor.matmul(po[:, 0:129], pt[:, jj * 128:(jj + 1) * 128], vaug[:, j, h, :], start=(j == 0), stop=(j == i))
        s = kb.sem["pe"]
        kb.semval[s] += 1
        ins.then_inc(kb.semobj[s], 1)
        tok = (s, kb.semval[s])
        pt.rd.append(tok)
        vaug.rd.append(tok)
        if j == 0:
            po.rd = []
        po.lw = [tok]


def load_vaug(kb, cx, s, col0, name):
    vaug = kb.sb([128, NT, 4, 129], BF16, name)
    kb.op("pool", lambda e: e.memset(vaug[:, :, :, 128:129], 1.0), reads=[], writes=[vaug])
    tm = cx.tm[s]
    for t in range(NT):
        src = tm.t[t * 128:(t + 1) * 128, col0:col0 + 512].rearrange("p (h d) -> p h d", h=4)
        kb.dma("sp", vaug[:, t, :, 0:128], src, reads=[tm], writes=[vaug], sembuf=vaug)
    return vaug


def transpose_out(kb, cx, o_tm, ident, ptr, oTs, i):
    for c in range(4):
        kb.op("pe", lambda e: e.transpose(out=ptr[:, c * 128:(c + 1) * 128], in_=o_tm[:, c * 128:(c + 1) * 128], identity=ident[:]),
              reads=[o_tm, ident], writes=[ptr])
    kb.op("act", lambda e: e.activation(out=oTs[:, :, i * 128:(i + 1) * 128], in_=ptr[:].rearrange("p (a b) -> p a b", a=4), func=AF.Copy),
          reads=[ptr], writes=[oTs])


def phase_fox(kb, cx, l, s):
    nc = kb.nc
    with kb.phase():
        ident = load_const(kb, cx, "ident_bf", [128, 128], BF16)
        maskT = load_const(kb, cx, "mask_le_bf", [128, 128], BF16)
        tri = load_const(kb, cx, "tri_le_f", [128, 128], F32)
        ones = load_const(kb, cx, "ones_f", [128, 128], F32)
        fmT, tm, sm = cx.fmT[s], cx.tm[s], cx.sm[s]
        qT = kb.sb([128, 4, S], BF16, "qT")
        kT = kb.sb([128, 4, S], BF16, "kT")
        kb.dma("sp", qT[:], fmT.t[R_FQ:R_FQ + 512, :].rearrange("(h p) t -> p h t", p=128), reads=[fmT], writes=[qT], sembuf=qT)
        kb.dma("sp", kT[:], fmT.t[R_FK:R_FK + 512, :].rearrange("(h p) t -> p h t", p=128), reads=[fmT], writes=[kT], sembuf=kT)
        vaug = load_vaug(kb, cx, s, C_FV, "vaugf")
        ff = kb.sb([128, NT, 4], F32, "ff")
        kb.dma("sp", ff[:], sm.t[:, 0:4].rearrange("(t p) n -> p t n", p=128), reads=[sm], writes=[ff], sembuf=ff,
               allow_slow_non_contiguous=True)
        bb = kb.sb([128, 4], F32, "bb")
        kb.dma("sp", bb[:], cx.b_fox_f.t[l:l + 1, :].broadcast_to([128, 4]), reads=[cx.b_fox_f], writes=[bb], sembuf=bb)
        lf = kb.sb([128, NT, 4], F32, "lf")
        kb.op("dve", lambda e: e.tensor_tensor(out=lf[:], in0=ff[:], in1=bb[:].unsqueeze(1).broadcast_to([128, NT, 4]), op=ALU.add),
              reads=[ff, bb], writes=[lf])
        kb.op("act", lambda e: e.activation(out=lf[:], in_=lf[:], func=AF.Exp, scale=-1.0), reads=[lf], writes=[lf])
        kb.op("act", lambda e: e.activation(out=lf[:], in_=lf[:], func=AF.Ln, bias=1.0), reads=[lf], writes=[lf])
        sc_ps = [kb.ps([128, 512], F32, "sc") for _ in range(3)]
        pF, pT = sc_ps[0], sc_ps[1]
        lf2 = lf[:].rearrange("p t h -> p (t h)")
        kb.mm(pF[:, 0:64], [(tri[:], lf2)], reads=[tri, lf], writes=[pF])
        kb.mm(pT[:, 0:64], [(ones[:], lf2)], reads=[ones, lf], writes=[pT])
        Lloc = kb.sb([128, NT, 4], F32, "Lloc")
        tot = kb.sb([128, NT, 4], F32, "tot")
        Lend = kb.sb([128, NT, 4], F32, "Lend")
        Lc = kb.sb([128, NT, 4], F32, "Lc")
        kb.op("dve", lambda e: e.tensor_copy(out=Lloc[:].rearrange("p t h -> p (t h)"), in_=pF[:, 0:64]), reads=[pF], writes=[Lloc])
        kb.op("dve", lambda e: e.tensor_copy(out=tot[:].rearrange("p t h -> p (t h)"), in_=pT[:, 0:64]), reads=[pT], writes=[tot])
        kb.op("dve", lambda e: e.tensor_copy(out=Lend[:, 0, :], in_=tot[:, 0, :]), reads=[tot], writes=[Lend])
        for t in range(1, NT):
            kb.op("dve", lambda e: e.tensor_tensor(out=Lend[:, t, :], in0=Lend[:, t - 1, :], in1=tot[:, t, :], op=ALU.add),
                  reads=[Lend, tot], writes=[Lend])
        kb.op("dve", lambda e: e.tensor_tensor(out=Lc[:], in0=Lend[:], in1=tot[:], op=ALU.subtract), reads=[Lend, tot], writes=[Lc])
        kb.op("dve", lambda e: e.tensor_tensor(out=Lc[:], in0=Lc[:], in1=Lloc[:], op=ALU.add), reads=[Lc, Lloc], writes=[Lc])
        bm = kb.sb([128, 4, NT, NT], F32, "bm")
        for h in range(4):
            for i in range(NT):
                kb.op("dve", lambda e: e.tensor_scalar(out=bm[:, h, i, 0:i + 1], in0=Lc[:, 0:i + 1, h], scalar1=Lend[:, i, h:h + 1],
                                                       scalar2=None, op0=ALU.subtract), reads=[Lc, Lend], writes=[bm])
        pt_sb = [kb.sb([128, 512], BF16, "pt") for _ in range(3)]
        po_ps = [kb.ps([128, 512], F32, "po") for _ in range(4)]
        ptr = kb.ps([128, 512], BF16, "ptr")
        oTs = kb.sb([128, 4, S], BF16, "oTs")
        otm = kb.sb([128, NT, 512], BF16, "otm")
        sg = kb.sb([128, NT, 512], F32, "sg")
        gin = kb.sb([128, NT, 512], BF16, "gin")
        kb.dma("sp", gin[:], tm.t[:, C_FG:C_FG + 512].rearrange("(t p) n -> p t n", p=128), reads=[tm], writes=[gin], sembuf=gin)
        kb.op("act", lambda e: e.activation(out=sg[:], in_=gin[:], func=AF.Silu), reads=[gin], writes=[sg])
        rec = kb.sb([128, 4 * NT], F32, "rec")

        def fin(h, i, pos):
            po = pos[0]
            r = rec[:, h * NT + i:h * NT + i + 1]
            kb.op("dve", lambda e: e.reciprocal(out=r, in_=po[:, 128:129]), reads=[po], writes=[rec])
            kb.op("dve", lambda e: e.scalar_tensor_tensor(out=otm[:, i, h * 128:(h + 1) * 128], in0=po[:, 0:128], scalar=r,
                                                          in1=sg[:, i, h * 128:(h + 1) * 128], op0=ALU.mult, op1=ALU.mult),
                  reads=[po, rec, sg], writes=[otm])

        maps = [dict(kT=lambda h, j: kT[:, h, j * 128:(j + 1) * 128], qT=lambda h, i: qT[:, h, i * 128:(i + 1) * 128],
                     bias=lambda h, i, j: bm[:, h, i, j:j + 1], bias_reads=[bm], reads=[kT, qT])]
        attn_core(kb, cx, maps, 4, vaug, fin, 128 ** -0.5, (sc_ps, pt_sb, po_ps, maskT))
        for i in range(NT):
            transpose_out(kb, cx, otm[:, i, :], ident, ptr, oTs, i) if False else None
        for i in range(NT):
            for c in range(4):
                kb.op("pe", lambda e: e.transpose(out=ptr[:, c * 128:(c + 1) * 128], in_=otm[:, i, c * 128:(c + 1) * 128], identity=ident[:]),
                      reads=[otm, ident], writes=[ptr])
            kb.op("act", lambda e: e.activation(out=oTs[:, :, i * 128:(i + 1) * 128], in_=ptr[:].rearrange("p (a b) -> p a b", a=4),
                                                func=AF.Copy), reads=[ptr], writes=[oTs])
        oT = cx.oT[s]
        kb.dma("sp", oT.t[0:512, :].rearrange("(c p) t -> p c t", p=128), oTs[:], reads=[oTs], writes=[oT], sembuf=oTs)


def rstd_from_ss(kb, ss_ap, out_ap, n, ssb, outb):
    kb.op("dve", lambda e: e.tensor_scalar(out=out_ap, in0=ss_ap, scalar1=1.0 / n, scalar2=EPS, op0=ALU.mult, op1=ALU.add),
          reads=[ssb], writes=[outb])
    kb.op("act", lambda e: e.activation(out=out_ap, in_=out_ap, func=AF.Ln), reads=[outb], writes=[outb])
    kb.op("act", lambda e: e.activation(out=out_ap, in_=out_ap, func=AF.Exp, scale=-0.5), reads=[outb], writes=[outb])


def phase_diff(kb, cx, l, s):
    lambda_init = 0.8 - 0.6 * math.exp(-0.3 * l)
    with kb.phase():
        ident = load_const(kb, cx, "ident_bf", [128, 128], BF16)
        maskT = load_const(kb, cx, "mask_le_bf", [128, 128], BF16)
        rt = load_const(kb, cx, "rope_rt", [128, 128], BF16)
        rcos = load_const(kb, cx, "rope_cos", [128, S], F32)
        rsin = load_const(kb, cx, "rope_sin", [128, S], F32)
        fmT, tm = cx.fmT[s], cx.tm[s]
        sc_ps = [kb.ps([128, 512], F32, "sc") for _ in range(3)]
        pt_sb = [kb.sb([128, 512], BF16, "pt") for _ in range(3)]
        po_ps = [kb.ps([128, 512], F32, "po") for _ in range(4)]
        ptr = kb.ps([128, 512], BF16, "ptr")
        raw = kb.sb([128, 4, S], BF16, "raw")
        qz = [kb.sb([128, 4, S], BF16, f"qz{m}") for m in range(2)]
        for m in range(2):
            kb.op("pool", lambda e: e.memset(qz[m][:], 0.0), reads=[], writes=[qz[m]])
        qr = None
        kr = kb.sb([128, 4, S], BF16, "kr")
        t1s = [kb.sb([128, 512], F32, "t1") for _ in range(2)]
        t2s = [kb.sb([128, 512], F32, "t2") for _ in range(2)]
        n = 0
        for (r0, dst) in ((R_DQ, qr), (R_DK, kr)):
            kb.dma("sp", raw[:], fmT.t[r0:r0 + 512, :].rearrange("(h p) t -> p h t", p=128), reads=[fmT], writes=[raw], sembuf=raw)
            for h in range(4):
                for tb in range(4):
                    ps = sc_ps[n % 2]
                    t1 = t1s[n % 2]
                    t2 = t2s[n % 2]
                    n += 1
                    cs = slice(tb * 512, (tb + 1) * 512)
                    kb.mm(ps[:], [(rt[:], raw[:, h, cs])], reads=[rt, raw], writes=[ps])
                    kb.op("dve", lambda e: e.tensor_tensor(out=t1[:], in0=ps[:], in1=rsin[:, cs], op=ALU.mult), reads=[ps, rsin], writes=[t1])
                    kb.op("pool", lambda e: e.tensor_tensor(out=t2[:], in0=raw[:, h, cs], in1=rcos[:, cs], op=ALU.mult),
                          reads=[raw, rcos], writes=[t2])
                    if dst is None:
                        for m in range(2):
                            ps_ = slice(m * 64, (m + 1) * 64)
                            kb.op("dve", lambda e: e.tensor_tensor(out=qz[m][ps_, h, cs], in0=t1[ps_, :], in1=t2[ps_, :], op=ALU.add),
                                  reads=[t1, t2], writes=[qz[m]])
                    else:
                        kb.op("dve", lambda e: e.tensor_tensor(out=dst[:, h, cs], in0=t1[:], in1=t2[:], op=ALU.add), reads=[t1, t2], writes=[dst])
        vaug = load_vaug(kb, cx, s, C_DV, "vaugd")
        lpb = kb.sb([128, 256], F32, "lpb")
        kb.dma("sp", lpb[:], cx.diff_lambda.t[l:l + 1].rearrange("o a b -> o (a b)").broadcast_to([128, 256]), reads=[cx.diff_lambda],
               writes=[lpb], sembuf=lpb)
        lt = kb.sb([128, 8], F32, "lt")
        lj = kb.sb([128, 64], F32, "lj")
        for k in range(2):
            kb.op("dve", lambda e: e.scalar_tensor_tensor(out=lj[:], in0=lpb[:, k * 128:k * 128 + 64], scalar=1.0, in1=lpb[:, k * 128 + 64:k * 128 + 128],
                                                          op0=ALU.mult, op1=ALU.mult, accum_out=lt[:, k:k + 1]), reads=[lpb], writes=[lj, lt])
        kb.op("act", lambda e: e.activation(out=lt[:, 2:4], in_=lt[:, 0:2], func=AF.Exp), reads=[lt], writes=[lt])
        kb.op("dve", lambda e: e.tensor_tensor(out=lt[:, 4:5], in0=lt[:, 3:4], in1=lt[:, 2:3], op=ALU.subtract), reads=[lt], writes=[lt])
        kb.op("dve", lambda e: e.tensor_scalar(out=lt[:, 5:6], in0=lt[:, 4:5], scalar1=-lambda_init, scalar2=None, op0=ALU.add), reads=[lt], writes=[lt])
        nlam = lt[:, 5:6]
        gsb = kb.sb([128, 128], F32, "gsb")
        kb.dma("sp", gsb[:], cx.diff_subln_g.t[l:l + 1, :].broadcast_to([128, 128]), reads=[cx.diff_subln_g], writes=[gsb], sembuf=gsb)
        kb.op("dve", lambda e: e.tensor_scalar(out=gsb[:], in0=gsb[:], scalar1=1.0 - lambda_init, scalar2=None, op0=ALU.mult), reads=[gsb], writes=[gsb])
        oTs = kb.sb([128, 4, S], BF16, "oTs")
        otm = kb.sb([128, NT, 512], BF16, "otm")
        sg = kb.sb([128, NT, 512], F32, "sg")
        gin_v = raw[:].rearrange("p h t -> p (h t)").rearrange("p (t n) -> p t n", n=512)
        kb.dma("sp", gin_v, tm.t[:, C_DG:C_DG + 512].rearrange("(t p) n -> p t n", p=128), reads=[tm], writes=[raw], sembuf=raw)
        kb.op("act", lambda e: e.activation(out=sg[:], in_=gin_v, func=AF.Silu), reads=[raw], writes=[sg])
        sm4 = [kb.sb([128, 8], F32, "sm4") for _ in range(2)]
        o2s = [kb.sb([128, 128], F32, "o2s") for _ in range(2)]
        ob = [kb.sb([128, 128], F32, "ob") for _ in range(2)]
        jk = kb.sb([128, 128], F32, "jk")
        cnt = {"n": 0}

        def fin(h, i, pos):
            k = cnt["n"] % 2
            cnt["n"] += 1
            po1, po2 = pos
            v = sm4[k]
            kb.op("dve", lambda e: e.reciprocal(out=v[:, 0:1], in_=po1[:, 128:129]), reads=[po1], writes=[v])
            kb.op("dve", lambda e: e.reciprocal(out=v[:, 1:2], in_=po2[:, 128:129]), reads=[po2], writes=[v])
            kb.op("dve", lambda e: e.tensor_tensor(out=v[:, 2:3], in0=v[:, 1:2], in1=nlam, op=ALU.mult), reads=[v, lt], writes=[v])
            kb.op("dve", lambda e: e.tensor_scalar(out=o2s[k][:], in0=po2[:, 0:128], scalar1=v[:, 2:3], scalar2=None, op0=ALU.mult),
                  reads=[po2, v], writes=[o2s[k]])
            kb.op("dve", lambda e: e.scalar_tensor_tensor(out=ob[k][:], in0=po1[:, 0:128], scalar=v[:, 0:1], in1=o2s[k][:],
                                                          op0=ALU.mult, op1=ALU.add), reads=[po1, v, o2s[k]], writes=[ob[k]])
            kb.op("dve", lambda e: e.scalar_tensor_tensor(out=jk[:], in0=ob[k][:], scalar=1.0, in1=ob[k][:], op0=ALU.mult, op1=ALU.mult,
                                                          accum_out=v[:, 3:4]), reads=[ob[k]], writes=[jk, v])
            rstd_from_ss(kb, v[:, 3:4], v[:, 4:5], 128, v, v)
            kb.op("dve", lambda e: e.scalar_tensor_tensor(out=ob[k][:], in0=ob[k][:], scalar=v[:, 4:5], in1=gsb[:], op0=ALU.mult, op1=ALU.mult),
                  reads=[ob[k], v, gsb], writes=[ob[k]])
            kb.op("dve", lambda e: e.tensor_tensor(out=otm[:, i, h * 128:(h + 1) * 128], in0=ob[k][:], in1=sg[:, i, h * 128:(h + 1) * 128],
                                                   op=ALU.mult), reads=[ob[k], sg], writes=[otm])

        maps = []
        for m in range(2):
            maps.append(dict(kT=(lambda h, j, m=m: kr[:, h, j * 128:(j + 1) * 128]),
                             qT=(lambda h, i, m=m: qz[m][:, h, i * 128:(i + 1) * 128]),
                             bias=None, reads=[kr, qz[m]]))
        attn_core(kb, cx, maps, 4, vaug, fin, 64 ** -0.5, (sc_ps, pt_sb, po_ps, maskT))
        for i in range(NT):
            for c in range(4):
                kb.op("pe", lambda e: e.transpose(out=ptr[:, c * 128:(c + 1) * 128], in_=otm[:, i, c * 128:(c + 1) * 128], identity=ident[:]),
                      reads=[otm, ident], writes=[ptr])
            kb.op("act", lambda e: e.activation(out=oTs[:, :, i * 128:(i + 1) * 128], in_=ptr[:].rearrange("p (a b) -> p a b", a=4),
                                                func=AF.Copy), reads=[ptr], writes=[oTs])
        oT = cx.oT[s]
        kb.dma("sp", oT.t[512:1024, :].rearrange("(c p) t -> p c t", p=128), oTs[:], reads=[oTs], writes=[oT], sembuf=oTs)


def bc3(ap2, n):
    a = ap2.shape[1]
    return ap2.unsqueeze(2).broadcast_to([ap2.shape[0], a, n])


def phase_ssd(kb, cx, l, s):
    with kb.phase():
        ident = load_const(kb, cx, "ident_bf", [128, 128], BF16)
        identf = load_const(kb, cx, "ident_f", [128, 128], F32)
        tri = load_const(kb, cx, "tri_le_f", [128, 128], F32)
        gtf = load_const(kb, cx, "gt_f", [128, 128], F32)
        negm = load_const(kb, cx, "negmask_f", [128, 128], F32)
        ones = load_const(kb, cx, "ones_f", [128, 128], F32)
        fmT, tm, sm = cx.fmT[s], cx.tm[s], cx.sm[s]
        pG = kb.ps([128, 2, 128], F32, "pG")
        pD = [kb.ps([128, 128], F32, "pD") for _ in range(2)]
        pY = kb.ps([128, 512], F32, "pY")
        pY2 = kb.ps([128, 512], F32, "pY2")
        pS = kb.ps([128, 512], F32, "pS")
        ptr = kb.ps([128, 512], BF16, "ptr")
        ptr2 = kb.ps([128, 512], BF16, "ptr2")
        cw4 = kb.sb([4, 1024], F32, "cw4")
        kb.dma("sp", cw4[:], cx.ssd_conv_w.t[l], reads=[cx.ssd_conv_w], writes=[cw4], sembuf=cw4)
        cb8 = kb.sb([8, 128], F32, "cb8")
        kb.dma("sp", cb8[:], cx.ssd_conv_b.t[l:l + 1, :].rearrange("o (c p) -> (o c) p", p=128), reads=[cx.ssd_conv_b], writes=[cb8], sembuf=cb8)
        par = kb.sb([128, 40], F32, "par")
        for c in range(8):
            kb.op("pe", lambda e: e.transpose(out=pD[0][:, c * 4:(c + 1) * 4], in_=cw4[0:4, c * 128:(c + 1) * 128], identity=identf[0:4, 0:4]),
                  reads=[cw4, identf], writes=[pD[0]])
        kb.op("pe", lambda e: e.transpose(out=pD[0][:, 32:40], in_=cb8[0:8, :], identity=identf[0:8, 0:8]), reads=[cb8, identf], writes=[pD[0]])
        kb.op("dve", lambda e: e.tensor_copy(out=par[:], in_=pD[0][:, 0:40]), reads=[pD[0]], writes=[par])
        if cx.stop < 11:
            return
        dtb = kb.sb([128, 8], F32, "dtb")
        kb.dma("sp", dtb[:], cx.ssd_dt_bias.t[l:l + 1, :].broadcast_to([128, 8]), reads=[cx.ssd_dt_bias], writes=[dtb], sembuf=dtb)
        negA = kb.sb([128, 8], F32, "negA")
        kb.dma("sp", negA[:], cx.ssd_a_log.t[l:l + 1, :].broadcast_to([128, 8]), reads=[cx.ssd_a_log], writes=[negA], sembuf=negA)
        kb.op("act", lambda e: e.activation(out=negA[:], in_=negA[:], func=AF.Exp), reads=[negA], writes=[negA])
        kb.op("dve", lambda e: e.tensor_scalar(out=negA[:], in0=negA[:], scalar1=-1.0, scalar2=None, op0=ALU.mult), reads=[negA], writes=[negA])
        dsk = kb.sb([128, 8], F32, "dsk")
        kb.dma("sp", dsk[:], cx.ssd_d.t[l:l + 1, :].broadcast_to([128, 8]), reads=[cx.ssd_d], writes=[dsk], sembuf=dsk)
        ng = kb.sb([128, 512], F32, "ng")
        kb.dma("sp", ng[:], cx.ssd_norm_g.t[l:l + 1, :].broadcast_to([128, 512]), reads=[cx.ssd_norm_g], writes=[ng], sembuf=ng)
        xbc = kb.sb([128, 8, S], BF16, "xbc")
        kb.dma("sp", xbc[:], fmT.t[R_SX:R_SX + 1024, :].rearrange("(c p) t -> p c t", p=128), reads=[fmT], writes=[xbc], sembuf=xbc)
        xc = kb.sb([128, 8, S], BF16, "xc")
        accs = [kb.sb([128, S], F32, "acc") for _ in range(2)]
        for c in range(8):
            acc = accs[c % 2]
            w = lambda j: par[:, c * 4 + j:c * 4 + j + 1]
            kb.op("dve", lambda e: e.tensor_scalar(out=acc[:], in0=xbc[:, c, :], scalar1=w(3), scalar2=None, op0=ALU.mult), reads=[xbc, par], writes=[acc])
            for sh in (1, 2, 3):
                kb.op("dve", lambda e: e.scalar_tensor_tensor(out=acc[:, sh:], in0=xbc[:, c, 0:S - sh], scalar=w(3 - sh), in1=acc[:, sh:],
                                                              op0=ALU.mult, op1=ALU.add), reads=[xbc, par, acc], writes=[acc])
            kb.op("act", lambda e: e.activation(out=xc[:, c, :], in_=acc[:], func=AF.Silu, bias=par[:, 32 + c:33 + c]), reads=[acc, par], writes=[xc])
        if cx.stop < 12:
            return
        dt = kb.sb([128, NT, 8], F32, "dt")
        kb.dma("sp", dt[:], sm.t[:, 4:12].rearrange("(t p) n -> p t n", p=128), reads=[sm], writes=[dt], sembuf=dt, allow_slow_non_contiguous=True)
        kb.op("dve", lambda e: e.tensor_tensor(out=dt[:], in0=dt[:], in1=dtb[:].unsqueeze(1).broadcast_to([128, NT, 8]), op=ALU.add), reads=[dt, dtb], writes=[dt])
        if cx.stop2 == 1:
            return
        kb.op("act", lambda e: e.activation(out=dt[:], in_=dt[:], func=AF.Exp), reads=[dt], writes=[dt])
        kb.op("act", lambda e: e.activation(out=dt[:], in_=dt[:], func=AF.Ln, bias=1.0), reads=[dt], writes=[dt])
        if cx.stop2 == 2:
            return
        av = kb.sb([128, NT, 8], F32, "av")
        kb.op("dve", lambda e: e.tensor_tensor(out=av[:], in0=dt[:], in1=negA[:].unsqueeze(1).broadcast_to([128, NT, 8]), op=ALU.mult), reads=[dt, negA], writes=[av])
        if cx.stop2 == 3:
            return
        av2 = av[:].rearrange("p t h -> p (t h)")
        kb.mm(pY[:, 0:128], [(tri[:], av2)], reads=[tri, av], writes=[pY])
        kb.mm(pY2[:, 0:128], [(ones[:], av2)], reads=[ones, av], writes=[pY2])
        if cx.stop2 == 4:
            return
        Acum = kb.sb([128, NT, 8], F32, "Acum")
        eA = kb.sb([128, NT, 8], F32, "eA")
        eAt = kb.sb([128, NT, 8], F32, "eAt")
        dsv = kb.sb([128, NT, 8], F32, "dsv")
        f2 = lambda b: b[:].rearrange("p t h -> p (t h)")
        kb.op("dve", lambda e: e.tensor_copy(out=f2(Acum), in_=pY[:, 0:128]), reads=[pY], writes=[Acum])
        kb.op("dve", lambda e: e.tensor_tensor(out=f2(dsv), in0=pY2[:, 0:128], in1=f2(Acum), op=ALU.subtract), reads=[pY2, Acum], writes=[dsv])
        if cx.stop2 == 5:
            return
        kb.op("dve", lambda e: e.tensor_copy(out=f2(eAt), in_=pY2[:, 0:128]), reads=[pY2], writes=[eAt])
        kb.op("act", lambda e: e.activation(out=f2(eAt), in_=f2(eAt), func=AF.Exp), reads=[eAt], writes=[eAt])
        if cx.stop2 == 6:
            return
        kb.op("act", lambda e: e.activation(out=f2(eA), in_=f2(Acum), func=AF.Exp), reads=[Acum], writes=[eA])
        if cx.stop2 == 7:
            return
        dsv0 = dsv
        dsv = kb.sb([128, NT, 8], F32, "dsv2")
        kb.op("act", lambda e: e.activation(out=f2(dsv), in_=f2(dsv0), func=AF.Exp), reads=[dsv0], writes=[dsv])
        if cx.stop < 13:
            return
        xs_tm = kb.sb([128, NT, 512], BF16, "xs_tm")
        B_tm = kb.sb([128, NT, 256], BF16, "B_tm")
        for t in range(NT):
            ts_ = slice(t * 128, (t + 1) * 128)
            for c in range(4):
                kb.op("pe", lambda e: e.transpose(out=ptr[:, c * 128:(c + 1) * 128], in_=xc[:, c, ts_], identity=ident[:]), reads=[xc, ident], writes=[ptr])
            kb.op("act", lambda e: e.activation(out=xs_tm[:, t, :], in_=ptr[:, 0:512], func=AF.Copy), reads=[ptr], writes=[xs_tm])
            for c in range(2):
                kb.op("pe", lambda e: e.transpose(out=ptr2[:, c * 128:(c + 1) * 128], in_=xc[:, 4 + c, ts_], identity=ident[:]), reads=[xc, ident], writes=[ptr2])
            kb.op("dve", lambda e: e.tensor_copy(out=B_tm[:, t, :], in_=ptr2[:, 0:256]), reads=[ptr2], writes=[B_tm])
        if cx.stop < 14:
            return
        zin = kb.sb([128, NT, 512], BF16, "zin")
        kb.dma("sp", zin[:], tm.t[:, C_SZ:C_SZ + 512].rearrange("(t p) n -> p t n", p=128), reads=[tm], writes=[zin], sembuf=zin)
        oTs = kb.sb([128, 4, S], BF16, "oTs")
        ST = kb.sb([128, 512], F32, "ST")
        STb = kb.sb([128, 512], BF16, "STb")
        tmpS = kb.sb([128, 512], F32, "tmpS")
        kb.op("dve", lambda e: e.memset(ST[:], 0.0), reads=[], writes=[ST])
        Xd = [kb.sb([128, 512], BF16, "Xd") for _ in range(2)]
        Xds = [kb.sb([128, 512], BF16, "Xds") for _ in range(2)]
        adg8 = [kb.sb([128, 128], F32, "adg") for _ in range(8)]
        Eb8 = [kb.sb([128, 128], F32, "Eb") for _ in range(8)]
        Gm = [kb.sb([128, 2, 128], F32, "Gm") for _ in range(2)]
        maskle = load_const(kb, cx, "tri_le_f", [128, 128], F32) if False else tri
        MT = [kb.sb([128, 128], BF16, "MT") for _ in range(2)]
        ysb = kb.sb([128, 512], F32, "ysb")
        y2 = kb.sb([128, 512], F32, "y2")
        szt = kb.sb([128, 512], F32, "szt")
        jk = kb.sb([128, 512], F32, "jk")
        otm = [kb.sb([128, 512], BF16, "otm") for _ in range(2)]
        sv = kb.sb([128, NT, 4], F32, "sv")
        v3 = lambda b: b[:].rearrange("p (h d) -> p h d", h=8)
        n = 0
        for t in range(NT if cx.stop >= 20 else max(0, cx.stop - 14)):
            ts_ = slice(t * 128, (t + 1) * 128)
            xd, xds = Xd[t % 2], Xds[t % 2]
            kb.op("dve", lambda e: e.tensor_tensor(out=v3(xd), in0=xs_tm[:, t, :].rearrange("p (h d) -> p h d", h=8), in1=bc3(dt[:, t, :], 64),
                                                   op=ALU.mult), reads=[xs_tm, dt], writes=[xd])
            kb.op("dve", lambda e: e.tensor_tensor(out=v3(xds), in0=v3(xd), in1=bc3(dsv[:, t, :], 64), op=ALU.mult), reads=[xd, dsv], writes=[xds])
            for g in range(2):
                kb.mm(pG[:, g, :], [(xc[:, 4 + g, ts_], xc[:, 6 + g, ts_])], reads=[xc], writes=[pG])
            gm = Gm[t % 2]
            kb.op("dve", lambda e: e.tensor_tensor(out=gm[:], in0=pG[:], in1=maskle[:].unsqueeze(1).broadcast_to([128, 2, 128]), op=ALU.mult),
                  reads=[pG, maskle], writes=[gm])
            for h in range(8):
                a_ = adg8[h]
                kb.op("dve", lambda e: e.tensor_scalar(out=a_[:], in0=tri[:], scalar1=av[:, t, h:h + 1], scalar2=None, op0=ALU.mult),
                      reads=[tri, av], writes=[a_])
            for h in range(8):
                pd = pD[h % 2]
                kb.mm(pd[:], [(gtf[:], adg8[h][:])], reads=[gtf, adg8[h]], writes=[pd])
                kb.op("act", lambda e: e.activation(out=Eb8[h][:], in_=pd[:], func=AF.Exp), reads=[pd], writes=[Eb8[h]])
            for h in range(8):
                g = h // 4
                m_ = MT[h % 2]
                kb.op("dve", lambda e: e.tensor_tensor(out=m_[:], in0=gm[:, g, :], in1=Eb8[h][:], op=ALU.mult), reads=[gm, Eb8[h]], writes=[m_])
                hs = slice(h * 64, (h + 1) * 64)
                kb.mm(pY[:, hs], [(m_[:], xd[:, hs])], reads=[m_, xd], writes=[pY])
                if t > 0:
                    kb.mm(pY2[:, hs], [(xc[:, 6 + g, ts_], STb[:, hs])], reads=[xc, STb], writes=[pY2])
                kb.mm(pS[:, hs], [(B_tm[:, t, g * 128:(g + 1) * 128], xds[:, hs])], reads=[B_tm, xds], writes=[pS])
            kb.op("act", lambda e: e.activation(out=ysb[:], in_=pY[:], func=AF.Copy), reads=[pY], writes=[ysb])
            if t > 0:
                kb.op("dve", lambda e: e.tensor_tensor(out=v3(y2), in0=pY2[:].rearrange("p (h d) -> p h d", h=8), in1=bc3(eA[:, t, :], 64), op=ALU.mult),
                      reads=[pY2, eA], writes=[y2])
                kb.op("dve", lambda e: e.tensor_tensor(out=ysb[:], in0=ysb[:], in1=y2[:], op=ALU.add), reads=[ysb, y2], writes=[ysb])
            if t < NT - 1:
                kb.op("dve", lambda e: e.tensor_tensor(out=v3(tmpS), in0=v3(ST), in1=bc3(eAt[:, t, :], 64), op=ALU.mult), reads=[ST, eAt], writes=[tmpS])
                kb.op("dve", lambda e: e.tensor_tensor(out=ST[:], in0=tmpS[:], in1=pS[:], op=ALU.add), reads=[tmpS, pS], writes=[ST])
                kb.op("act", lambda e: e.activation(out=STb[:], in_=ST[:], func=AF.Copy), reads=[ST], writes=[STb])
            kb.op("dve", lambda e: e.tensor_tensor(out=v3(y2), in0=xs_tm[:, t, :].rearrange("p (h d) -> p h d", h=8), in1=bc3(dsk[:], 64), op=ALU.mult),
                  reads=[xs_tm, dsk], writes=[y2])
            kb.op("dve", lambda e: e.tensor_tensor(out=ysb[:], in0=ysb[:], in1=y2[:], op=ALU.add), reads=[ysb, y2], writes=[ysb])
            kb.op("act", lambda e: e.activation(out=szt[:], in_=zin[:, t, :], func=AF.Silu), reads=[zin], writes=[szt])
            kb.op("dve", lambda e: e.tensor_tensor(out=ysb[:], in0=ysb[:], in1=szt[:], op=ALU.mult), reads=[ysb, szt], writes=[ysb])
            kb.op("dve", lambda e: e.scalar_tensor_tensor(out=jk[:], in0=ysb[:], scalar=1.0, in1=ysb[:], op0=ALU.mult, op1=ALU.mult,
                                                          accum_out=sv[:, t, 0:1]), reads=[ysb], writes=[jk, sv])
            rstd_from_ss(kb, sv[:, t, 0:1], sv[:, t, 1:2], 512, sv, sv)
            om = otm[t % 2]
            kb.op("dve", lambda e: e.scalar_tensor_tensor(out=om[:], in0=ysb[:], scalar=sv[:, t, 1:2], in1=ng[:], op0=ALU.mult, op1=ALU.mult),
                  reads=[ysb, sv, ng], writes=[om])
            for c in range(4):
                kb.op("pe", lambda e: e.transpose(out=ptr[:, c * 128:(c + 1) * 128], in_=om[:, c * 128:(c + 1) * 128], identity=ident[:]),
                      reads=[om, ident], writes=[ptr])
            kb.op("act", lambda e: e.activation(out=oTs[:, :, ts_], in_=ptr[:, 0:512].rearrange("p (a b) -> p a b", a=4), func=AF.Copy),
                  reads=[ptr], writes=[oTs])
        oT = cx.oT[s]
        kb.dma("sp", oT.t[1024:1536, :].rearrange("(c p) t -> p c t", p=128), oTs[:], reads=[oTs], writes=[oT], sembuf=oTs)


def s5_alloc(kb, cx):
    def P(shape, dt, name):
        t = kb.es.enter_context(kb.nc.sbuf_tensor("s5p_" + name, list(shape), dt))
        return Buf(kb, name, t)
    cx.s5 = dict(BT=P([128, 16, 2, 128], BF16, "BT"), CP=P([128, 16, 2, 128], BF16, "CP"),
                 PWr=P([128, 11, 16], F32, "PWr"), PWi=P([128, 11, 16], F32, "PWi"), NPWi=P([128, 11, 16], F32, "NPWi"),
                 dsk=P([128, 4], F32, "dsk"), Ur=P([128, 11, 16], F32, "Ur"), Ui=P([128, 11, 16], F32, "Ui"),
                 NUi=P([128, 11, 16], F32, "NUi"), R=P([128, 16], F32, "R"))


def phase_s5_prep(kb, cx, l):
    TWO_PI = 2.0 * math.pi
    s5 = cx.s5
    with kb.phase():
        identf = load_const(kb, cx, "ident_f", [128, 128], F32)
        ones = load_const(kb, cx, "ones_f", [128, 128], F32)
        gmask = load_const(kb, cx, "gmask", [128, 8], F32)
        pt = kb.ps([128, 128], F32, "pt")
        pts = [kb.ps([128, 128], F32, "pts") for _ in range(2)]
        p3 = kb.sb([16, 3, 128], F32, "p3")
        kb.dma("sp", p3[:, 0, :], cx.s5_a_re.t[l].rearrange("(st g) p -> st (g p)", g=2), reads=[cx.s5_a_re], writes=[p3], sembuf=p3)
        kb.dma("sp", p3[:, 1, :], cx.s5_a_im.t[l].rearrange("(st g) p -> st (g p)", g=2), reads=[cx.s5_a_im], writes=[p3], sembuf=p3)
        ld = kb.sb([16, 2], F32, "ld")
        kb.dma("sp", ld[:], cx.s5_log_dt.t[l:l + 1, :].rearrange("o (st g) -> (o st) g", g=2), reads=[cx.s5_log_dt], writes=[ld], sembuf=ld)
        for g in range(2):
            kb.op("dve", lambda e: e.tensor_scalar(out=p3[:, 2, g * 64:(g + 1) * 64], in0=ones[0:16, 0:64], scalar1=ld[:, g:g + 1], scalar2=None,
                                                   op0=ALU.mult), reads=[ones, ld], writes=[p3])
        d4 = kb.sb([4, 128], F32, "d4")
        kb.dma("sp", d4[:], cx.s5_d.t[l].rearrange("(cc gl) h -> cc (gl h)", gl=8), reads=[cx.s5_d], writes=[d4], sembuf=d4)
        for k in range(3):
            kb.op("pe", lambda e: e.transpose(out=pt[:, k * 16:(k + 1) * 16], in_=p3[0:16, k, :], identity=identf[0:16, 0:16]),
                  reads=[p3, identf], writes=[pt])
        kb.op("pe", lambda e: e.transpose(out=pt[:, 48:52], in_=d4[0:4, :], identity=identf[0:4, 0:4]), reads=[d4, identf], writes=[pt])
        prm = kb.sb([128, 3, 16], F32, "prm")
        kb.op("dve", lambda e: e.tensor_copy(out=prm[:].rearrange("p a b -> p (a b)"), in_=pt[:, 0:48]), reads=[pt], writes=[prm])
        kb.op("dve", lambda e: e.tensor_copy(out=s5["dsk"][:], in_=pt[:, 48:52]), reads=[pt], writes=[s5["dsk"]])
        aR, aI, LD = prm[:, 0, :], prm[:, 1, :], prm[:, 2, :]
        w = kb.sb([128, 24, 16], F32, "w")
        W = lambda i: w[:, i, :]
        dve = lambda fn, rd=(), wr=(): kb.op("dve", fn, reads=list(rd) or [w, prm], writes=list(wr) or [w])
        act = lambda fn, rd=(), wr=(): kb.op("act", fn, reads=list(rd) or [w, prm], writes=list(wr) or [w])
        TT = lambda o, a, b, op: dve(lambda e: e.tensor_tensor(out=o, in0=a, in1=b, op=op))
        act(lambda e: e.activation(out=W(0), in_=LD, func=AF.Exp))
        TT(W(1), aR, W(0), ALU.mult)
        act(lambda e: e.activation(out=W(1), in_=W(1), func=AF.Exp))
        TT(W(2), aI, W(0), ALU.mult)
        dve(lambda e: e.tensor_scalar(out=W(2), in0=W(2), scalar1=1.0 / TWO_PI, scalar2=None, op0=ALU.mult))
        wi = kb.sb([128, 16], mybir.dt.int32, "wi")
        dve(lambda e: e.tensor_copy(out=wi[:], in_=W(2)), wr=[wi])
        dve(lambda e: e.tensor_copy(out=W(3), in_=wi[:]), rd=[wi])
        TT(W(3), W(2), W(3), ALU.subtract)
        dve(lambda e: e.tensor_scalar(out=W(4), in0=W(3), scalar1=0.25, scalar2=None, op0=ALU.add))

        def wrap(src, dst, t1, t2):
            dve(lambda e: e.tensor_single_scalar(out=t1, in_=src, scalar=0.5, op=ALU.is_gt))
            dve(lambda e: e.tensor_single_scalar(out=t2, in_=src, scalar=-0.5, op=ALU.is_lt))
            TT(dst, src, t1, ALU.subtract)
            TT(dst, dst, t2, ALU.add)
        wrap(W(3), W(5), W(6), W(7))
        wrap(W(4), W(8), W(6), W(7))
        act(lambda e: e.activation(out=W(9), in_=W(5), func=AF.Sin, scale=TWO_PI))
        act(lambda e: e.activation(out=W(10), in_=W(8), func=AF.Sin, scale=TWO_PI))
        PWr, PWi, NPWi = s5["PWr"], s5["PWi"], s5["NPWi"]
        kb.op("dve", lambda e: e.tensor_tensor(out=PWr[:, 0, :], in0=W(1), in1=W(10), op=ALU.mult), reads=[w], writes=[PWr])
        kb.op("dve", lambda e: e.tensor_tensor(out=PWi[:, 0, :], in0=W(1), in1=W(9), op=ALU.mult), reads=[w], writes=[PWi])
        for k in range(10):
            kb.op("dve", lambda e: e.tensor_tensor(out=W(11), in0=PWr[:, k, :], in1=PWr[:, k, :], op=ALU.mult), reads=[PWr], writes=[w])
            kb.op("dve", lambda e: e.tensor_tensor(out=W(12), in0=PWi[:, k, :], in1=PWi[:, k, :], op=ALU.mult), reads=[PWi], writes=[w])
            kb.op("dve", lambda e: e.tensor_tensor(out=PWr[:, k + 1, :], in0=W(11), in1=W(12), op=ALU.subtract), reads=[w], writes=[PWr])
            kb.op("dve", lambda e: e.scalar_tensor_tensor(out=PWi[:, k + 1, :], in0=PWr[:, k, :], scalar=2.0, in1=PWi[:, k, :], op0=ALU.mult, op1=ALU.mult),
                  reads=[PWr, PWi], writes=[PWi])
        kb.op("dve", lambda e: e.tensor_scalar(out=NPWi[:], in0=PWi[:], scalar1=-1.0, scalar2=None, op0=ALU.mult), reads=[PWi], writes=[NPWi])
        Ur, Ui, NUi, Rm = s5["Ur"], s5["Ui"], s5["NUi"], s5["R"]
        kb.op("dve", lambda e: e.tensor_copy(out=Ur[:, 0, :], in_=W(10)), reads=[w], writes=[Ur])
        kb.op("dve", lambda e: e.tensor_copy(out=Ui[:, 0, :], in_=W(9)), reads=[w], writes=[Ui])
        kb.op("dve", lambda e: e.tensor_copy(out=Rm[:], in_=W(1)), reads=[w], writes=[Rm])
        for k in range(10):
            kb.op("dve", lambda e: e.tensor_tensor(out=W(19), in0=Ur[:, k, :], in1=Ur[:, k, :], op=ALU.mult), reads=[Ur], writes=[w])
            kb.op("dve", lambda e: e.tensor_tensor(out=W(20), in0=Ui[:, k, :], in1=Ui[:, k, :], op=ALU.mult), reads=[Ui], writes=[w])
            kb.op("dve", lambda e: e.tensor_tensor(out=Ur[:, k + 1, :], in0=W(19), in1=W(20), op=ALU.subtract), reads=[w], writes=[Ur])
            kb.op("dve", lambda e: e.scalar_tensor_tensor(out=Ui[:, k + 1, :], in0=Ur[:, k, :], scalar=2.0, in1=Ui[:, k, :], op0=ALU.mult, op1=ALU.mult),
                  reads=[Ur, Ui], writes=[Ui])
        kb.op("dve", lambda e: e.tensor_scalar(out=NUi[:], in0=Ui[:], scalar1=-1.0, scalar2=None, op0=ALU.mult), reads=[Ui], writes=[NUi])
        Ets = [kb.sb([128, 2, S], F32, "Et") for _ in range(2)]
        for st in range(16):
            E = Ets[st % 2]
            kb.op("dve", lambda e: e.memset(E[:, 0, 0:1], 1.0), reads=[], writes=[E])
            kb.op("dve", lambda e: e.memset(E[:, 1, 0:1], 0.0), reads=[], writes=[E])
            for k in range(11):
                sh = 1 << k
                ur, ui, nui = Ur[:, k, st:st + 1], Ui[:, k, st:st + 1], NUi[:, k, st:st + 1]
                kb.op("dve", lambda e: e.tensor_scalar(out=E[:, 0, sh:2 * sh], in0=E[:, 0, 0:sh], scalar1=ur, scalar2=None, op0=ALU.mult),
                      reads=[E, Ur], writes=[E])
                kb.op("dve", lambda e: e.scalar_tensor_tensor(out=E[:, 0, sh:2 * sh], in0=E[:, 1, 0:sh], scalar=nui, in1=E[:, 0, sh:2 * sh],
                                                              op0=ALU.mult, op1=ALU.add), reads=[E, NUi], writes=[E])
                kb.op("dve", lambda e: e.tensor_scalar(out=E[:, 1, sh:2 * sh], in0=E[:, 0, 0:sh], scalar1=ui, scalar2=None, op0=ALU.mult),
                      reads=[E, Ui], writes=[E])
                kb.op("dve", lambda e: e.scalar_tensor_tensor(out=E[:, 1, sh:2 * sh], in0=E[:, 1, 0:sh], scalar=ur, in1=E[:, 1, sh:2 * sh],
                                                              op0=ALU.mult, op1=ALU.add), reads=[E, Ur], writes=[E])
            kb.dma("sp", cx.Etab.t[st].rearrange("a p t -> p a t"), E[:], reads=[E], writes=[cx.Etab], sembuf=E)
        ar, ai = PWr[:, 0, :], PWi[:, 0, :]
        rdP = [w, prm, PWr, PWi]
        dveP = lambda fn: kb.op("dve", fn, reads=rdP, writes=[w])
        dveP(lambda e: e.tensor_scalar(out=W(13), in0=ar, scalar1=-1.0, scalar2=None, op0=ALU.add))
        dveP(lambda e: e.tensor_tensor(out=W(14), in0=aR, in1=aR, op=ALU.mult))
        dveP(lambda e: e.tensor_tensor(out=W(15), in0=aI, in1=aI, op=ALU.mult))
        dveP(lambda e: e.tensor_tensor(out=W(14), in0=W(14), in1=W(15), op=ALU.add))
        dveP(lambda e: e.reciprocal(out=W(14), in_=W(14)))
        dveP(lambda e: e.tensor_tensor(out=W(15), in0=W(13), in1=aR, op=ALU.mult))
        dveP(lambda e: e.tensor_tensor(out=W(16), in0=ai, in1=aI, op=ALU.mult))
        dveP(lambda e: e.tensor_tensor(out=W(15), in0=W(15), in1=W(16), op=ALU.add))
        dveP(lambda e: e.tensor_tensor(out=W(17), in0=W(15), in1=W(14), op=ALU.mult))
        dveP(lambda e: e.tensor_tensor(out=W(15), in0=ai, in1=aR, op=ALU.mult))
        dveP(lambda e: e.tensor_tensor(out=W(16), in0=W(13), in1=aI, op=ALU.mult))
        dveP(lambda e: e.tensor_tensor(out=W(15), in0=W(15), in1=W(16), op=ALU.subtract))
        dveP(lambda e: e.tensor_tensor(out=W(18), in0=W(15), in1=W(14), op=ALU.mult))
        bq = kb.sb([128, 2, 16, 16], F32, "bq")
        for ri, src in enumerate((cx.s5_b_re, cx.s5_b_im)):
            v = src.t[l].rearrange("(st g) p h -> g p st h", g=2)
            for g in range(2):
                kb.dma("sp", bq[g * 64:(g + 1) * 64, ri, :, :], v[g], reads=[src], writes=[bq], sembuf=bq, allow_slow_non_contiguous=True)
        bb = kb.sb([128, 2, 16, 16], F32, "bb")
        tq = kb.sb([128, 2, 16, 16], F32, "tq")
        gr, gi = bc3(W(17), 16), bc3(W(18), 16)
        rdB = [bq, w, tq, bb]
        kb.op("dve", lambda e: e.tensor_tensor(out=tq[:, 0], in0=bq[:, 0], in1=gr, op=ALU.mult), reads=rdB, writes=[tq])
        kb.op("dve", lambda e: e.tensor_tensor(out=tq[:, 1], in0=bq[:, 1], in1=gi, op=ALU.mult), reads=rdB, writes=[tq])
        kb.op("dve", lambda e: e.tensor_tensor(out=bb[:, 0], in0=tq[:, 0], in1=tq[:, 1], op=ALU.subtract), reads=rdB, writes=[bb])
        kb.op("dve", lambda e: e.tensor_tensor(out=tq[:, 0], in0=bq[:, 1], in1=gr, op=ALU.mult), reads=rdB, writes=[tq])
        kb.op("dve", lambda e: e.tensor_tensor(out=tq[:, 1], in0=bq[:, 0], in1=gi, op=ALU.mult), reads=rdB, writes=[tq])
        kb.op("dve", lambda e: e.tensor_tensor(out=bb[:, 1], in0=tq[:, 0], in1=tq[:, 1], op=ALU.add), reads=rdB, writes=[bb])
        bpad = kb.sb([128, 16, 2, 128], F32, "bpad")
        kb.op("pool", lambda e: e.memset(bpad[:], 0.0), reads=[], writes=[bpad])
        for st4 in range(4):
            for g in range(2):
                c0 = (st4 * 2 + g) * 16
                for ri in range(2):
                    kb.op("dve", lambda e: e.tensor_copy(out=bpad[g * 64:(g + 1) * 64, st4::4, ri, c0:c0 + 16], in_=bb[g * 64:(g + 1) * 64, ri, st4::4, :]),
                          reads=[bb], writes=[bpad])
        BT, CP = s5["BT"], s5["CP"]
        n = 0
        for st in range(16):
            for ri in range(2):
                p = pts[n % 2]
                n += 1
                kb.op("pe", lambda e: e.transpose(out=p[:], in_=bpad[:, st, ri, :], identity=identf[:]), reads=[bpad, identf], writes=[p])
                kb.op("act", lambda e: e.activation(out=BT[:, st, ri, :], in_=p[:], func=AF.Copy), reads=[p], writes=[BT])
        cn = kb.sb([128, 2, 4, 64], F32, "cn")
        for ri, src in enumerate((cx.s5_c_re, cx.s5_c_im)):
            for cc in range(4):
                kb.dma("sp", cn[:, ri, cc, :], src.t[l, cc * 8:(cc + 1) * 8].rearrange("g h p -> (g h) p"), reads=[src], writes=[cn], sembuf=cn)
        mts = [kb.sb([128, 128], F32, "mt") for _ in range(2)]
        for st in range(16):
            cc, gl0 = st // 4, (st % 4) * 2
            for ri in range(2):
                mt = mts[n % 2]
                p = pts[n % 2]
                n += 1
                for g in range(2):
                    kb.op("dve", lambda e: e.tensor_scalar(out=mt[:, g * 64:(g + 1) * 64], in0=cn[:, ri, cc, :], scalar1=gmask[:, gl0 + g:gl0 + g + 1],
                                                           scalar2=None, op0=ALU.mult), reads=[cn, gmask], writes=[mt])
                kb.op("pe", lambda e: e.transpose(out=p[:], in_=mt[:], identity=identf[:]), reads=[mt, identf], writes=[p])
                sgn = 1.0 if ri == 0 else -1.0
                kb.op("dve", lambda e: e.tensor_scalar(out=CP[:, st, ri, :], in0=p[:], scalar1=sgn, scalar2=None, op0=ALU.mult), reads=[p], writes=[CP])


def phase_s5(kb, cx, l, s):
    s5 = cx.s5
    BT, CP, PWr, PWi, NPWi, dsk = s5["BT"], s5["CP"], s5["PWr"], s5["PWi"], s5["NPWi"], s5["dsk"]
    with kb.phase():
        ident = load_const(kb, cx, "ident_bf", [128, 128], BF16)
        fmT, tm = cx.fmT[s], cx.tm[s]
        pYo = [kb.ps([128, 512], F32, "pYo") for _ in range(4)]
        pB = [kb.ps([128, 512], F32, "pB") for _ in range(2)]
        ptr = kb.ps([128, 512], BF16, "ptr")
        uT = kb.sb([128, 4, S], BF16, "uT")
        kb.dma("sp", uT[:], fmT.t[R_SU:R_SU + 512, :].rearrange("(c p) t -> p c t", p=128), reads=[fmT], writes=[uT], sembuf=uT)
        bp = [kb.sb([128, S], F32, f"bp{b}") for b in range(2)]
        zz = [kb.sb([128, S], F32, f"zz{b}") for b in range(2)]
        xb = [kb.sb([128, S], BF16, f"xb{b}") for b in range(2)]
        Etl = [kb.sb([128, 2, S], F32, "Etl") for _ in range(2)]
        tA = [kb.sb([128, 512], F32, "tA") for _ in range(2)]
        tB = [kb.sb([128, 512], F32, "tB") for _ in range(2)]
        gT = kb.sb([128, 4, S], BF16, "gT")
        wss = [kb.sb([128, 1024], F32, "ws") for _ in range(2)]
        wg = kb.sb([128, 4, 1024], BF16, "wg")
        wv = cx.s5_w_glu.t[l].rearrange("(c p) n -> p c n", p=128)
        for c in range(4):
            kb.dma("sp", wss[c % 2][:], wv[:, c, :], reads=[cx.s5_w_glu], writes=[wss[c % 2]], sembuf=wss[c % 2])
            kb.op("pool", lambda e: e.tensor_copy(out=wg[:, c, :], in_=wss[c % 2][:]), reads=[wss[c % 2]], writes=[wg])
        sgin = [kb.sb([128, 512], BF16, "sgin") for _ in range(2)]
        yb = [kb.sb([128, 512], F32, "yb") for _ in range(2)]
        q1 = [kb.sb([128, 512], F32, "q1") for _ in range(2)]
        Rm = s5["R"]
        nt = 0
        for st in range(16):
            cc = st // 4
            Et = Etl[st % 2]
            kb.dma("sp", Et[:], cx.Etab.t[st].rearrange("a p t -> p a t"), reads=[cx.Etab], writes=[Et], sembuf=Et)
            for tb in range(4):
                cs = slice(tb * 512, (tb + 1) * 512)
                a_, b_ = tA[nt % 2], tB[nt % 2]
                nt += 1
                kb.mm(pB[0][:], [(BT[:, st, 0, :], uT[:, cc, cs])], reads=[BT, uT], writes=[pB[0]])
                kb.mm(pB[1][:], [(BT[:, st, 1, :], uT[:, cc, cs])], reads=[BT, uT], writes=[pB[1]])
                kb.op("act", lambda e: e.activation(out=zz[0][:, cs], in_=pB[0][:], func=AF.Copy), reads=[pB[0]], writes=[zz[0]])
                kb.op("act", lambda e: e.activation(out=zz[1][:, cs], in_=pB[1][:], func=AF.Copy), reads=[pB[1]], writes=[zz[1]])
                kb.op("dve", lambda e: e.tensor_tensor(out=a_[:], in0=zz[0][:, cs], in1=Et[:, 0, cs], op=ALU.mult), reads=[zz[0], Et], writes=[a_])
                kb.op("pool", lambda e: e.tensor_tensor(out=b_[:], in0=zz[1][:, cs], in1=Et[:, 1, cs], op=ALU.mult), reads=[zz[1], Et], writes=[b_])
                kb.op("dve", lambda e: e.tensor_tensor(out=bp[0][:, cs], in0=a_[:], in1=b_[:], op=ALU.add), reads=[a_, b_], writes=[bp[0]])
                kb.op("dve", lambda e: e.tensor_tensor(out=a_[:], in0=zz[1][:, cs], in1=Et[:, 0, cs], op=ALU.mult), reads=[zz[1], Et], writes=[a_])
                kb.op("pool", lambda e: e.tensor_tensor(out=b_[:], in0=zz[0][:, cs], in1=Et[:, 1, cs], op=ALU.mult), reads=[zz[0], Et], writes=[b_])
                kb.op("dve", lambda e: e.tensor_tensor(out=bp[1][:, cs], in0=a_[:], in1=b_[:], op=ALU.subtract), reads=[a_, b_], writes=[bp[1]])
            rb = Rm[:, st:st + 1].broadcast_to([128, S])
            for ri in range(2):
                kb.op("dve", lambda e: e.tensor_tensor_scan(out=zz[ri][:], data0=rb, data1=bp[ri][:], initial=0.0, op0=ALU.mult, op1=ALU.add),
                      reads=[Rm, bp[ri], zz[ri]], writes=[zz[ri]])
            for tb in range(4):
                cs = slice(tb * 512, (tb + 1) * 512)
                a_, b_ = tA[nt % 2], tB[nt % 2]
                nt += 1
                kb.op("dve", lambda e: e.tensor_tensor(out=a_[:], in0=zz[0][:, cs], in1=Et[:, 0, cs], op=ALU.mult), reads=[zz[0], Et], writes=[a_])
                kb.op("pool", lambda e: e.tensor_tensor(out=b_[:], in0=zz[1][:, cs], in1=Et[:, 1, cs], op=ALU.mult), reads=[zz[1], Et], writes=[b_])
                kb.op("dve", lambda e: e.tensor_tensor(out=xb[0][:, cs], in0=a_[:], in1=b_[:], op=ALU.subtract), reads=[a_, b_], writes=[xb[0]])
                kb.op("dve", lambda e: e.tensor_tensor(out=a_[:], in0=zz[0][:, cs], in1=Et[:, 1, cs], op=ALU.mult), reads=[zz[0], Et], writes=[a_])
                kb.op("pool", lambda e: e.tensor_tensor(out=b_[:], in0=zz[1][:, cs], in1=Et[:, 0, cs], op=ALU.mult), reads=[zz[1], Et], writes=[b_])
                kb.op("dve", lambda e: e.tensor_tensor(out=xb[1][:, cs], in0=a_[:], in1=b_[:], op=ALU.add), reads=[a_, b_], writes=[xb[1]])
            for tb in range(4):
                cs = slice(tb * 512, (tb + 1) * 512)
                py = pYo[tb]
                first, last = (st % 4 == 0), (st % 4 == 3)
                kb._wait("pe", kb._deps([CP, xb[0], xb[1]], [py] if first else []))
                kb.nc.tensor.matmul(py[:], CP[:, st, 0, :], xb[0][:, cs], start=first, stop=False)
                ins = kb.nc.tensor.matmul(py[:], CP[:, st, 1, :], xb[1][:, cs], start=False, stop=last)
                sname = kb.sem["pe"]
                kb.semval[sname] += 1
                ins.then_inc(kb.semobj[sname], 1)
                tok = (sname, kb.semval[sname])
                xb[0].rd.append(tok)
                xb[1].rd.append(tok)
                if first:
                    py.rd = []
                py.lw = [tok]
            if st % 4 == 3:
                for tb in range(4):
                    cs = slice(tb * 512, (tb + 1) * 512)
                    y, q = yb[tb % 2], q1[tb % 2]
                    kb.op("dve", lambda e: e.scalar_tensor_tensor(out=y[:], in0=uT[:, cc, cs], scalar=dsk[:, cc:cc + 1], in1=pYo[tb][:], op0=ALU.mult, op1=ALU.add),
                          reads=[uT, dsk, pYo[tb]], writes=[y])
                    kb.op("dve", lambda e: e.tensor_tensor(out=q[:], in0=y[:], in1=y[:], op=ALU.mult), reads=[y], writes=[q])
                    kb.op("dve", lambda e: e.tensor_scalar(out=q[:], in0=q[:], scalar1=0.044715, scalar2=1.0, op0=ALU.mult, op1=ALU.add), reads=[q], writes=[q])
                    kb.op("dve", lambda e: e.tensor_tensor(out=q[:], in0=q[:], in1=y[:], op=ALU.mult), reads=[q, y], writes=[q])
                    kb.op("act", lambda e: e.activation(out=q[:], in_=q[:], func=AF.Sigmoid, scale=1.5957691216057308), reads=[q], writes=[q])
                    kb.op("dve", lambda e: e.tensor_tensor(out=gT[:, cc, cs], in0=q[:], in1=y[:], op=ALU.mult), reads=[q, y], writes=[gT])
        oTs = kb.sb([128, 4, S], BF16, "oTs")
        sgs = [kb.sb([128, 512], F32, "sgs") for _ in range(2)]
        sig = [kb.sb([128, 512], F32, "sig") for _ in range(2)]
        od = [kb.sb([128, 512], F32, "od") for _ in range(2)]
        om = [kb.sb([128, 512], BF16, "om") for _ in range(2)]
        for t in range(NT):
            ts_ = slice(t * 128, (t + 1) * 128)
            k = t % 2
            kb.mm(pB[0][:], [(gT[:, c, ts_], wg[:, c, 0:512]) for c in range(4)], reads=[gT, wg], writes=[pB[0]])
            kb.mm(pB[1][:], [(gT[:, c, ts_], wg[:, c, 512:1024]) for c in range(4)], reads=[gT, wg], writes=[pB[1]])
            kb.op("act", lambda e: e.activation(out=sig[k][:], in_=pB[1][:], func=AF.Sigmoid), reads=[pB[1]], writes=[sig[k]])
            kb.dma("sp", sgin[k][:], tm.t[ts_, C_SG:C_SG + 512], reads=[tm], writes=[sgin[k]], sembuf=sgin[k])
            kb.op("act", lambda e: e.activation(out=sgs[k][:], in_=sgin[k][:], func=AF.Silu), reads=[sgin[k]], writes=[sgs[k]])
            kb.op("dve", lambda e: e.tensor_tensor(out=od[k][:], in0=pB[0][:], in1=sig[k][:], op=ALU.mult), reads=[pB[0], sig[k]], writes=[od[k]])
            kb.op("dve", lambda e: e.tensor_tensor(out=om[k][:], in0=od[k][:], in1=sgs[k][:], op=ALU.mult), reads=[od[k], sgs[k]], writes=[om[k]])
            for c in range(4):
                kb.op("pe", lambda e: e.transpose(out=ptr[:, c * 128:(c + 1) * 128], in_=om[k][:, c * 128:(c + 1) * 128], identity=ident[:]),
                      reads=[om[k], ident], writes=[ptr])
            kb.op("act", lambda e: e.activation(out=oTs[:, :, ts_], in_=ptr[:].rearrange("p (a b) -> p a b", a=4), func=AF.Copy), reads=[ptr], writes=[oTs])
        oT = cx.oT[s]
        kb.dma("sp", oT.t[1536:2048, :].rearrange("(c p) t -> p c t", p=128), oTs[:], reads=[oTs], writes=[oT], sembuf=oTs)


def load_w_bf16(kb, dst, src_ap2d, srcbuf, stg):
    v = src_ap2d.rearrange("(c p) n -> p c n", p=128)
    for c in range(16):
        st = stg[c % 2]
        kb.dma("sp", st[:], v[:, c, :], reads=[srcbuf], writes=[st], sembuf=st)
        kb.op("pool", lambda e: e.tensor_copy(out=dst[:, c, :], in_=st[:]), reads=[st], writes=[dst])


def phase_out(kb, cx, l):
    mTs = cx.mTs
    with kb.phase():
        ident = load_const(kb, cx, "ident_bf", [128, 128], BF16)
        Wb = kb.sb([128, 16, D], BF16, "Wb")
        stg = [kb.sb([128, D], F32, "stg") for _ in range(2)]
        load_w_bf16(kb, Wb, cx.w_branch.t[l].rearrange("n k d -> (n k) d"), cx.w_branch, stg)
        oTt = [kb.sb([128, 16, 128], BF16, "oTt") for _ in range(2)]
        mgt = [kb.sb([128, 4 * D], BF16, "mgt") for _ in range(2)]
        sgm = [kb.sb([128, D], F32, "sgm") for _ in range(2)]
        mergedb = [[kb.sb([128, 512], F32, "merged") for _ in range(4)] for _ in range(2)]
        tmp = [kb.sb([128, 512], F32, "tmp") for _ in range(4)]
        mbf = kb.sb([128, D], BF16, "mbf")
        mst = [kb.sb([128, 16, 128], BF16, "mst") for _ in range(2)]
        pm = [kb.ps([128, 512], F32, "pm") for _ in range(4)]
        ptr = [kb.ps([128, 512], BF16, "ptr") for _ in range(2)]
        n_pm = 0
        n_sg = 0
        it = 0
        for s in range(NSEQ):
            oT, tm = cx.oT[s], cx.tm[s]
            for t in range(NT):
                k = it % 2
                it += 1
                ts_ = slice(t * 128, (t + 1) * 128)
                kb.dma("sp", oTt[k][:], oT.t[:, ts_].rearrange("(c p) t -> p c t", p=128), reads=[oT], writes=[oTt[k]], sembuf=oTt[k])
                kb.dma("sp", mgt[k][:], tm.t[ts_, C_MG:C_MG + 4 * D], reads=[tm], writes=[mgt[k]], sembuf=mgt[k])
                for n in range(4):
                    sg = sgm[n_sg % 2]
                    n_sg += 1
                    kb.op("act", lambda e: e.activation(out=sg[:], in_=mgt[k][:, n * D:(n + 1) * D], func=AF.Sigmoid), reads=[mgt[k]], writes=[sg])
                    for cb in range(4):
                        cs = slice(cb * 512, (cb + 1) * 512)
                        p = pm[n_pm % 4]
                        tp = tmp[n_pm % 4]
                        n_pm += 1
                        kb.mm(p[:], [(oTt[k][:, n * 4 + c, :], Wb[:, n * 4 + c, cs]) for c in range(4)], reads=[oTt[k], Wb], writes=[p])
                        mg_ = mergedb[k][cb]
                        if n == 0:
                            kb.op("dve", lambda e: e.tensor_tensor(out=mg_[:], in0=p[:], in1=sg[:, cs], op=ALU.mult), reads=[p, sg], writes=[mg_])
                        else:
                            kb.op("dve", lambda e: e.tensor_tensor(out=tp[:], in0=p[:], in1=sg[:, cs], op=ALU.mult), reads=[p, sg], writes=[tp])
                            kb.op("pool", lambda e: e.tensor_tensor(out=mg_[:], in0=mg_[:], in1=tp[:], op=ALU.add), reads=[mg_, tp], writes=[mg_])
                for cb in range(4):
                    kb.op("act", lambda e: e.activation(out=mbf[:, cb * 512:(cb + 1) * 512], in_=mergedb[k][cb][:], func=AF.Copy),
                          reads=[mergedb[k][cb]], writes=[mbf])
                ms = mst[k]
                for g4 in range(4):
                    pt = ptr[g4 % 2]
                    for j in range(4):
                        c = g4 * 4 + j
                        kb.op("pe", lambda e: e.transpose(out=pt[:, j * 128:(j + 1) * 128], in_=mbf[:, c * 128:(c + 1) * 128], identity=ident[:]),
                              reads=[mbf, ident], writes=[pt])
                    kb.op("act", lambda e: e.activation(out=ms[:, g4 * 4:(g4 + 1) * 4, :], in_=pt[:].rearrange("p (a b) -> p a b", a=4), func=AF.Copy),
                          reads=[pt], writes=[ms])
                c0 = s * S + t * 128
                kb.dma("act", mTs.t[:, c0:c0 + 128].rearrange("(c p) t -> p c t", p=128), ms[:], reads=[ms], writes=[mTs], sembuf=ms)
    with kb.phase():
        Wo = kb.sb([128, 16, D], BF16, "Wo")
        stg = [kb.sb([128, D], F32, "stg") for _ in range(2)]
        load_w_bf16(kb, Wo, cx.w_out.t[l], cx.w_out, stg)
        gp = kb.sb([128, D], F32, "gp")
        kb.dma("sp", gp[:], cx.post_norm_g.t[l:l + 1, :].broadcast_to([128, D]), reads=[cx.post_norm_g], writes=[gp], sembuf=gp)
        mT = [kb.sb([128, 16, 128], BF16, "mT") for _ in range(2)]
        ht = [kb.sb([128, D], F32, "ht") for _ in range(2)]
        hn = [kb.sb([128, D], F32, "hn") for _ in range(2)]
        t1 = [kb.sb([128, 512], F32, "t1") for _ in range(2)]
        jk = kb.sb([128, 512], BF16, "jk")
        sq = kb.sb([128, 2 * NT * NSEQ, 8], F32, "sq")
        py = [kb.ps([128, 512], F32, "py") for _ in range(8)]
        hin, hout = cx.h_in, cx.h_out
        it = 0
        for s in range(NSEQ):
            for t in range(NT):
                k = it % 2
                c0 = s * S + t * 128
                kb.dma("sp", mT[k][:], mTs.t[:, c0:c0 + 128].rearrange("(c p) t -> p c t", p=128), reads=[mTs], writes=[mT[k]], sembuf=mT[k])
                kb.dma("sp", ht[k][:], hin.t[c0:c0 + 128, :], reads=[hin], writes=[ht[k]], sembuf=ht[k])
                pys = py[k * 4:(k + 1) * 4]
                for cb in range(4):
                    cs = slice(cb * 512, (cb + 1) * 512)
                    kb.mm(pys[cb][:], [(mT[k][:, c, :], Wo[:, c, cs]) for c in range(16)], reads=[mT[k], Wo], writes=[pys[cb]])
                    kb.op("act", lambda e: e.activation(out=jk[:], in_=pys[cb][:], func=AF.Square, accum_out=sq[:, it, cb:cb + 1]),
                          reads=[pys[cb]], writes=[jk, sq])
                kb.op("dve", lambda e: e.tensor_reduce(out=sq[:, it, 4:5], in_=sq[:, it, 0:4], axis=AX.X, op=ALU.add), reads=[sq], writes=[sq])
                rstd_from_ss(kb, sq[:, it, 4:5], sq[:, it, 5:6], D, sq, sq)
                for cb in range(4):
                    cs = slice(cb * 512, (cb + 1) * 512)
                    tt = t1[cb % 2]
                    kb.op("dve", lambda e: e.scalar_tensor_tensor(out=tt[:], in0=pys[cb][:], scalar=sq[:, it, 5:6], in1=gp[:, cs], op0=ALU.mult, op1=ALU.mult),
                          reads=[pys[cb], sq, gp], writes=[tt])
                    kb.op("pool", lambda e: e.tensor_tensor(out=hn[k][:, cs], in0=tt[:], in1=ht[k][:, cs], op=ALU.add), reads=[tt, ht[k]], writes=[hn[k]])
                kb.dma("act", hout.t[c0:c0 + 128, :], hn[k][:], reads=[hn[k]], writes=[hout], sembuf=hn[k])
                it += 1
```

```python
import contextlib
import math
import numpy as np
import ml_dtypes
import concourse.bass as bass
import concourse.mybir as mybir
from concourse.bass_utils import run_bass_kernel_spmd

F32 = mybir.dt.float32
BF16 = mybir.dt.bfloat16
AF = mybir.ActivationFunctionType
ALU = mybir.AluOpType
AX = mybir.AxisListType

D = 2048
S = 2048
NT = S // 128
DEPTH = 4
NSEQ = 2
INC = 14860
EPS = 1e-6
O_FQ, O_FK, O_FV, O_FF, O_FG = 0, 512, 1024, 1536, 1540
O_DQ, O_DK, O_DV, O_DG = 2052, 2564, 3076, 3588
O_SZ, O_SX, O_SDT, O_SU, O_SG, O_MG = 4100, 4612, 5636, 5644, 6156, 6668
FM_SEGS = [(O_FQ, 512), (O_FK, 512), (O_DQ, 512), (O_DK, 512), (O_SX, 1024), (O_SU, 512)]
R_FQ, R_FK, R_DQ, R_DK, R_SX, R_SU = 0, 512, 1024, 1536, 2048, 3072
FM_ROWS = 3584
TM_SEGS = [(O_FV, 512), (O_FG, 512), (O_DV, 512), (O_DG, 512), (O_SZ, 512), (O_SG, 512), (O_MG, 8192)]
C_FV, C_FG, C_DV, C_DG, C_SZ, C_SG, C_MG = 0, 512, 1024, 1536, 2048, 2560, 3072
TM_COLS = 11264


class Buf:
    def __init__(self, kb, name, t):
        self.kb = kb
        self.name = name
        self.t = t
        self.lw = []
        self.rd = []
        self.dsem = None

    def __getitem__(self, k):
        return self.t[k]


class KB:
    ENG = ("pe", "act", "dve", "pool", "sp")

    def __init__(self, nc, es):
        self.nc = nc
        self.es = es
        self.e = {"pe": nc.tensor, "act": nc.scalar, "dve": nc.vector, "pool": nc.gpsimd, "sp": nc.sync}
        self.sem = {}
        self.cnt = {}
        self.seen = {k: {} for k in self.ENG}
        self.semval = {}
        self.semobj = {}
        self.epoch = 0
        self.new_epoch()
        self.dma_free = []
        self.dma_all = []
        for i in range(64):
            s = es.enter_context(nc.semaphore(f"dq{i}"))
            nm = f"dq{i}"
            self.semobj[nm] = s
            self.semval[nm] = 0
            self.dma_free.append(nm)
            self.dma_all.append(nm)
        self.bar = es.enter_context(nc.semaphore("bar"))
        self.barv = 0
        self.phase_bufs = []
        self.uid = 0

    def new_epoch(self):
        self.epoch += 1
        for k in self.ENG:
            nm = f"e{self.epoch}_{k}"
            s = self.es.enter_context(self.nc.semaphore(nm))
            self.semobj[nm] = s
            self.semval[nm] = 0
            self.sem[k] = nm

    def sb(self, shape, dt, name=None):
        self.uid += 1
        name = f"{name or 'sb'}_{self.uid}"
        t = self.pes.enter_context(self.nc.sbuf_tensor(name, list(shape), dt))
        b = Buf(self, name, t)
        self.phase_bufs.append(b)
        return b

    def ps(self, shape, dt=F32, name=None):
        self.uid += 1
        name = f"{name or 'ps'}_{self.uid}"
        t = self.pes.enter_context(self.nc.psum_tensor(name, list(shape), dt))
        b = Buf(self, name, t)
        self.phase_bufs.append(b)
        return b

    def dram(self, name, t):
        return Buf(self, name, t)

    @contextlib.contextmanager
    def phase(self):
        self.pes = contextlib.ExitStack()
        self.phase_bufs = []
        try:
            yield
            self.barrier()
        finally:
            for b in self.phase_bufs:
                if b.dsem is not None:
                    self.dma_free.append(b.dsem)
                    b.dsem = None
            self.pes.close()
            self.pes = None

    def _wait(self, eng, toks):
        need = {}
        for (s, v) in toks:
            if need.get(s, 0) < v:
                need[s] = v
        for s, v in need.items():
            if self.seen[eng].get(s, 0) >= v:
                continue
            self.e[eng].wait_ge(self.semobj[s], v)
            self.seen[eng][s] = v

    def _deps(self, reads, writes):
        toks = []
        for b in reads:
            toks += b.lw
        for b in writes:
            toks += b.lw
            toks += b.rd
        return toks

    def op(self, eng, fn, reads=(), writes=()):
        self._wait(eng, self._deps(reads, writes))
        ins = fn(self.e[eng])
        s = self.sem[eng]
        self.semval[s] += 1
        ins.then_inc(self.semobj[s], 1)
        tok = (s, self.semval[s])
        self._record(tok, reads, writes)
        return tok

    def _record(self, tok, reads, writes):
        for b in reads:
            b.rd.append(tok)
            if len(b.rd) > 64:
                b.rd = self._compact(b.rd)
        for b in writes:
            b.lw = [tok]
            b.rd = []

    @staticmethod
    def _compact(toks):
        need = {}
        for (s, v) in toks:
            if need.get(s, 0) < v:
                need[s] = v
        return list(need.items())

    def mm(self, out_ap, pairs, reads, writes, start=True, stop=True):
        eng = "pe"
        self._wait(eng, self._deps(reads, writes))
        n = len(pairs)
        ins = None
        for i, (l, r) in enumerate(pairs):
            ins = self.nc.tensor.matmul(out_ap, l, r, start=(start and i == 0), stop=(stop and i == n - 1))
        s = self.sem[eng]
        self.semval[s] += 1
        ins.then_inc(self.semobj[s], 1)
        tok = (s, self.semval[s])
        self._record(tok, reads, writes)
        return tok

    def dma(self, q, out_ap, in_ap, reads=(), writes=(), sembuf=None, **kw):
        b = sembuf
        if b.dsem is None:
            b.dsem = self.dma_free.pop(0)
        self._wait(q, self._deps(reads, writes))
        ins = self.e[q].dma_start(out=out_ap, in_=in_ap, **kw)
        s = b.dsem
        self.semval[s] += 16
        ins.then_inc(self.semobj[s], 16)
        tok = (s, self.semval[s])
        for r in reads:
            r.rd.append(tok)
        for w in writes:
            if w.lw and all(t[0] == s for t in w.lw):
                w.lw = [tok]
            else:
                w.lw = [tok]
            w.rd = []
        return tok

    def barrier(self):
        toks = [(s, v) for s, v in self.semval.items() if v > 0 and not s.startswith("bar")]
        cur = set(self.sem.values()) | set(self.dma_all)
        toks = [(s, v) for (s, v) in toks if s in cur]
        self._wait("sp", toks)
        self.barv += 1
        self.nc.sync.sem_inc(self.bar, 1)
        for k in self.ENG:
            if k == "sp":
                continue
            self.e[k].wait_ge(self.bar, self.barv)
            for (s, v) in toks:
                self.seen[k][s] = max(self.seen[k].get(s, 0), v)


def bcast_ap(t_ap, shape):
    return t_ap.broadcast_to(list(shape))


class Ctx:
    pass


def load_const(kb, cx, name, shape, dt):
    b = kb.sb(shape, dt, name=name)
    kb.dma("sp", b[:], cx.cst[name].t[:], reads=[cx.cst[name]], writes=[b], sembuf=b)
    return b


def phase_A(kb, cx, l, s):
    nc = kb.nc
    with kb.phase():
        ident = load_const(kb, cx, "ident_bf", [128, 128], BF16)
        xnT = kb.sb([128, 16, S], BF16, "xnT")
        gb = kb.sb([128, D], F32, "gb")
        kb.dma("sp", gb[:], cx.pre_g.t[l:l + 1, :].broadcast_to([128, D]), reads=[cx.pre_g], writes=[gb], sembuf=gb)
        hts = [kb.sb([128, D], F32, "ht") for _ in range(2)]
        junk = kb.sb([128, D], BF16, "junk")
        xss = [kb.sb([128, D], BF16, "xs") for _ in range(2)]
        ss = kb.sb([128, NT], F32, "ss")
        rs = kb.sb([128, NT], F32, "rs")
        ptr = [kb.ps([128, 512], BF16, "ptr") for _ in range(2)]
        pmm = [kb.ps([128, 512], F32, "pmm") for _ in range(4)]
        hin = cx.h_in
        for t in range(NT if cx.stop >= 2 else 0):
            ht = hts[t % 2]
            xs = xss[t % 2]
            r0 = s * S + t * 128
            kb.dma("sp", ht[:], hin.t[r0:r0 + 128, :], reads=[hin], writes=[ht], sembuf=ht)
            kb.op("act", lambda e: e.activation(out=junk[:], in_=ht[:], func=AF.Square, accum_out=ss[:, t:t + 1]),
                  reads=[ht], writes=[junk, ss])
            kb.op("dve", lambda e: e.tensor_scalar(out=rs[:, t:t + 1], in0=ss[:, t:t + 1], scalar1=1.0 / D, scalar2=EPS,
                                                   op0=ALU.mult, op1=ALU.add), reads=[ss], writes=[rs])
            kb.op("act", lambda e: e.activation(out=rs[:, t:t + 1], in_=rs[:, t:t + 1], func=AF.Sqrt), reads=[rs], writes=[rs])
            kb.op("dve", lambda e: e.reciprocal(out=rs[:, t:t + 1], in_=rs[:, t:t + 1]), reads=[rs], writes=[rs])
            kb.op("dve", lambda e: e.scalar_tensor_tensor(out=xs[:], in0=ht[:], scalar=rs[:, t:t + 1], in1=gb[:],
                                                          op0=ALU.mult, op1=ALU.mult), reads=[ht, rs, gb], writes=[xs])
            for g4 in range(4):
                pt = ptr[g4 % 2]
                for j in range(4):
                    c = g4 * 4 + j
                    kb.op("pe", lambda e: e.transpose(out=pt[:, j * 128:(j + 1) * 128], in_=xs[:, c * 128:(c + 1) * 128],
                                                      identity=ident[:]), reads=[xs, ident], writes=[pt])
                dst = xnT[:, g4 * 4:(g4 + 1) * 4, t * 128:(t + 1) * 128]
                src = pt[:].rearrange("p (a b) -> p a b", a=4)
                if g4 % 2 == 0:
                    kb.op("dve", lambda e: e.tensor_copy(out=dst, in_=src), reads=[pt], writes=[xnT])
                else:
                    kb.op("act", lambda e: e.activation(out=dst, in_=src, func=AF.Copy), reads=[pt], writes=[xnT])

        W = cx.w_in
        wst = [kb.sb([128, 4, 512], F32, "wst") for _ in range(2)]
        wbfs = [kb.sb([128, 16, 512], BF16, "wbf") for _ in range(2)]
        stfm = [kb.sb([128, S], BF16, "stfm") for _ in range(2)]
        sttm = [kb.sb([128, 512], BF16, "sttm") for _ in range(3)]
        blocks = []
        for (wc, n), r in zip(FM_SEGS, [R_FQ, R_FK, R_DQ, R_DK, R_SX, R_SU]):
            for k in range(n // 512):
                blocks.append(("fm", wc + k * 512, r + k * 512))
        col = 0
        for (wc, n) in TM_SEGS:
            for k in range(n // 512):
                blocks.append(("tm", wc + k * 512, col))
                col += 512
        if cx.dbg_nblocks is not None:
            blocks = blocks[:cx.dbg_nblocks[0]] + [b for b in blocks if b[0] == "tm"][:cx.dbg_nblocks[1]]
        fmT, tm, sm = cx.fmT[s], cx.tm[s], cx.sm[s]
        state = {"k": 0, "pm": 0, "sf": 0, "st": 0}

        def load_block(i):
            kind, wc, dst = blocks[i]
            wbf = wbfs[i % 2]
            for piece in range(4):
                st = wst[state["k"] % 2]
                state["k"] += 1
                src = W.t[l, piece * 512:(piece + 1) * 512, wc:wc + 512].rearrange("(c p) n -> p c n", p=128)
                kb.dma("sp", st[:], src, reads=[W], writes=[st], sembuf=st)
                kb.op("pool", lambda e: e.tensor_copy(out=wbf[:, piece * 4:(piece + 1) * 4, :], in_=st[:]),
                      reads=[st], writes=[wbf])

        if cx.stop < 3:
            return
        wsm_f = kb.sb([128, 16, 12], F32, "wsmf")
        wsm = kb.sb([128, 16, 12], BF16, "wsm")
        smst = kb.sb([128, NT, 12], F32, "smst")
        for (wc, n, o) in ((O_FF, 4, 0), (O_SDT, 8, 4)):
            src = W.t[l, :, wc:wc + n].rearrange("(c p) n -> p c n", p=128)
            kb.dma("sp", wsm_f[:, :, o:o + n], src, reads=[W], writes=[wsm_f], sembuf=wsm_f, allow_slow_non_contiguous=True)
        kb.op("pool", lambda e: e.tensor_copy(out=wsm[:], in_=wsm_f[:]), reads=[wsm_f], writes=[wsm])
        for t in range(NT):
            pm = pmm[state["pm"] % 4]
            state["pm"] += 1
            kb.mm(pm[:, 0:12], [(xnT[:, c, t * 128:(t + 1) * 128], wsm[:, c, :]) for c in range(16)],
                  reads=[xnT, wsm], writes=[pm])
            kb.op("act", lambda e: e.activation(out=smst[:, t, :], in_=pm[:, 0:12], func=AF.Copy), reads=[pm], writes=[smst])
        kb.dma("act", sm.t[:, 0:12].rearrange("(t p) n -> p t n", p=128), smst[:], reads=[smst], writes=[sm], sembuf=smst,
               allow_slow_non_contiguous=True)

        if cx.stop < 4:
            return
        load_block(0)
        for i, (kind, wc, dst) in enumerate(blocks):
            if i + 1 < len(blocks):
                load_block(i + 1)
            wbf = wbfs[i % 2]
            if kind == "fm":
                for j in range(4):
                    sf = stfm[state["sf"] % 2]
                    state["sf"] += 1
                    for tb in range(4):
                        pm = pmm[state["pm"] % 4]
                        state["pm"] += 1
                        kb.mm(pm[:], [(wbf[:, c, j * 128:(j + 1) * 128], xnT[:, c, tb * 512:(tb + 1) * 512]) for c in range(16)],
                              reads=[xnT, wbf], writes=[pm])
                        kb.op("act", lambda e: e.activation(out=sf[:, tb * 512:(tb + 1) * 512], in_=pm[:], func=AF.Copy),
                              reads=[pm], writes=[sf])
                    kb.dma("act", fmT.t[dst + j * 128:dst + (j + 1) * 128, :], sf[:], reads=[sf], writes=[fmT], sembuf=sf)
            else:
                for t in range(NT):
                    pm = pmm[state["pm"] % 4]
                    state["pm"] += 1
                    stt = sttm[state["st"] % 3]
                    state["st"] += 1
                    kb.mm(pm[:], [(xnT[:, c, t * 128:(t + 1) * 128], wbf[:, c, :]) for c in range(16)],
                          reads=[xnT, wbf], writes=[pm])
                    if t % 2 == 0:
                        kb.op("act", lambda e: e.activation(out=stt[:], in_=pm[:], func=AF.Copy), reads=[pm], writes=[stt])
                    else:
                        kb.op("dve", lambda e: e.tensor_copy(out=stt[:], in_=pm[:]), reads=[pm], writes=[stt])
                    kb.dma("act", tm.t[t * 128:(t + 1) * 128, dst:dst + 512], stt[:], reads=[stt], writes=[tm], sembuf=stt)


BF = ml_dtypes.bfloat16


def make_consts():
    c = {}
    c["ident_bf"] = np.eye(128, dtype=np.float32).astype(BF)
    c["ident_f"] = np.eye(128, dtype=np.float32)
    k = np.arange(128)[:, None]
    q = np.arange(128)[None, :]
    c["mask_le_bf"] = (k <= q).astype(np.float32).astype(BF)
    c["tri_le_f"] = (k <= q).astype(np.float32)
    c["gt_f"] = (k > q).astype(np.float32)
    c["negmask_f"] = np.where(q < k, -30000.0, 0.0).astype(np.float32)
    c["ones_f"] = np.ones((128, 128), np.float32)
    d = np.arange(128) % 32
    inv = (10000.0 ** (-d.astype(np.float64) / 32.0))
    ang = inv[:, None] * np.arange(S, dtype=np.float64)[None, :]
    c["rope_cos"] = np.cos(ang).astype(np.float32)
    c["rope_sin"] = np.sin(ang).astype(np.float32)
    RT = np.zeros((128, 128), np.float32)
    for p in range(128):
        dd = p % 64
        if dd < 32:
            RT[p + 32, p] = -1.0
        else:
            RT[p - 32, p] = 1.0
    c["rope_rt"] = RT.astype(BF)
    gm = np.zeros((128, 8), np.float32)
    for r in range(128):
        gm[r, r // 16] = 1.0
    c["gmask"] = gm
    return c


CONST_DT = {"ident_bf": BF16, "mask_le_bf": BF16, "rope_rt": BF16}

PARAMS = ["w_in", "b_fox_f", "pre_norm_g", "post_norm_g", "diff_lambda", "diff_subln_g", "ssd_conv_w", "ssd_conv_b",
          "ssd_dt_bias", "ssd_a_log", "ssd_d", "ssd_norm_g", "s5_a_re", "s5_a_im", "s5_b_re", "s5_b_im", "s5_c_re",
          "s5_c_im", "s5_d", "s5_log_dt", "s5_w_glu", "w_branch", "w_out"]


def build(shapes, layers=range(DEPTH), phases="ABCDEF", dbg=None, dbg_nblocks=None, stop=99, stop2=0):
    dbg = dbg or {}
    nc = bass.Bass("TRN2", target_bir_lowering=False)
    es = contextlib.ExitStack()
    kb = KB(nc, es)
    cx = Ctx()
    cx.dbg_nblocks = dbg_nblocks
    cx.stop = stop
    cx.stop2 = stop2
    x = nc.dram_tensor("x", [NSEQ * S, D], F32, kind="ExternalInput")
    out = nc.dram_tensor("out", [NSEQ * S, D], F32, kind="ExternalOutput")
    cx.cst = {}
    for k, v in make_consts().items():
        cx.cst[k] = kb.dram(k, nc.dram_tensor("c_" + k, list(v.shape), CONST_DT.get(k, F32), kind="ExternalInput"))
    for p in PARAMS:
        setattr(cx, p, kb.dram(p, nc.dram_tensor(p, list(shapes[p]), F32, kind="ExternalInput")))
    cx.pre_g = cx.pre_norm_g

    def scratch(name, shape, dt):
        kind = {"in": "ExternalInput", "out": "ExternalOutput"}.get(dbg.get(name), "Internal")
        return kb.dram(name, nc.dram_tensor(name, shape, dt, kind=kind))

    cx.fmT = [scratch(f"fmT{s}", [FM_ROWS, S], BF16) for s in range(NSEQ)]
    cx.tm = [scratch(f"tm{s}", [S, TM_COLS], BF16) for s in range(NSEQ)]
    cx.sm = [scratch(f"sm{s}", [S, 16], F32) for s in range(NSEQ)]
    cx.oT = [scratch(f"oT{s}", [4 * 512, S], BF16) for s in range(NSEQ)]
    cx.mTs = scratch("mTs", [D, NSEQ * S], BF16)
    cx.Etab = scratch("Etab", [16, 2, 128, S], F32)
    hA = scratch("hA", [NSEQ * S, D], F32)
    hB = scratch("hB", [NSEQ * S, D], F32)
    xb = kb.dram("x", x)
    ob = kb.dram("out", out)
    layers = list(layers)
    if "E" in phases:
        s5_alloc(kb, cx)
    for li, l in enumerate(layers):
        cx.h_in = xb if li == 0 else (hA if li % 2 == 1 else hB)
        cx.h_out = ob if li == len(layers) - 1 else (hA if li % 2 == 0 else hB)
        cx.l = l
        if "E" in phases:
            phase_s5_prep(kb, cx, l)
        for s in range(NSEQ):
            if "A" in phases:
                phase_A(kb, cx, l, s)
            if "B" in phases:
                phase_fox(kb, cx, l, s)
            if "C" in phases:
                phase_diff(kb, cx, l, s)
            if "D" in phases:
                phase_ssd(kb, cx, l, s)
            if "E" in phases:
                phase_s5(kb, cx, l, s)
        if "F" in phases:
            phase_out(kb, cx, l)
        if li + 1 < len(layers):
            kb.new_epoch()
    es.close()
    return nc


_CACHE = {}


def kernel(**inputs):
    x = np.ascontiguousarray(inputs["x"], dtype=np.float32)
    shapes = {p: inputs[p].shape for p in PARAMS}
    key = "full"
    if key not in _CACHE:
        _CACHE[key] = build(shapes)
    nc = _CACHE[key]
    consts = make_consts()
    base = {p: np.ascontiguousarray(inputs[p], dtype=np.float32) for p in PARAMS}
    for k, v in consts.items():
        base["c_" + k] = v
    in_maps = []
    for c in range(8):
        m = dict(base)
        m["x"] = x[c * NSEQ:(c + 1) * NSEQ].reshape(NSEQ * S, D)
        in_maps.append(m)
    res = run_bass_kernel_spmd(nc, in_maps, core_ids=list(range(8)))
    outs = [np.asarray(r["out"]).reshape(NSEQ, S, D) for r in res.results]
    return np.concatenate(outs, axis=0).astype(np.float32)


def attn_core(kb, cx, maps, nheads, vaug, finalize, scale, pools):
    sc_ps, pt_sb, po_ps, maskT = pools
    st = {"sc": 0, "po": 0}
    for h in range(nheads):
        for i in range(NT):
            pos = []
            for mi, mp in enumerate(maps):
                po = po_ps[st["po"] % len(po_ps)]
                st["po"] += 1
                pos.append(po)
                groups = [list(range(j0, min(j0 + 4, i + 1))) for j0 in range(0, i + 1, 4)]
                pend = None
                for gi, js in enumerate(groups):
                    sc = sc_ps[st["sc"] % len(sc_ps)]
                    pt = pt_sb[st["sc"] % len(pt_sb)]
                    st["sc"] += 1
                    for jj, j in enumerate(js):
                        kb.mm(sc[:, jj * 128:(jj + 1) * 128], [(mp["kT"](h, j), mp["qT"](h, i))], reads=mp["reads"], writes=[sc])
                    if mp["bias"] is None:
                        n = len(js) * 128
                        kb.op("act", lambda e: e.activation(out=pt[:, 0:n], in_=sc[:, 0:n], func=AF.Exp, scale=scale),
                              reads=[sc], writes=[pt])
                    else:
                        for jj, j in enumerate(js):
                            kb.op("act", lambda e: e.activation(out=pt[:, jj * 128:(jj + 1) * 128], in_=sc[:, jj * 128:(jj + 1) * 128],
                                                                func=AF.Exp, scale=scale, bias=mp["bias"](h, i, j)),
                                  reads=[sc] + mp["bias_reads"], writes=[pt])
                    if js[-1] == i:
                        jj = len(js) - 1
                        kb.op("dve", lambda e: e.tensor_tensor(out=pt[:, jj * 128:(jj + 1) * 128], in0=pt[:, jj * 128:(jj + 1) * 128],
                                                               in1=maskT[:], op=ALU.mult), reads=[pt, maskT], writes=[pt])
                    if pend is not None:
                        _pv(kb, pend, po, vaug, h, i)
                    pend = (js, pt)
                _pv(kb, pend, po, vaug, h, i)
            finalize(h, i, pos)


def _pv(kb, pend, po, vaug, h, i):
    js, pt = pend
    for jj, j in enumerate(js):
        kb.nc.tensor
        kb._wait("pe", kb._deps([pt, vaug], [po] if j == 0 else []))
        ins = kb.nc.tensor.matmul(po[:, 0:129], pt[:, jj * 128:(jj + 1) * 128], vaug[:, j, h, :], start=(j == 0), stop=(j == i))
        s = kb.sem["pe"]
        kb.semval[s] += 1
        ins.then_inc(kb.semobj[s], 1)
        tok = (s, kb.semval[s])
        pt.rd.append(tok)
        vaug.rd.append(tok)
        if j == 0:
            po.rd = []
        po.lw = [tok]


def load_vaug(kb, cx, s, col0, name):
    vaug = kb.sb([128, NT, 4, 129], BF16, name)
    kb.op("pool", lambda e: e.memset(vaug[:, :, :, 128:129], 1.0), reads=[], writes=[vaug])
    tm = cx.tm[s]
    for t in range(NT):
        src = tm.t[t * 128:(t + 1) * 128, col0:col0 + 512].rearrange("p (h d) -> p h d", h=4)
        kb.dma("sp", vaug[:, t, :, 0:128], src, reads=[tm], writes=[vaug], sembuf=vaug)
    return vaug


def transpose_out(kb, cx, o_tm, ident, ptr, oTs, i):
    for c in range(4):
        kb.op("pe", lambda e: e.transpose(out=ptr[:, c * 128:(c + 1) * 128], in_=o_tm[:, c * 128:(c + 1) * 128], identity=ident[:]),
              reads=[o_tm, ident], writes=[ptr])
    kb.op("act", lambda e: e.activation(out=oTs[:, :, i * 128:(i + 1) * 128], in_=ptr[:].rearrange("p (a b) -> p a b", a=4), func=AF.Copy),
          reads=[ptr], writes=[oTs])


def phase_fox(kb, cx, l, s):
    nc = kb.nc
    with kb.phase():
        ident = load_const(kb, cx, "ident_bf", [128, 128], BF16)
        maskT = load_const(kb, cx, "mask_le_bf", [128, 128], BF16)
        tri = load_const(kb, cx, "tri_le_f", [128, 128], F32)
        ones = load_const(kb, cx, "ones_f", [128, 128], F32)
        fmT, tm, sm = cx.fmT[s], cx.tm[s], cx.sm[s]
        qT = kb.sb([128, 4, S], BF16, "qT")
        kT = kb.sb([128, 4, S], BF16, "kT")
        kb.dma("sp", qT[:], fmT.t[R_FQ:R_FQ + 512, :].rearrange("(h p) t -> p h t", p=128), reads=[fmT], writes=[qT], sembuf=qT)
        kb.dma("sp", kT[:], fmT.t[R_FK:R_FK + 512, :].rearrange("(h p) t -> p h t", p=128), reads=[fmT], writes=[kT], sembuf=kT)
        vaug = load_vaug(kb, cx, s, C_FV, "vaugf")
        ff = kb.sb([128, NT, 4], F32, "ff")
        kb.dma("sp", ff[:], sm.t[:, 0:4].rearrange("(t p) n -> p t n", p=128), reads=[sm], writes=[ff], sembuf=ff,
               allow_slow_non_contiguous=True)
        bb = kb.sb([128, 4], F32, "bb")
        kb.dma("sp", bb[:], cx.b_fox_f.t[l:l + 1, :].broadcast_to([128, 4]), reads=[cx.b_fox_f], writes=[bb], sembuf=bb)
        lf = kb.sb([128, NT, 4], F32, "lf")
        kb.op("dve", lambda e: e.tensor_tensor(out=lf[:], in0=ff[:], in1=bb[:].unsqueeze(1).broadcast_to([128, NT, 4]), op=ALU.add),
              reads=[ff, bb], writes=[lf])
        kb.op("act", lambda e: e.activation(out=lf[:], in_=lf[:], func=AF.Exp, scale=-1.0), reads=[lf], writes=[lf])
        kb.op("act", lambda e: e.activation(out=lf[:], in_=lf[:], func=AF.Ln, bias=1.0), reads=[lf], writes=[lf])
        sc_ps = [kb.ps([128, 512], F32, "sc") for _ in range(3)]
        pF, pT = sc_ps[0], sc_ps[1]
        lf2 = lf[:].rearrange("p t h -> p (t h)")
        kb.mm(pF[:, 0:64], [(tri[:], lf2)], reads=[tri, lf], writes=[pF])
        kb.mm(pT[:, 0:64], [(ones[:], lf2)], reads=[ones, lf], writes=[pT])
        Lloc = kb.sb([128, NT, 4], F32, "Lloc")
        tot = kb.sb([128, NT, 4], F32, "tot")
        Lend = kb.sb([128, NT, 4], F32, "Lend")
        Lc = kb.sb([128, NT, 4], F32, "Lc")
        kb.op("dve", lambda e: e.tensor_copy(out=Lloc[:].rearrange("p t h -> p (t h)"), in_=pF[:, 0:64]), reads=[pF], writes=[Lloc])
        kb.op("dve", lambda e: e.tensor_copy(out=tot[:].rearrange("p t h -> p (t h)"), in_=pT[:, 0:64]), reads=[pT], writes=[tot])
        kb.op("dve", lambda e: e.tensor_copy(out=Lend[:, 0, :], in_=tot[:, 0, :]), reads=[tot], writes=[Lend])
        for t in range(1, NT):
            kb.op("dve", lambda e: e.tensor_tensor(out=Lend[:, t, :], in0=Lend[:, t - 1, :], in1=tot[:, t, :], op=ALU.add),
                  reads=[Lend, tot], writes=[Lend])
        kb.op("dve", lambda e: e.tensor_tensor(out=Lc[:], in0=Lend[:], in1=tot[:], op=ALU.subtract), reads=[Lend, tot], writes=[Lc])
        kb.op("dve", lambda e: e.tensor_tensor(out=Lc[:], in0=Lc[:], in1=Lloc[:], op=ALU.add), reads=[Lc, Lloc], writes=[Lc])
        bm = kb.sb([128, 4, NT, NT], F32, "bm")
        for h in range(4):
            for i in range(NT):
                kb.op("dve", lambda e: e.tensor_scalar(out=bm[:, h, i, 0:i + 1], in0=Lc[:, 0:i + 1, h], scalar1=Lend[:, i, h:h + 1],
                                                       scalar2=None, op0=ALU.subtract), reads=[Lc, Lend], writes=[bm])
        pt_sb = [kb.sb([128, 512], BF16, "pt") for _ in range(3)]
        po_ps = [kb.ps([128, 512], F32, "po") for _ in range(4)]
        ptr = kb.ps([128, 512], BF16, "ptr")
        oTs = kb.sb([128, 4, S], BF16, "oTs")
        otm = kb.sb([128, NT, 512], BF16, "otm")
        sg = kb.sb([128, NT, 512], F32, "sg")
        gin = kb.sb([128, NT, 512], BF16, "gin")
        kb.dma("sp", gin[:], tm.t[:, C_FG:C_FG + 512].rearrange("(t p) n -> p t n", p=128), reads=[tm], writes=[gin], sembuf=gin)
        kb.op("act", lambda e: e.activation(out=sg[:], in_=gin[:], func=AF.Silu), reads=[gin], writes=[sg])
        rec = kb.sb([128, 4 * NT], F32, "rec")

        def fin(h, i, pos):
            po = pos[0]
            r = rec[:, h * NT + i:h * NT + i + 1]
            kb.op("dve", lambda e: e.reciprocal(out=r, in_=po[:, 128:129]), reads=[po], writes=[rec])
            kb.op("dve", lambda e: e.scalar_tensor_tensor(out=otm[:, i, h * 128:(h + 1) * 128], in0=po[:, 0:128], scalar=r,
                                                          in1=sg[:, i, h * 128:(h + 1) * 128], op0=ALU.mult, op1=ALU.mult),
                  reads=[po, rec, sg], writes=[otm])

        maps = [dict(kT=lambda h, j: kT[:, h, j * 128:(j + 1) * 128], qT=lambda h, i: qT[:, h, i * 128:(i + 1) * 128],
                     bias=lambda h, i, j: bm[:, h, i, j:j + 1], bias_reads=[bm], reads=[kT, qT])]
        attn_core(kb, cx, maps, 4, vaug, fin, 128 ** -0.5, (sc_ps, pt_sb, po_ps, maskT))
        for i in range(NT):
            transpose_out(kb, cx, otm[:, i, :], ident, ptr, oTs, i) if False else None
        for i in range(NT):
            for c in range(4):
                kb.op("pe", lambda e: e.transpose(out=ptr[:, c * 128:(c + 1) * 128], in_=otm[:, i, c * 128:(c + 1) * 128], identity=ident[:]),
                      reads=[otm, ident], writes=[ptr])
            kb.op("act", lambda e: e.activation(out=oTs[:, :, i * 128:(i + 1) * 128], in_=ptr[:].rearrange("p (a b) -> p a b", a=4),
                                                func=AF.Copy), reads=[ptr], writes=[oTs])
        oT = cx.oT[s]
        kb.dma("sp", oT.t[0:512, :].rearrange("(c p) t -> p c t", p=128), oTs[:], reads=[oTs], writes=[oT], sembuf=oTs)


def rstd_from_ss(kb, ss_ap, out_ap, n, ssb, outb):
    kb.op("dve", lambda e: e.tensor_scalar(out=out_ap, in0=ss_ap, scalar1=1.0 / n, scalar2=EPS, op0=ALU.mult, op1=ALU.add),
          reads=[ssb], writes=[outb])
    kb.op("act", lambda e: e.activation(out=out_ap, in_=out_ap, func=AF.Ln), reads=[outb], writes=[outb])
    kb.op("act", lambda e: e.activation(out=out_ap, in_=out_ap, func=AF.Exp, scale=-0.5), reads=[outb], writes=[outb])


def phase_diff(kb, cx, l, s):
    lambda_init = 0.8 - 0.6 * math.exp(-0.3 * l)
    with kb.phase():
        ident = load_const(kb, cx, "ident_bf", [128, 128], BF16)
        maskT = load_const(kb, cx, "mask_le_bf", [128, 128], BF16)
        rt = load_const(kb, cx, "rope_rt", [128, 128], BF16)
        rcos = load_const(kb, cx, "rope_cos", [128, S], F32)
        rsin = load_const(kb, cx, "rope_sin", [128, S], F32)
        fmT, tm = cx.fmT[s], cx.tm[s]
        sc_ps = [kb.ps([128, 512], F32, "sc") for _ in range(3)]
        pt_sb = [kb.sb([128, 512], BF16, "pt") for _ in range(3)]
        po_ps = [kb.ps([128, 512], F32, "po") for _ in range(4)]
        ptr = kb.ps([128, 512], BF16, "ptr")
        raw = kb.sb([128, 4, S], BF16, "raw")
        qz = [kb.sb([128, 4, S], BF16, f"qz{m}") for m in range(2)]
        for m in range(2):
            kb.op("pool", lambda e: e.memset(qz[m][:], 0.0), reads=[], writes=[qz[m]])
        qr = None
        kr = kb.sb([128, 4, S], BF16, "kr")
        t1s = [kb.sb([128, 512], F32, "t1") for _ in range(2)]
        t2s = [kb.sb([128, 512], F32, "t2") for _ in range(2)]
        n = 0
        for (r0, dst) in ((R_DQ, qr), (R_DK, kr)):
            kb.dma("sp", raw[:], fmT.t[r0:r0 + 512, :].rearrange("(h p) t -> p h t", p=128), reads=[fmT], writes=[raw], sembuf=raw)
            for h in range(4):
                for tb in range(4):
                    ps = sc_ps[n % 2]
                    t1 = t1s[n % 2]
                    t2 = t2s[n % 2]
                    n += 1
                    cs = slice(tb * 512, (tb + 1) * 512)
                    kb.mm(ps[:], [(rt[:], raw[:, h, cs])], reads=[rt, raw], writes=[ps])
                    kb.op("dve", lambda e: e.tensor_tensor(out=t1[:], in0=ps[:], in1=rsin[:, cs], op=ALU.mult), reads=[ps, rsin], writes=[t1])
                    kb.op("pool", lambda e: e.tensor_tensor(out=t2[:], in0=raw[:, h, cs], in1=rcos[:, cs], op=ALU.mult),
                          reads=[raw, rcos], writes=[t2])
                    if dst is None:
                        for m in range(2):
                            ps_ = slice(m * 64, (m + 1) * 64)
                            kb.op("dve", lambda e: e.tensor_tensor(out=qz[m][ps_, h, cs], in0=t1[ps_, :], in1=t2[ps_, :], op=ALU.add),
                                  reads=[t1, t2], writes=[qz[m]])
                    else:
                        kb.op("dve", lambda e: e.tensor_tensor(out=dst[:, h, cs], in0=t1[:], in1=t2[:], op=ALU.add), reads=[t1, t2], writes=[dst])
        vaug = load_vaug(kb, cx, s, C_DV, "vaugd")
        lpb = kb.sb([128, 256], F32, "lpb")
        kb.dma("sp", lpb[:], cx.diff_lambda.t[l:l + 1].rearrange("o a b -> o (a b)").broadcast_to([128, 256]), reads=[cx.diff_lambda],
               writes=[lpb], sembuf=lpb)
        lt = kb.sb([128, 8], F32, "lt")
        lj = kb.sb([128, 64], F32, "lj")
        for k in range(2):
            kb.op("dve", lambda e: e.scalar_tensor_tensor(out=lj[:], in0=lpb[:, k * 128:k * 128 + 64], scalar=1.0, in1=lpb[:, k * 128 + 64:k * 128 + 128],
                                                          op0=ALU.mult, op1=ALU.mult, accum_out=lt[:, k:k + 1]), reads=[lpb], writes=[lj, lt])
        kb.op("act", lambda e: e.activation(out=lt[:, 2:4], in_=lt[:, 0:2], func=AF.Exp), reads=[lt], writes=[lt])
        kb.op("dve", lambda e: e.tensor_tensor(out=lt[:, 4:5], in0=lt[:, 3:4], in1=lt[:, 2:3], op=ALU.subtract), reads=[lt], writes=[lt])
        kb.op("dve", lambda e: e.tensor_scalar(out=lt[:, 5:6], in0=lt[:, 4:5], scalar1=-lambda_init, scalar2=None, op0=ALU.add), reads=[lt], writes=[lt])
        nlam = lt[:, 5:6]
        gsb = kb.sb([128, 128], F32, "gsb")
        kb.dma("sp", gsb[:], cx.diff_subln_g.t[l:l + 1, :].broadcast_to([128, 128]), reads=[cx.diff_subln_g], writes=[gsb], sembuf=gsb)
        kb.op("dve", lambda e: e.tensor_scalar(out=gsb[:], in0=gsb[:], scalar1=1.0 - lambda_init, scalar2=None, op0=ALU.mult), reads=[gsb], writes=[gsb])
        oTs = kb.sb([128, 4, S], BF16, "oTs")
        otm = kb.sb([128, NT, 512], BF16, "otm")
        sg = kb.sb([128, NT, 512], F32, "sg")
        gin_v = raw[:].rearrange("p h t -> p (h t)").rearrange("p (t n) -> p t n", n=512)
        kb.dma("sp", gin_v, tm.t[:, C_DG:C_DG + 512].rearrange("(t p) n -> p t n", p=128), reads=[tm], writes=[raw], sembuf=raw)
        kb.op("act", lambda e: e.activation(out=sg[:], in_=gin_v, func=AF.Silu), reads=[raw], writes=[sg])
        sm4 = [kb.sb([128, 8], F32, "sm4") for _ in range(2)]
        o2s = [kb.sb([128, 128], F32, "o2s") for _ in range(2)]
        ob = [kb.sb([128, 128], F32, "ob") for _ in range(2)]
        jk = kb.sb([128, 128], F32, "jk")
        cnt = {"n": 0}

        def fin(h, i, pos):
            k = cnt["n"] % 2
            cnt["n"] += 1
            po1, po2 = pos
            v = sm4[k]
            kb.op("dve", lambda e: e.reciprocal(out=v[:, 0:1], in_=po1[:, 128:129]), reads=[po1], writes=[v])
            kb.op("dve", lambda e: e.reciprocal(out=v[:, 1:2], in_=po2[:, 128:129]), reads=[po2], writes=[v])
            kb.op("dve", lambda e: e.tensor_tensor(out=v[:, 2:3], in0=v[:, 1:2], in1=nlam, op=ALU.mult), reads=[v, lt], writes=[v])
            kb.op("dve", lambda e: e.tensor_scalar(out=o2s[k][:], in0=po2[:, 0:128], scalar1=v[:, 2:3], scalar2=None, op0=ALU.mult),
                  reads=[po2, v], writes=[o2s[k]])
            kb.op("dve", lambda e: e.scalar_tensor_tensor(out=ob[k][:], in0=po1[:, 0:128], scalar=v[:, 0:1], in1=o2s[k][:],
                                                          op0=ALU.mult, op1=ALU.add), reads=[po1, v, o2s[k]], writes=[ob[k]])
            kb.op("dve", lambda e: e.scalar_tensor_tensor(out=jk[:], in0=ob[k][:], scalar=1.0, in1=ob[k][:], op0=ALU.mult, op1=ALU.mult,
                                                          accum_out=v[:, 3:4]), reads=[ob[k]], writes=[jk, v])
            rstd_from_ss(kb, v[:, 3:4], v[:, 4:5], 128, v, v)
            kb.op("dve", lambda e: e.scalar_tensor_tensor(out=ob[k][:], in0=ob[k][:], scalar=v[:, 4:5], in1=gsb[:], op0=ALU.mult, op1=ALU.mult),
                  reads=[ob[k], v, gsb], writes=[ob[k]])
            kb.op("dve", lambda e: e.tensor_tensor(out=otm[:, i, h * 128:(h + 1) * 128], in0=ob[k][:], in1=sg[:, i, h * 128:(h + 1) * 128],
                                                   op=ALU.mult), reads=[ob[k], sg], writes=[otm])

        maps = []
        for m in range(2):
            maps.append(dict(kT=(lambda h, j, m=m: kr[:, h, j * 128:(j + 1) * 128]),
                             qT=(lambda h, i, m=m: qz[m][:, h, i * 128:(i + 1) * 128]),
                             bias=None, reads=[kr, qz[m]]))
        attn_core(kb, cx, maps, 4, vaug, fin, 64 ** -0.5, (sc_ps, pt_sb, po_ps, maskT))
        for i in range(NT):
            for c in range(4):
                kb.op("pe", lambda e: e.transpose(out=ptr[:, c * 128:(c + 1) * 128], in_=otm[:, i, c * 128:(c + 1) * 128], identity=ident[:]),
                      reads=[otm, ident], writes=[ptr])
            kb.op("act", lambda e: e.activation(out=oTs[:, :, i * 128:(i + 1) * 128], in_=ptr[:].rearrange("p (a b) -> p a b", a=4),
                                                func=AF.Copy), reads=[ptr], writes=[oTs])
        oT = cx.oT[s]
        kb.dma("sp", oT.t[512:1024, :].rearrange("(c p) t -> p c t", p=128), oTs[:], reads=[oTs], writes=[oT], sembuf=oTs)


def bc3(ap2, n):
    a = ap2.shape[1]
    return ap2.unsqueeze(2).broadcast_to([ap2.shape[0], a, n])


def phase_ssd(kb, cx, l, s):
    with kb.phase():
        ident = load_const(kb, cx, "ident_bf", [128, 128], BF16)
        identf = load_const(kb, cx, "ident_f", [128, 128], F32)
        tri = load_const(kb, cx, "tri_le_f", [128, 128], F32)
        gtf = load_const(kb, cx, "gt_f", [128, 128], F32)
        negm = load_const(kb, cx, "negmask_f", [128, 128], F32)
        ones = load_const(kb, cx, "ones_f", [128, 128], F32)
        fmT, tm, sm = cx.fmT[s], cx.tm[s], cx.sm[s]
        pG = kb.ps([128, 2, 128], F32, "pG")
        pD = [kb.ps([128, 128], F32, "pD") for _ in range(2)]
        pY = kb.ps([128, 512], F32, "pY")
        pY2 = kb.ps([128, 512], F32, "pY2")
        pS = kb.ps([128, 512], F32, "pS")
        ptr = kb.ps([128, 512], BF16, "ptr")
        ptr2 = kb.ps([128, 512], BF16, "ptr2")
        cw4 = kb.sb([4, 1024], F32, "cw4")
        kb.dma("sp", cw4[:], cx.ssd_conv_w.t[l], reads=[cx.ssd_conv_w], writes=[cw4], sembuf=cw4)
        cb8 = kb.sb([8, 128], F32, "cb8")
        kb.dma("sp", cb8[:], cx.ssd_conv_b.t[l:l + 1, :].rearrange("o (c p) -> (o c) p", p=128), reads=[cx.ssd_conv_b], writes=[cb8], sembuf=cb8)
        par = kb.sb([128, 40], F32, "par")
        for c in range(8):
            kb.op("pe", lambda e: e.transpose(out=pD[0][:, c * 4:(c + 1) * 4], in_=cw4[0:4, c * 128:(c + 1) * 128], identity=identf[0:4, 0:4]),
                  reads=[cw4, identf], writes=[pD[0]])
        kb.op("pe", lambda e: e.transpose(out=pD[0][:, 32:40], in_=cb8[0:8, :], identity=identf[0:8, 0:8]), reads=[cb8, identf], writes=[pD[0]])
        kb.op("dve", lambda e: e.tensor_copy(out=par[:], in_=pD[0][:, 0:40]), reads=[pD[0]], writes=[par])
        if cx.stop < 11:
            return
        dtb = kb.sb([128, 8], F32, "dtb")
        kb.dma("sp", dtb[:], cx.ssd_dt_bias.t[l:l + 1, :].broadcast_to([128, 8]), reads=[cx.ssd_dt_bias], writes=[dtb], sembuf=dtb)
        negA = kb.sb([128, 8], F32, "negA")
        kb.dma("sp", negA[:], cx.ssd_a_log.t[l:l + 1, :].broadcast_to([128, 8]), reads=[cx.ssd_a_log], writes=[negA], sembuf=negA)
        kb.op("act", lambda e: e.activation(out=negA[:], in_=negA[:], func=AF.Exp), reads=[negA], writes=[negA])
        kb.op("dve", lambda e: e.tensor_scalar(out=negA[:], in0=negA[:], scalar1=-1.0, scalar2=None, op0=ALU.mult), reads=[negA], writes=[negA])
        dsk = kb.sb([128, 8], F32, "dsk")
        kb.dma("sp", dsk[:], cx.ssd_d.t[l:l + 1, :].broadcast_to([128, 8]), reads=[cx.ssd_d], writes=[dsk], sembuf=dsk)
        ng = kb.sb([128, 512], F32, "ng")
        kb.dma("sp", ng[:], cx.ssd_norm_g.t[l:l + 1, :].broadcast_to([128, 512]), reads=[cx.ssd_norm_g], writes=[ng], sembuf=ng)
        xbc = kb.sb([128, 8, S], BF16, "xbc")
        kb.dma("sp", xbc[:], fmT.t[R_SX:R_SX + 1024, :].rearrange("(c p) t -> p c t", p=128), reads=[fmT], writes=[xbc], sembuf=xbc)
        xc = kb.sb([128, 8, S], BF16, "xc")
        accs = [kb.sb([128, S], F32, "acc") for _ in range(2)]
        for c in range(8):
            acc = accs[c % 2]
            w = lambda j: par[:, c * 4 + j:c * 4 + j + 1]
            kb.op("dve", lambda e: e.tensor_scalar(out=acc[:], in0=xbc[:, c, :], scalar1=w(3), scalar2=None, op0=ALU.mult), reads=[xbc, par], writes=[acc])
            for sh in (1, 2, 3):
                kb.op("dve", lambda e: e.scalar_tensor_tensor(out=acc[:, sh:], in0=xbc[:, c, 0:S - sh], scalar=w(3 - sh), in1=acc[:, sh:],
                                                              op0=ALU.mult, op1=ALU.add), reads=[xbc, par, acc], writes=[acc])
            kb.op("act", lambda e: e.activation(out=xc[:, c, :], in_=acc[:], func=AF.Silu, bias=par[:, 32 + c:33 + c]), reads=[acc, par], writes=[xc])
        if cx.stop < 12:
            return
        dt = kb.sb([128, NT, 8], F32, "dt")
        kb.dma("sp", dt[:], sm.t[:, 4:12].rearrange("(t p) n -> p t n", p=128), reads=[sm], writes=[dt], sembuf=dt, allow_slow_non_contiguous=True)
        kb.op("dve", lambda e: e.tensor_tensor(out=dt[:], in0=dt[:], in1=dtb[:].unsqueeze(1).broadcast_to([128, NT, 8]), op=ALU.add), reads=[dt, dtb], writes=[dt])
        if cx.stop2 == 1:
            return
        kb.op("act", lambda e: e.activation(out=dt[:], in_=dt[:], func=AF.Exp), reads=[dt], writes=[dt])
        kb.op("act", lambda e: e.activation(out=dt[:], in_=dt[:], func=AF.Ln, bias=1.0), reads=[dt], writes=[dt])
        if cx.stop2 == 2:
            return
        av = kb.sb([128, NT, 8], F32, "av")
        kb.op("dve", lambda e: e.tensor_tensor(out=av[:], in0=dt[:], in1=negA[:].unsqueeze(1).broadcast_to([128, NT, 8]), op=ALU.mult), reads=[dt, negA], writes=[av])
        if cx.stop2 == 3:
            return
        av2 = av[:].rearrange("p t h -> p (t h)")
        kb.mm(pY[:, 0:128], [(tri[:], av2)], reads=[tri, av], writes=[pY])
        kb.mm(pY2[:, 0:128], [(ones[:], av2)], reads=[ones, av], writes=[pY2])
        if cx.stop2 == 4:
            return
        Acum = kb.sb([128, NT, 8], F32, "Acum")
        eA = kb.sb([128, NT, 8], F32, "eA")
        eAt = kb.sb([128, NT, 8], F32, "eAt")
        dsv = kb.sb([128, NT, 8], F32, "dsv")
        f2 = lambda b: b[:].rearrange("p t h -> p (t h)")
        kb.op("dve", lambda e: e.tensor_copy(out=f2(Acum), in_=pY[:, 0:128]), reads=[pY], writes=[Acum])
        kb.op("dve", lambda e: e.tensor_tensor(out=f2(dsv), in0=pY2[:, 0:128], in1=f2(Acum), op=ALU.subtract), reads=[pY2, Acum], writes=[dsv])
        if cx.stop2 == 5:
            return
        kb.op("dve", lambda e: e.tensor_copy(out=f2(eAt), in_=pY2[:, 0:128]), reads=[pY2], writes=[eAt])
        kb.op("act", lambda e: e.activation(out=f2(eAt), in_=f2(eAt), func=AF.Exp), reads=[eAt], writes=[eAt])
        if cx.stop2 == 6:
            return
        kb.op("act", lambda e: e.activation(out=f2(eA), in_=f2(Acum), func=AF.Exp), reads=[Acum], writes=[eA])
        if cx.stop2 == 7:
            return
        dsv0 = dsv
        dsv = kb.sb([128, NT, 8], F32, "dsv2")
        kb.op("act", lambda e: e.activation(out=f2(dsv), in_=f2(dsv0), func=AF.Exp), reads=[dsv0], writes=[dsv])
        if cx.stop < 13:
            return
        xs_tm = kb.sb([128, NT, 512], BF16, "xs_tm")
        B_tm = kb.sb([128, NT, 256], BF16, "B_tm")
        for t in range(NT):
            ts_ = slice(t * 128, (t + 1) * 128)
            for c in range(4):
                kb.op("pe", lambda e: e.transpose(out=ptr[:, c * 128:(c + 1) * 128], in_=xc[:, c, ts_], identity=ident[:]), reads=[xc, ident], writes=[ptr])
            kb.op("act", lambda e: e.activation(out=xs_tm[:, t, :], in_=ptr[:, 0:512], func=AF.Copy), reads=[ptr], writes=[xs_tm])
            for c in range(2):
                kb.op("pe", lambda e: e.transpose(out=ptr2[:, c * 128:(c + 1) * 128], in_=xc[:, 4 + c, ts_], identity=ident[:]), reads=[xc, ident], writes=[ptr2])
            kb.op("dve", lambda e: e.tensor_copy(out=B_tm[:, t, :], in_=ptr2[:, 0:256]), reads=[ptr2], writes=[B_tm])
        if cx.stop < 14:
            return
        zin = kb.sb([128, NT, 512], BF16, "zin")
        kb.dma("sp", zin[:], tm.t[:, C_SZ:C_SZ + 512].rearrange("(t p) n -> p t n", p=128), reads=[tm], writes=[zin], sembuf=zin)
        oTs = kb.sb([128, 4, S], BF16, "oTs")
        ST = kb.sb([128, 512], F32, "ST")
        STb = kb.sb([128, 512], BF16, "STb")
        tmpS = kb.sb([128, 512], F32, "tmpS")
        kb.op("dve", lambda e: e.memset(ST[:], 0.0), reads=[], writes=[ST])
        Xd = [kb.sb([128, 512], BF16, "Xd") for _ in range(2)]
        Xds = [kb.sb([128, 512], BF16, "Xds") for _ in range(2)]
        adg8 = [kb.sb([128, 128], F32, "adg") for _ in range(8)]
        Eb8 = [kb.sb([128, 128], F32, "Eb") for _ in range(8)]
        Gm = [kb.sb([128, 2, 128], F32, "Gm") for _ in range(2)]
        maskle = load_const(kb, cx, "tri_le_f", [128, 128], F32) if False else tri
        MT = [kb.sb([128, 128], BF16, "MT") for _ in range(2)]
        ysb = kb.sb([128, 512], F32, "ysb")
        y2 = kb.sb([128, 512], F32, "y2")
        szt = kb.sb([128, 512], F32, "szt")
        jk = kb.sb([128, 512], F32, "jk")
        otm = [kb.sb([128, 512], BF16, "otm") for _ in range(2)]
        sv = kb.sb([128, NT, 4], F32, "sv")
        v3 = lambda b: b[:].rearrange("p (h d) -> p h d", h=8)
        n = 0
        for t in range(NT if cx.stop >= 20 else max(0, cx.stop - 14)):
            ts_ = slice(t * 128, (t + 1) * 128)
            xd, xds = Xd[t % 2], Xds[t % 2]
            kb.op("dve", lambda e: e.tensor_tensor(out=v3(xd), in0=xs_tm[:, t, :].rearrange("p (h d) -> p h d", h=8), in1=bc3(dt[:, t, :], 64),
                                                   op=ALU.mult), reads=[xs_tm, dt], writes=[xd])
            kb.op("dve", lambda e: e.tensor_tensor(out=v3(xds), in0=v3(xd), in1=bc3(dsv[:, t, :], 64), op=ALU.mult), reads=[xd, dsv], writes=[xds])
            for g in range(2):
                kb.mm(pG[:, g, :], [(xc[:, 4 + g, ts_], xc[:, 6 + g, ts_])], reads=[xc], writes=[pG])
            gm = Gm[t % 2]
            kb.op("dve", lambda e: e.tensor_tensor(out=gm[:], in0=pG[:], in1=maskle[:].unsqueeze(1).broadcast_to([128, 2, 128]), op=ALU.mult),
                  reads=[pG, maskle], writes=[gm])
            for h in range(8):
                a_ = adg8[h]
                kb.op("dve", lambda e: e.tensor_scalar(out=a_[:], in0=tri[:], scalar1=av[:, t, h:h + 1], scalar2=None, op0=ALU.mult),
                      reads=[tri, av], writes=[a_])
            for h in range(8):
                pd = pD[h % 2]
                kb.mm(pd[:], [(gtf[:], adg8[h][:])], reads=[gtf, adg8[h]], writes=[pd])
                kb.op("act", lambda e: e.activation(out=Eb8[h][:], in_=pd[:], func=AF.Exp), reads=[pd], writes=[Eb8[h]])
            for h in range(8):
                g = h // 4
                m_ = MT[h % 2]
                kb.op("dve", lambda e: e.tensor_tensor(out=m_[:], in0=gm[:, g, :], in1=Eb8[h][:], op=ALU.mult), reads=[gm, Eb8[h]], writes=[m_])
                hs = slice(h * 64, (h + 1) * 64)
                kb.mm(pY[:, hs], [(m_[:], xd[:, hs])], reads=[m_, xd], writes=[pY])
                if t > 0:
                    kb.mm(pY2[:, hs], [(xc[:, 6 + g, ts_], STb[:, hs])], reads=[xc, STb], writes=[pY2])
                kb.mm(pS[:, hs], [(B_tm[:, t, g * 128:(g + 1) * 128], xds[:, hs])], reads=[B_tm, xds], writes=[pS])
            kb.op("act", lambda e: e.activation(out=ysb[:], in_=pY[:], func=AF.Copy), reads=[pY], writes=[ysb])
            if t > 0:
                kb.op("dve", lambda e: e.tensor_tensor(out=v3(y2), in0=pY2[:].rearrange("p (h d) -> p h d", h=8), in1=bc3(eA[:, t, :], 64), op=ALU.mult),
                      reads=[pY2, eA], writes=[y2])
                kb.op("dve", lambda e: e.tensor_tensor(out=ysb[:], in0=ysb[:], in1=y2[:], op=ALU.add), reads=[ysb, y2], writes=[ysb])
            if t < NT - 1:
                kb.op("dve", lambda e: e.tensor_tensor(out=v3(tmpS), in0=v3(ST), in1=bc3(eAt[:, t, :], 64), op=ALU.mult), reads=[ST, eAt], writes=[tmpS])
                kb.op("dve", lambda e: e.tensor_tensor(out=ST[:], in0=tmpS[:], in1=pS[:], op=ALU.add), reads=[tmpS, pS], writes=[ST])
                kb.op("act", lambda e: e.activation(out=STb[:], in_=ST[:], func=AF.Copy), reads=[ST], writes=[STb])
            kb.op("dve", lambda e: e.tensor_tensor(out=v3(y2), in0=xs_tm[:, t, :].rearrange("p (h d) -> p h d", h=8), in1=bc3(dsk[:], 64), op=ALU.mult),
                  reads=[xs_tm, dsk], writes=[y2])
            kb.op("dve", lambda e: e.tensor_tensor(out=ysb[:], in0=ysb[:], in1=y2[:], op=ALU.add), reads=[ysb, y2], writes=[ysb])
            kb.op("act", lambda e: e.activation(out=szt[:], in_=zin[:, t, :], func=AF.Silu), reads=[zin], writes=[szt])
            kb.op("dve", lambda e: e.tensor_tensor(out=ysb[:], in0=ysb[:], in1=szt[:], op=ALU.mult), reads=[ysb, szt], writes=[ysb])
            kb.op("dve", lambda e: e.scalar_tensor_tensor(out=jk[:], in0=ysb[:], scalar=1.0, in1=ysb[:], op0=ALU.mult, op1=ALU.mult,
                                                          accum_out=sv[:, t, 0:1]), reads=[ysb], writes=[jk, sv])
            rstd_from_ss(kb, sv[:, t, 0:1], sv[:, t, 1:2], 512, sv, sv)
            om = otm[t % 2]
            kb.op("dve", lambda e: e.scalar_tensor_tensor(out=om[:], in0=ysb[:], scalar=sv[:, t, 1:2], in1=ng[:], op0=ALU.mult, op1=ALU.mult),
                  reads=[ysb, sv, ng], writes=[om])
            for c in range(4):
                kb.op("pe", lambda e: e.transpose(out=ptr[:, c * 128:(c + 1) * 128], in_=om[:, c * 128:(c + 1) * 128], identity=ident[:]),
                      reads=[om, ident], writes=[ptr])
            kb.op("act", lambda e: e.activation(out=oTs[:, :, ts_], in_=ptr[:, 0:512].rearrange("p (a b) -> p a b", a=4), func=AF.Copy),
                  reads=[ptr], writes=[oTs])
        oT = cx.oT[s]
        kb.dma("sp", oT.t[1024:1536, :].rearrange("(c p) t -> p c t", p=128), oTs[:], reads=[oTs], writes=[oT], sembuf=oTs)


def s5_alloc(kb, cx):
    def P(shape, dt, name):
        t = kb.es.enter_context(kb.nc.sbuf_tensor("s5p_" + name, list(shape), dt))
        return Buf(kb, name, t)
    cx.s5 = dict(BT=P([128, 16, 2, 128], BF16, "BT"), CP=P([128, 16, 2, 128], BF16, "CP"),
                 PWr=P([128, 11, 16], F32, "PWr"), PWi=P([128, 11, 16], F32, "PWi"), NPWi=P([128, 11, 16], F32, "NPWi"),
                 dsk=P([128, 4], F32, "dsk"), Ur=P([128, 11, 16], F32, "Ur"), Ui=P([128, 11, 16], F32, "Ui"),
                 NUi=P([128, 11, 16], F32, "NUi"), R=P([128, 16], F32, "R"))


def phase_s5_prep(kb, cx, l):
    TWO_PI = 2.0 * math.pi
    s5 = cx.s5
    with kb.phase():
        identf = load_const(kb, cx, "ident_f", [128, 128], F32)
        ones = load_const(kb, cx, "ones_f", [128, 128], F32)
        gmask = load_const(kb, cx, "gmask", [128, 8], F32)
        pt = kb.ps([128, 128], F32, "pt")
        pts = [kb.ps([128, 128], F32, "pts") for _ in range(2)]
        p3 = kb.sb([16, 3, 128], F32, "p3")
        kb.dma("sp", p3[:, 0, :], cx.s5_a_re.t[l].rearrange("(st g) p -> st (g p)", g=2), reads=[cx.s5_a_re], writes=[p3], sembuf=p3)
        kb.dma("sp", p3[:, 1, :], cx.s5_a_im.t[l].rearrange("(st g) p -> st (g p)", g=2), reads=[cx.s5_a_im], writes=[p3], sembuf=p3)
        ld = kb.sb([16, 2], F32, "ld")
        kb.dma("sp", ld[:], cx.s5_log_dt.t[l:l + 1, :].rearrange("o (st g) -> (o st) g", g=2), reads=[cx.s5_log_dt], writes=[ld], sembuf=ld)
        for g in range(2):
            kb.op("dve", lambda e: e.tensor_scalar(out=p3[:, 2, g * 64:(g + 1) * 64], in0=ones[0:16, 0:64], scalar1=ld[:, g:g + 1], scalar2=None,
                                                   op0=ALU.mult), reads=[ones, ld], writes=[p3])
        d4 = kb.sb([4, 128], F32, "d4")
        kb.dma("sp", d4[:], cx.s5_d.t[l].rearrange("(cc gl) h -> cc (gl h)", gl=8), reads=[cx.s5_d], writes=[d4], sembuf=d4)
        for k in range(3):
            kb.op("pe", lambda e: e.transpose(out=pt[:, k * 16:(k + 1) * 16], in_=p3[0:16, k, :], identity=identf[0:16, 0:16]),
                  reads=[p3, identf], writes=[pt])
        kb.op("pe", lambda e: e.transpose(out=pt[:, 48:52], in_=d4[0:4, :], identity=identf[0:4, 0:4]), reads=[d4, identf], writes=[pt])
        prm = kb.sb([128, 3, 16], F32, "prm")
        kb.op("dve", lambda e: e.tensor_copy(out=prm[:].rearrange("p a b -> p (a b)"), in_=pt[:, 0:48]), reads=[pt], writes=[prm])
        kb.op("dve", lambda e: e.tensor_copy(out=s5["dsk"][:], in_=pt[:, 48:52]), reads=[pt], writes=[s5["dsk"]])
        aR, aI, LD = prm[:, 0, :], prm[:, 1, :], prm[:, 2, :]
        w = kb.sb([128, 24, 16], F32, "w")
        W = lambda i: w[:, i, :]
        dve = lambda fn, rd=(), wr=(): kb.op("dve", fn, reads=list(rd) or [w, prm], writes=list(wr) or [w])
        act = lambda fn, rd=(), wr=(): kb.op("act", fn, reads=list(rd) or [w, prm], writes=list(wr) or [w])
        TT = lambda o, a, b, op: dve(lambda e: e.tensor_tensor(out=o, in0=a, in1=b, op=op))
        act(lambda e: e.activation(out=W(0), in_=LD, func=AF.Exp))
        TT(W(1), aR, W(0), ALU.mult)
        act(lambda e: e.activation(out=W(1), in_=W(1), func=AF.Exp))
        TT(W(2), aI, W(0), ALU.mult)
        dve(lambda e: e.tensor_scalar(out=W(2), in0=W(2), scalar1=1.0 / TWO_PI, scalar2=None, op0=ALU.mult))
        wi = kb.sb([128, 16], mybir.dt.int32, "wi")
        dve(lambda e: e.tensor_copy(out=wi[:], in_=W(2)), wr=[wi])
        dve(lambda e: e.tensor_copy(out=W(3), in_=wi[:]), rd=[wi])
        TT(W(3), W(2), W(3), ALU.subtract)
        dve(lambda e: e.tensor_scalar(out=W(4), in0=W(3), scalar1=0.25, scalar2=None, op0=ALU.add))

        def wrap(src, dst, t1, t2):
            dve(lambda e: e.tensor_single_scalar(out=t1, in_=src, scalar=0.5, op=ALU.is_gt))
            dve(lambda e: e.tensor_single_scalar(out=t2, in_=src, scalar=-0.5, op=ALU.is_lt))
            TT(dst, src, t1, ALU.subtract)
            TT(dst, dst, t2, ALU.add)
        wrap(W(3), W(5), W(6), W(7))
        wrap(W(4), W(8), W(6), W(7))
        act(lambda e: e.activation(out=W(9), in_=W(5), func=AF.Sin, scale=TWO_PI))
        act(lambda e: e.activation(out=W(10), in_=W(8), func=AF.Sin, scale=TWO_PI))
        PWr, PWi, NPWi = s5["PWr"], s5["PWi"], s5["NPWi"]
        kb.op("dve", lambda e: e.tensor_tensor(out=PWr[:, 0, :], in0=W(1), in1=W(10), op=ALU.mult), reads=[w], writes=[PWr])
        kb.op("dve", lambda e: e.tensor_tensor(out=PWi[:, 0, :], in0=W(1), in1=W(9), op=ALU.mult), reads=[w], writes=[PWi])
        for k in range(10):
            kb.op("dve", lambda e: e.tensor_tensor(out=W(11), in0=PWr[:, k, :], in1=PWr[:, k, :], op=ALU.mult), reads=[PWr], writes=[w])
            kb.op("dve", lambda e: e.tensor_tensor(out=W(12), in0=PWi[:, k, :], in1=PWi[:, k, :], op=ALU.mult), reads=[PWi], writes=[w])
            kb.op("dve", lambda e: e.tensor_tensor(out=PWr[:, k + 1, :], in0=W(11), in1=W(12), op=ALU.subtract), reads=[w], writes=[PWr])
            kb.op("dve", lambda e: e.scalar_tensor_tensor(out=PWi[:, k + 1, :], in0=PWr[:, k, :], scalar=2.0, in1=PWi[:, k, :], op0=ALU.mult, op1=ALU.mult),
                  reads=[PWr, PWi], writes=[PWi])
        kb.op("dve", lambda e: e.tensor_scalar(out=NPWi[:], in0=PWi[:], scalar1=-1.0, scalar2=None, op0=ALU.mult), reads=[PWi], writes=[NPWi])
        Ur, Ui, NUi, Rm = s5["Ur"], s5["Ui"], s5["NUi"], s5["R"]
        kb.op("dve", lambda e: e.tensor_copy(out=Ur[:, 0, :], in_=W(10)), reads=[w], writes=[Ur])
        kb.op("dve", lambda e: e.tensor_copy(out=Ui[:, 0, :], in_=W(9)), reads=[w], writes=[Ui])
        kb.op("dve", lambda e: e.tensor_copy(out=Rm[:], in_=W(1)), reads=[w], writes=[Rm])
        for k in range(10):
            kb.op("dve", lambda e: e.tensor_tensor(out=W(19), in0=Ur[:, k, :], in1=Ur[:, k, :], op=ALU.mult), reads=[Ur], writes=[w])
            kb.op("dve", lambda e: e.tensor_tensor(out=W(20), in0=Ui[:, k, :], in1=Ui[:, k, :], op=ALU.mult), reads=[Ui], writes=[w])
            kb.op("dve", lambda e: e.tensor_tensor(out=Ur[:, k + 1, :], in0=W(19), in1=W(20), op=ALU.subtract), reads=[w], writes=[Ur])
            kb.op("dve", lambda e: e.scalar_tensor_tensor(out=Ui[:, k + 1, :], in0=Ur[:, k, :], scalar=2.0, in1=Ui[:, k, :], op0=ALU.mult, op1=ALU.mult),
                  reads=[Ur, Ui], writes=[Ui])
        kb.op("dve", lambda e: e.tensor_scalar(out=NUi[:], in0=Ui[:], scalar1=-1.0, scalar2=None, op0=ALU.mult), reads=[Ui], writes=[NUi])
        Ets = [kb.sb([128, 2, S], F32, "Et") for _ in range(2)]
        for st in range(16):
            E = Ets[st % 2]
            kb.op("dve", lambda e: e.memset(E[:, 0, 0:1], 1.0), reads=[], writes=[E])
            kb.op("dve", lambda e: e.memset(E[:, 1, 0:1], 0.0), reads=[], writes=[E])
            for k in range(11):
                sh = 1 << k
                ur, ui, nui = Ur[:, k, st:st + 1], Ui[:, k, st:st + 1], NUi[:, k, st:st + 1]
                kb.op("dve", lambda e: e.tensor_scalar(out=E[:, 0, sh:2 * sh], in0=E[:, 0, 0:sh], scalar1=ur, scalar2=None, op0=ALU.mult),
                      reads=[E, Ur], writes=[E])
                kb.op("dve", lambda e: e.scalar_tensor_tensor(out=E[:, 0, sh:2 * sh], in0=E[:, 1, 0:sh], scalar=nui, in1=E[:, 0, sh:2 * sh],
                                                              op0=ALU.mult, op1=ALU.add), reads=[E, NUi], writes=[E])
                kb.op("dve", lambda e: e.tensor_scalar(out=E[:, 1, sh:2 * sh], in0=E[:, 0, 0:sh], scalar1=ui, scalar2=None, op0=ALU.mult),
                      reads=[E, Ui], writes=[E])
                kb.op("dve", lambda e: e.scalar_tensor_tensor(out=E[:, 1, sh:2 * sh], in0=E[:, 1, 0:sh], scalar=ur, in1=E[:, 1, sh:2 * sh],
                                                              op0=ALU.mult, op1=ALU.add), reads=[E, Ur], writes=[E])
            kb.dma("sp", cx.Etab.t[st].rearrange("a p t -> p a t"), E[:], reads=[E], writes=[cx.Etab], sembuf=E)
        ar, ai = PWr[:, 0, :], PWi[:, 0, :]
        rdP = [w, prm, PWr, PWi]
        dveP = lambda fn: kb.op("dve", fn, reads=rdP, writes=[w])
        dveP(lambda e: e.tensor_scalar(out=W(13), in0=ar, scalar1=-1.0, scalar2=None, op0=ALU.add))
        dveP(lambda e: e.tensor_tensor(out=W(14), in0=aR, in1=aR, op=ALU.mult))
        dveP(lambda e: e.tensor_tensor(out=W(15), in0=aI, in1=aI, op=ALU.mult))
        dveP(lambda e: e.tensor_tensor(out=W(14), in0=W(14), in1=W(15), op=ALU.add))
        dveP(lambda e: e.reciprocal(out=W(14), in_=W(14)))
        dveP(lambda e: e.tensor_tensor(out=W(15), in0=W(13), in1=aR, op=ALU.mult))
        dveP(lambda e: e.tensor_tensor(out=W(16), in0=ai, in1=aI, op=ALU.mult))
        dveP(lambda e: e.tensor_tensor(out=W(15), in0=W(15), in1=W(16), op=ALU.add))
        dveP(lambda e: e.tensor_tensor(out=W(17), in0=W(15), in1=W(14), op=ALU.mult))
        dveP(lambda e: e.tensor_tensor(out=W(15), in0=ai, in1=aR, op=ALU.mult))
        dveP(lambda e: e.tensor_tensor(out=W(16), in0=W(13), in1=aI, op=ALU.mult))
        dveP(lambda e: e.tensor_tensor(out=W(15), in0=W(15), in1=W(16), op=ALU.subtract))
        dveP(lambda e: e.tensor_tensor(out=W(18), in0=W(15), in1=W(14), op=ALU.mult))
        bq = kb.sb([128, 2, 16, 16], F32, "bq")
        for ri, src in enumerate((cx.s5_b_re, cx.s5_b_im)):
            v = src.t[l].rearrange("(st g) p h -> g p st h", g=2)
            for g in range(2):
                kb.dma("sp", bq[g * 64:(g + 1) * 64, ri, :, :], v[g], reads=[src], writes=[bq], sembuf=bq, allow_slow_non_contiguous=True)
        bb = kb.sb([128, 2, 16, 16], F32, "bb")
        tq = kb.sb([128, 2, 16, 16], F32, "tq")
        gr, gi = bc3(W(17), 16), bc3(W(18), 16)
        rdB = [bq, w, tq, bb]
        kb.op("dve", lambda e: e.tensor_tensor(out=tq[:, 0], in0=bq[:, 0], in1=gr, op=ALU.mult), reads=rdB, writes=[tq])
        kb.op("dve", lambda e: e.tensor_tensor(out=tq[:, 1], in0=bq[:, 1], in1=gi, op=ALU.mult), reads=rdB, writes=[tq])
        kb.op("dve", lambda e: e.tensor_tensor(out=bb[:, 0], in0=tq[:, 0], in1=tq[:, 1], op=ALU.subtract), reads=rdB, writes=[bb])
        kb.op("dve", lambda e: e.tensor_tensor(out=tq[:, 0], in0=bq[:, 1], in1=gr, op=ALU.mult), reads=rdB, writes=[tq])
        kb.op("dve", lambda e: e.tensor_tensor(out=tq[:, 1], in0=bq[:, 0], in1=gi, op=ALU.mult), reads=rdB, writes=[tq])
        kb.op("dve", lambda e: e.tensor_tensor(out=bb[:, 1], in0=tq[:, 0], in1=tq[:, 1], op=ALU.add), reads=rdB, writes=[bb])
        bpad = kb.sb([128, 16, 2, 128], F32, "bpad")
        kb.op("pool", lambda e: e.memset(bpad[:], 0.0), reads=[], writes=[bpad])
        for st4 in range(4):
            for g in range(2):
                c0 = (st4 * 2 + g) * 16
                for ri in range(2):
                    kb.op("dve", lambda e: e.tensor_copy(out=bpad[g * 64:(g + 1) * 64, st4::4, ri, c0:c0 + 16], in_=bb[g * 64:(g + 1) * 64, ri, st4::4, :]),
                          reads=[bb], writes=[bpad])
        BT, CP = s5["BT"], s5["CP"]
        n = 0
        for st in range(16):
            for ri in range(2):
                p = pts[n % 2]
                n += 1
                kb.op("pe", lambda e: e.transpose(out=p[:], in_=bpad[:, st, ri, :], identity=identf[:]), reads=[bpad, identf], writes=[p])
                kb.op("act", lambda e: e.activation(out=BT[:, st, ri, :], in_=p[:], func=AF.Copy), reads=[p], writes=[BT])
        cn = kb.sb([128, 2, 4, 64], F32, "cn")
        for ri, src in enumerate((cx.s5_c_re, cx.s5_c_im)):
            for cc in range(4):
                kb.dma("sp", cn[:, ri, cc, :], src.t[l, cc * 8:(cc + 1) * 8].rearrange("g h p -> (g h) p"), reads=[src], writes=[cn], sembuf=cn)
        mts = [kb.sb([128, 128], F32, "mt") for _ in range(2)]
        for st in range(16):
            cc, gl0 = st // 4, (st % 4) * 2
            for ri in range(2):
                mt = mts[n % 2]
                p = pts[n % 2]
                n += 1
                for g in range(2):
                    kb.op("dve", lambda e: e.tensor_scalar(out=mt[:, g * 64:(g + 1) * 64], in0=cn[:, ri, cc, :], scalar1=gmask[:, gl0 + g:gl0 + g + 1],
                                                           scalar2=None, op0=ALU.mult), reads=[cn, gmask], writes=[mt])
                kb.op("pe", lambda e: e.transpose(out=p[:], in_=mt[:], identity=identf[:]), reads=[mt, identf], writes=[p])
                sgn = 1.0 if ri == 0 else -1.0
                kb.op("dve", lambda e: e.tensor_scalar(out=CP[:, st, ri, :], in0=p[:], scalar1=sgn, scalar2=None, op0=ALU.mult), reads=[p], writes=[CP])


def phase_s5(kb, cx, l, s):
    s5 = cx.s5
    BT, CP, PWr, PWi, NPWi, dsk = s5["BT"], s5["CP"], s5["PWr"], s5["PWi"], s5["NPWi"], s5["dsk"]
    with kb.phase():
        ident = load_const(kb, cx, "ident_bf", [128, 128], BF16)
        fmT, tm = cx.fmT[s], cx.tm[s]
        pYo = [kb.ps([128, 512], F32, "pYo") for _ in range(4)]
        pB = [kb.ps([128, 512], F32, "pB") for _ in range(2)]
        ptr = kb.ps([128, 512], BF16, "ptr")
        uT = kb.sb([128, 4, S], BF16, "uT")
        kb.dma("sp", uT[:], fmT.t[R_SU:R_SU + 512, :].rearrange("(c p) t -> p c t", p=128), reads=[fmT], writes=[uT], sembuf=uT)
        bp = [kb.sb([128, S], F32, f"bp{b}") for b in range(2)]
        zz = [kb.sb([128, S], F32, f"zz{b}") for b in range(2)]
        xb = [kb.sb([128, S], BF16, f"xb{b}") for b in range(2)]
        Etl = [kb.sb([128, 2, S], F32, "Etl") for _ in range(2)]
        tA = [kb.sb([128, 512], F32, "tA") for _ in range(4)]
        tB = [kb.sb([128, 512], F32, "tB") for _ in range(4)]
        gT = kb.sb([128, 4, S], BF16, "gT")
        wss = [kb.sb([128, 1024], F32, "ws") for _ in range(2)]
        wg = kb.sb([128, 4, 1024], BF16, "wg")
        wv = cx.s5_w_glu.t[l].rearrange("(c p) n -> p c n", p=128)
        for c in range(4):
            kb.dma("sp", wss[c % 2][:], wv[:, c, :], reads=[cx.s5_w_glu], writes=[wss[c % 2]], sembuf=wss[c % 2])
            kb.op("pool", lambda e: e.tensor_copy(out=wg[:, c, :], in_=wss[c % 2][:]), reads=[wss[c % 2]], writes=[wg])
        sgin = [kb.sb([128, 512], BF16, "sgin") for _ in range(2)]
        yb = [kb.sb([128, 512], F32, "yb") for _ in range(2)]
        q1 = [kb.sb([128, 512], F32, "q1") for _ in range(2)]
        Rm = s5["R"]
        nt = 0
        for st in range(16):
            cc = st // 4
            Et = Etl[st % 2]
            kb.dma("sp", Et[:], cx.Etab.t[st].rearrange("a p t -> p a t"), reads=[cx.Etab], writes=[Et], sembuf=Et)
            for tb in range(4):
                cs = slice(tb * 512, (tb + 1) * 512)
                a_, b_ = tA[(2 * nt) % 4], tB[(2 * nt) % 4]
                a2_, b2_ = tA[(2 * nt + 1) % 4], tB[(2 * nt + 1) % 4]
                nt += 1
                kb.mm(pB[0][:], [(BT[:, st, 0, :], uT[:, cc, cs])], reads=[BT, uT], writes=[pB[0]])
                kb.mm(pB[1][:], [(BT[:, st, 1, :], uT[:, cc, cs])], reads=[BT, uT], writes=[pB[1]])
                kb.op("act", lambda e: e.activation(out=zz[0][:, cs], in_=pB[0][:], func=AF.Copy), reads=[pB[0]], writes=[zz[0]])
                kb.op("act", lambda e: e.activation(out=zz[1][:, cs], in_=pB[1][:], func=AF.Copy), reads=[pB[1]], writes=[zz[1]])
                kb.op("dve", lambda e: e.tensor_tensor(out=a_[:], in0=zz[0][:, cs], in1=Et[:, 0, cs], op=ALU.mult), reads=[zz[0], Et], writes=[a_])
                kb.op("pool", lambda e: e.tensor_tensor(out=b_[:], in0=zz[1][:, cs], in1=Et[:, 1, cs], op=ALU.mult), reads=[zz[1], Et], writes=[b_])
                kb.op("pool", lambda e: e.tensor_tensor(out=b2_[:], in0=zz[0][:, cs], in1=Et[:, 1, cs], op=ALU.mult), reads=[zz[0], Et], writes=[b2_])
                kb.op("dve", lambda e: e.tensor_tensor(out=a2_[:], in0=zz[1][:, cs], in1=Et[:, 0, cs], op=ALU.mult), reads=[zz[1], Et], writes=[a2_])
                kb.op("dve", lambda e: e.tensor_tensor(out=bp[0][:, cs], in0=a_[:], in1=b_[:], op=ALU.add), reads=[a_, b_], writes=[bp[0]])
                kb.op("dve", lambda e: e.tensor_tensor(out=bp[1][:, cs], in0=a2_[:], in1=b2_[:], op=ALU.subtract), reads=[a2_, b2_], writes=[bp[1]])
            rb = Rm[:, st:st + 1].broadcast_to([128, S])
            for ri in range(2):
                kb.op("dve", lambda e: e.tensor_tensor_scan(out=zz[ri][:], data0=rb, data1=bp[ri][:], initial=0.0, op0=ALU.mult, op1=ALU.add),
                      reads=[Rm, bp[ri], zz[ri]], writes=[zz[ri]])
            for tb in range(4):
                cs = slice(tb * 512, (tb + 1) * 512)
                a_, b_ = tA[(2 * nt) % 4], tB[(2 * nt) % 4]
                a2_, b2_ = tA[(2 * nt + 1) % 4], tB[(2 * nt + 1) % 4]
                nt += 1
                kb.op("dve", lambda e: e.tensor_tensor(out=a_[:], in0=zz[0][:, cs], in1=Et[:, 0, cs], op=ALU.mult), reads=[zz[0], Et], writes=[a_])
                kb.op("pool", lambda e: e.tensor_tensor(out=b_[:], in0=zz[1][:, cs], in1=Et[:, 1, cs], op=ALU.mult), reads=[zz[1], Et], writes=[b_])
                kb.op("pool", lambda e: e.tensor_tensor(out=b2_[:], in0=zz[1][:, cs], in1=Et[:, 0, cs], op=ALU.mult), reads=[zz[1], Et], writes=[b2_])
                kb.op("dve", lambda e: e.tensor_tensor(out=a2_[:], in0=zz[0][:, cs], in1=Et[:, 1, cs], op=ALU.mult), reads=[zz[0], Et], writes=[a2_])
                kb.op("dve", lambda e: e.tensor_tensor(out=xb[0][:, cs], in0=a_[:], in1=b_[:], op=ALU.subtract), reads=[a_, b_], writes=[xb[0]])
                kb.op("dve", lambda e: e.tensor_tensor(out=xb[1][:, cs], in0=a2_[:], in1=b2_[:], op=ALU.add), reads=[a2_, b2_], writes=[xb[1]])
            for tb in range(4):
                cs = slice(tb * 512, (tb + 1) * 512)
                py = pYo[tb]
                first, last = (st % 4 == 0), (st % 4 == 3)
                kb._wait("pe", kb._deps([CP, xb[0], xb[1]], [py] if first else []))
                kb.nc.tensor.matmul(py[:], CP[:, st, 0, :], xb[0][:, cs], start=first, stop=False)
                ins = kb.nc.tensor.matmul(py[:], CP[:, st, 1, :], xb[1][:, cs], start=False, stop=last)
                sname = kb.sem["pe"]
                kb.semval[sname] += 1
                ins.then_inc(kb.semobj[sname], 1)
                tok = (sname, kb.semval[sname])
                xb[0].rd.append(tok)
                xb[1].rd.append(tok)
                if first:
                    py.rd = []
                py.lw = [tok]
            if st % 4 == 3:
                for tb in range(4):
                    cs = slice(tb * 512, (tb + 1) * 512)
                    y, q = yb[tb % 2], q1[tb % 2]
                    kb.op("dve", lambda e: e.scalar_tensor_tensor(out=y[:], in0=uT[:, cc, cs], scalar=dsk[:, cc:cc + 1], in1=pYo[tb][:], op0=ALU.mult, op1=ALU.add),
                          reads=[uT, dsk, pYo[tb]], writes=[y])
                    kb.op("dve", lambda e: e.tensor_tensor(out=q[:], in0=y[:], in1=y[:], op=ALU.mult), reads=[y], writes=[q])
                    kb.op("dve", lambda e: e.tensor_scalar(out=q[:], in0=q[:], scalar1=0.044715, scalar2=1.0, op0=ALU.mult, op1=ALU.add), reads=[q], writes=[q])
                    kb.op("dve", lambda e: e.tensor_tensor(out=q[:], in0=q[:], in1=y[:], op=ALU.mult), reads=[q, y], writes=[q])
                    kb.op("act", lambda e: e.activation(out=q[:], in_=q[:], func=AF.Sigmoid, scale=1.5957691216057308), reads=[q], writes=[q])
                    kb.op("dve", lambda e: e.tensor_tensor(out=gT[:, cc, cs], in0=q[:], in1=y[:], op=ALU.mult), reads=[q, y], writes=[gT])
        oTs = kb.sb([128, 4, S], BF16, "oTs")
        sgs = [kb.sb([128, 512], F32, "sgs") for _ in range(2)]
        sig = [kb.sb([128, 512], F32, "sig") for _ in range(2)]
        od = [kb.sb([128, 512], F32, "od") for _ in range(2)]
        om = [kb.sb([128, 512], BF16, "om") for _ in range(2)]
        for t in range(NT):
            ts_ = slice(t * 128, (t + 1) * 128)
            k = t % 2
            kb.mm(pB[0][:], [(gT[:, c, ts_], wg[:, c, 0:512]) for c in range(4)], reads=[gT, wg], writes=[pB[0]])
            kb.mm(pB[1][:], [(gT[:, c, ts_], wg[:, c, 512:1024]) for c in range(4)], reads=[gT, wg], writes=[pB[1]])
            kb.op("act", lambda e: e.activation(out=sig[k][:], in_=pB[1][:], func=AF.Sigmoid), reads=[pB[1]], writes=[sig[k]])
            kb.dma("sp", sgin[k][:], tm.t[ts_, C_SG:C_SG + 512], reads=[tm], writes=[sgin[k]], sembuf=sgin[k])
            kb.op("act", lambda e: e.activation(out=sgs[k][:], in_=sgin[k][:], func=AF.Silu), reads=[sgin[k]], writes=[sgs[k]])
            kb.op("dve", lambda e: e.tensor_tensor(out=od[k][:], in0=pB[0][:], in1=sig[k][:], op=ALU.mult), reads=[pB[0], sig[k]], writes=[od[k]])
            kb.op("dve", lambda e: e.tensor_tensor(out=om[k][:], in0=od[k][:], in1=sgs[k][:], op=ALU.mult), reads=[od[k], sgs[k]], writes=[om[k]])
            for c in range(4):
                kb.op("pe", lambda e: e.transpose(out=ptr[:, c * 128:(c + 1) * 128], in_=om[k][:, c * 128:(c + 1) * 128], identity=ident[:]),
                      reads=[om[k], ident], writes=[ptr])
            kb.op("act", lambda e: e.activation(out=oTs[:, :, ts_], in_=ptr[:].rearrange("p (a b) -> p a b", a=4), func=AF.Copy), reads=[ptr], writes=[oTs])
        oT = cx.oT[s]
        kb.dma("sp", oT.t[1536:2048, :].rearrange("(c p) t -> p c t", p=128), oTs[:], reads=[oTs], writes=[oT], sembuf=oTs)


def load_w_bf16(kb, dst, src_ap2d, srcbuf, stg):
    v = src_ap2d.rearrange("(c p) n -> p c n", p=128)
    for c in range(16):
        st = stg[c % 2]
        kb.dma("sp", st[:], v[:, c, :], reads=[srcbuf], writes=[st], sembuf=st)
        kb.op("pool", lambda e: e.tensor_copy(out=dst[:, c, :], in_=st[:]), reads=[st], writes=[dst])


def phase_out(kb, cx, l):
    mTs = cx.mTs
    with kb.phase():
        ident = load_const(kb, cx, "ident_bf", [128, 128], BF16)
        Wb = kb.sb([128, 16, D], BF16, "Wb")
        stg = [kb.sb([128, D], F32, "stg") for _ in range(2)]
        load_w_bf16(kb, Wb, cx.w_branch.t[l].rearrange("n k d -> (n k) d"), cx.w_branch, stg)
        oTt = [kb.sb([128, 16, 128], BF16, "oTt") for _ in range(2)]
        mgt = [kb.sb([128, 4 * D], BF16, "mgt") for _ in range(2)]
        sgm = [kb.sb([128, D], F32, "sgm") for _ in range(2)]
        mergedb = [[kb.sb([128, 512], F32, "merged") for _ in range(4)] for _ in range(2)]
        tmp = [kb.sb([128, 512], F32, "tmp") for _ in range(4)]
        mbf = kb.sb([128, D], BF16, "mbf")
        mst = [kb.sb([128, 16, 128], BF16, "mst") for _ in range(2)]
        pm = [kb.ps([128, 512], F32, "pm") for _ in range(4)]
        ptr = [kb.ps([128, 512], BF16, "ptr") for _ in range(2)]
        n_pm = 0
        n_sg = 0
        it = 0
        for s in range(NSEQ):
            oT, tm = cx.oT[s], cx.tm[s]
            for t in range(NT):
                k = it % 2
                it += 1
                ts_ = slice(t * 128, (t + 1) * 128)
                kb.dma("sp", oTt[k][:], oT.t[:, ts_].rearrange("(c p) t -> p c t", p=128), reads=[oT], writes=[oTt[k]], sembuf=oTt[k])
                kb.dma("sp", mgt[k][:], tm.t[ts_, C_MG:C_MG + 4 * D], reads=[tm], writes=[mgt[k]], sembuf=mgt[k])
                for n in range(4):
                    sg = sgm[n_sg % 2]
                    n_sg += 1
                    kb.op("act", lambda e: e.activation(out=sg[:], in_=mgt[k][:, n * D:(n + 1) * D], func=AF.Sigmoid), reads=[mgt[k]], writes=[sg])
                    for cb in range(4):
                        cs = slice(cb * 512, (cb + 1) * 512)
                        p = pm[n_pm % 4]
                        tp = tmp[n_pm % 4]
                        n_pm += 1
                        kb.mm(p[:], [(oTt[k][:, n * 4 + c, :], Wb[:, n * 4 + c, cs]) for c in range(4)], reads=[oTt[k], Wb], writes=[p])
                        mg_ = mergedb[k][cb]
                        if n == 0:
                            kb.op("dve", lambda e: e.tensor_tensor(out=mg_[:], in0=p[:], in1=sg[:, cs], op=ALU.mult), reads=[p, sg], writes=[mg_])
                        else:
                            kb.op("dve", lambda e: e.tensor_tensor(out=tp[:], in0=p[:], in1=sg[:, cs], op=ALU.mult), reads=[p, sg], writes=[tp])
                            kb.op("pool", lambda e: e.tensor_tensor(out=mg_[:], in0=mg_[:], in1=tp[:], op=ALU.add), reads=[mg_, tp], writes=[mg_])
                for cb in range(4):
                    kb.op("act", lambda e: e.activation(out=mbf[:, cb * 512:(cb + 1) * 512], in_=mergedb[k][cb][:], func=AF.Copy),
                          reads=[mergedb[k][cb]], writes=[mbf])
                ms = mst[k]
                for g4 in range(4):
                    pt = ptr[g4 % 2]
                    for j in range(4):
                        c = g4 * 4 + j
                        kb.op("pe", lambda e: e.transpose(out=pt[:, j * 128:(j + 1) * 128], in_=mbf[:, c * 128:(c + 1) * 128], identity=ident[:]),
                              reads=[mbf, ident], writes=[pt])
                    kb.op("act", lambda e: e.activation(out=ms[:, g4 * 4:(g4 + 1) * 4, :], in_=pt[:].rearrange("p (a b) -> p a b", a=4), func=AF.Copy),
                          reads=[pt], writes=[ms])
                c0 = s * S + t * 128
                kb.dma("act", mTs.t[:, c0:c0 + 128].rearrange("(c p) t -> p c t", p=128), ms[:], reads=[ms], writes=[mTs], sembuf=ms)
    with kb.phase():
        Wo = kb.sb([128, 16, D], BF16, "Wo")
        stg = [kb.sb([128, D], F32, "stg") for _ in range(2)]
        load_w_bf16(kb, Wo, cx.w_out.t[l], cx.w_out, stg)
        gp = kb.sb([128, D], F32, "gp")
        kb.dma("sp", gp[:], cx.post_norm_g.t[l:l + 1, :].broadcast_to([128, D]), reads=[cx.post_norm_g], writes=[gp], sembuf=gp)
        mT = [kb.sb([128, 16, 128], BF16, "mT") for _ in range(2)]
        ht = [kb.sb([128, D], F32, "ht") for _ in range(2)]
        hn = [kb.sb([128, D], F32, "hn") for _ in range(2)]
        t1 = [kb.sb([128, 512], F32, "t1") for _ in range(2)]
        jk = kb.sb([128, 512], BF16, "jk")
        sq = kb.sb([128, 2 * NT * NSEQ, 8], F32, "sq")
        py = [kb.ps([128, 512], F32, "py") for _ in range(8)]
        hin, hout = cx.h_in, cx.h_out
        it = 0
        for s in range(NSEQ):
            for t in range(NT):
                k = it % 2
                c0 = s * S + t * 128
                kb.dma("sp", mT[k][:], mTs.t[:, c0:c0 + 128].rearrange("(c p) t -> p c t", p=128), reads=[mTs], writes=[mT[k]], sembuf=mT[k])
                kb.dma("sp", ht[k][:], hin.t[c0:c0 + 128, :], reads=[hin], writes=[ht[k]], sembuf=ht[k])
                pys = py[k * 4:(k + 1) * 4]
                for cb in range(4):
                    cs = slice(cb * 512, (cb + 1) * 512)
                    kb.mm(pys[cb][:], [(mT[k][:, c, :], Wo[:, c, cs]) for c in range(16)], reads=[mT[k], Wo], writes=[pys[cb]])
                    kb.op("act", lambda e: e.activation(out=jk[:], in_=pys[cb][:], func=AF.Square, accum_out=sq[:, it, cb:cb + 1]),
                          reads=[pys[cb]], writes=[jk, sq])
                kb.op("dve", lambda e: e.tensor_reduce(out=sq[:, it, 4:5], in_=sq[:, it, 0:4], axis=AX.X, op=ALU.add), reads=[sq], writes=[sq])
                rstd_from_ss(kb, sq[:, it, 4:5], sq[:, it, 5:6], D, sq, sq)
                for cb in range(4):
                    cs = slice(cb * 512, (cb + 1) * 512)
                    tt = t1[cb % 2]
                    kb.op("dve", lambda e: e.scalar_tensor_tensor(out=tt[:], in0=pys[cb][:], scalar=sq[:, it, 5:6], in1=gp[:, cs], op0=ALU.mult, op1=ALU.mult),
                          reads=[pys[cb], sq, gp], writes=[tt])
                    kb.op("pool", lambda e: e.tensor_tensor(out=hn[k][:, cs], in0=tt[:], in1=ht[k][:, cs], op=ALU.add), reads=[tt, ht[k]], writes=[hn[k]])
                kb.dma("act", hout.t[c0:c0 + 128, :], hn[k][:], reads=[hn[k]], writes=[hout], sembuf=hn[k])
                it += 1
```

```python
import contextlib
import math
import numpy as np
import ml_dtypes
import concourse.bass as bass
import concourse.mybir as mybir
from concourse.bass_utils import run_bass_kernel_spmd

F32 = mybir.dt.float32
BF16 = mybir.dt.bfloat16
AF = mybir.ActivationFunctionType
ALU = mybir.AluOpType
AX = mybir.AxisListType

D = 2048
S = 2048
NT = S // 128
DEPTH = 4
NSEQ = 2
INC = 14860
EPS = 1e-6
O_FQ, O_FK, O_FV, O_FF, O_FG = 0, 512, 1024, 1536, 1540
O_DQ, O_DK, O_DV, O_DG = 2052, 2564, 3076, 3588
O_SZ, O_SX, O_SDT, O_SU, O_SG, O_MG = 4100, 4612, 5636, 5644, 6156, 6668
FM_SEGS = [(O_FQ, 512), (O_FK, 512), (O_DQ, 512), (O_DK, 512), (O_SX, 1024), (O_SU, 512)]
R_FQ, R_FK, R_DQ, R_DK, R_SX, R_SU = 0, 512, 1024, 1536, 2048, 3072
FM_ROWS = 3584
TM_SEGS = [(O_FV, 512), (O_FG, 512), (O_DV, 512), (O_DG, 512), (O_SZ, 512), (O_SG, 512), (O_MG, 8192)]
C_FV, C_FG, C_DV, C_DG, C_SZ, C_SG, C_MG = 0, 512, 1024, 1536, 2048, 2560, 3072
TM_COLS = 11264


class Buf:
    def __init__(self, kb, name, t):
        self.kb = kb
        self.name = name
        self.t = t
        self.lw = []
        self.rd = []
        self.dsem = None

    def __getitem__(self, k):
        return self.t[k]


class KB:
    ENG = ("pe", "act", "dve", "pool", "sp")

    def __init__(self, nc, es):
        self.nc = nc
        self.es = es
        self.e = {"pe": nc.tensor, "act": nc.scalar, "dve": nc.vector, "pool": nc.gpsimd, "sp": nc.sync}
        self.sem = {}
        self.cnt = {}
        self.seen = {k: {} for k in self.ENG}
        self.semval = {}
        self.semobj = {}
        self.epoch = 0
        self.new_epoch()
        self.dma_free = []
        self.dma_all = []
        for i in range(64):
            s = es.enter_context(nc.semaphore(f"dq{i}"))
            nm = f"dq{i}"
            self.semobj[nm] = s
            self.semval[nm] = 0
            self.dma_free.append(nm)
            self.dma_all.append(nm)
        self.bar = es.enter_context(nc.semaphore("bar"))
        self.barv = 0
        self.phase_bufs = []
        self.uid = 0

    def new_epoch(self):
        self.epoch += 1
        for k in self.ENG:
            nm = f"e{self.epoch}_{k}"
            s = self.es.enter_context(self.nc.semaphore(nm))
            self.semobj[nm] = s
            self.semval[nm] = 0
            self.sem[k] = nm

    def sb(self, shape, dt, name=None):
        self.uid += 1
        name = f"{name or 'sb'}_{self.uid}"
        t = self.pes.enter_context(self.nc.sbuf_tensor(name, list(shape), dt))
        b = Buf(self, name, t)
        self.phase_bufs.append(b)
        return b

    def ps(self, shape, dt=F32, name=None):
        self.uid += 1
        name = f"{name or 'ps'}_{self.uid}"
        t = self.pes.enter_context(self.nc.psum_tensor(name, list(shape), dt))
        b = Buf(self, name, t)
        self.phase_bufs.append(b)
        return b

    def dram(self, name, t):
        return Buf(self, name, t)

    @contextlib.contextmanager
    def phase(self):
        self.pes = contextlib.ExitStack()
        self.phase_bufs = []
        try:
            yield
            self.barrier()
        finally:
            for b in self.phase_bufs:
                if b.dsem is not None:
                    self.dma_free.append(b.dsem)
                    b.dsem = None
            self.pes.close()
            self.pes = None

    def _wait(self, eng, toks):
        need = {}
        for (s, v) in toks:
            if need.get(s, 0) < v:
                need[s] = v
        for s, v in need.items():
            if self.seen[eng].get(s, 0) >= v:
                continue
            self.e[eng].wait_ge(self.semobj[s], v)
            self.seen[eng][s] = v

    def _deps(self, reads, writes):
        toks = []
        for b in reads:
            toks += b.lw
        for b in writes:
            toks += b.lw
            toks += b.rd
        return toks

    def op(self, eng, fn, reads=(), writes=()):
        self._wait(eng, self._deps(reads, writes))
        ins = fn(self.e[eng])
        s = self.sem[eng]
        self.semval[s] += 1
        ins.then_inc(self.semobj[s], 1)
        tok = (s, self.semval[s])
        self._record(tok, reads, writes)
        return tok

    def _record(self, tok, reads, writes):
        for b in reads:
            b.rd.append(tok)
            if len(b.rd) > 64:
                b.rd = self._compact(b.rd)
        for b in writes:
            b.lw = [tok]
            b.rd = []

    @staticmethod
    def _compact(toks):
        need = {}
        for (s, v) in toks:
            if need.get(s, 0) < v:
                need[s] = v
        return list(need.items())

    def mm(self, out_ap, pairs, reads, writes, start=True, stop=True):
        eng = "pe"
        self._wait(eng, self._deps(reads, writes))
        n = len(pairs)
        ins = None
        for i, (l, r) in enumerate(pairs):
            ins = self.nc.tensor.matmul(out_ap, l, r, start=(start and i == 0), stop=(stop and i == n - 1))
        s = self.sem[eng]
        self.semval[s] += 1
        ins.then_inc(self.semobj[s], 1)
        tok = (s, self.semval[s])
        self._record(tok, reads, writes)
        return tok

    def dma(self, q, out_ap, in_ap, reads=(), writes=(), sembuf=None, **kw):
        b = sembuf
        if b.dsem is None:
            b.dsem = self.dma_free.pop(0)
        self._wait(q, self._deps(reads, writes))
        ins = self.e[q].dma_start(out=out_ap, in_=in_ap, **kw)
        s = b.dsem
        self.semval[s] += 16
        ins.then_inc(self.semobj[s], 16)
        tok = (s, self.semval[s])
        for r in reads:
            r.rd.append(tok)
        for w in writes:
            if w.lw and all(t[0] == s for t in w.lw):
                w.lw = [tok]
            else:
                w.lw = [tok]
            w.rd = []
        return tok

    def barrier(self):
        toks = [(s, v) for s, v in self.semval.items() if v > 0 and not s.startswith("bar")]
        cur = set(self.sem.values()) | set(self.dma_all)
        toks = [(s, v) for (s, v) in toks if s in cur]
        self._wait("sp", toks)
        self.barv += 1
        self.nc.sync.sem_inc(self.bar, 1)
        for k in self.ENG:
            if k == "sp":
                continue
            self.e[k].wait_ge(self.bar, self.barv)
            for (s, v) in toks:
                self.seen[k][s] = max(self.seen[k].get(s, 0), v)


def bcast_ap(t_ap, shape):
    return t_ap.broadcast_to(list(shape))


class Ctx:
    pass


def load_const(kb, cx, name, shape, dt):
    b = kb.sb(shape, dt, name=name)
    kb.dma("sp", b[:], cx.cst[name].t[:], reads=[cx.cst[name]], writes=[b], sembuf=b)
    return b


def phase_A(kb, cx, l, s):
    nc = kb.nc
    with kb.phase():
        ident = load_const(kb, cx, "ident_bf", [128, 128], BF16)
        xnT = kb.sb([128, 16, S], BF16, "xnT")
        gb = kb.sb([128, D], F32, "gb")
        kb.dma("sp", gb[:], cx.pre_g.t[l:l + 1, :].broadcast_to([128, D]), reads=[cx.pre_g], writes=[gb], sembuf=gb)
        hts = [kb.sb([128, D], F32, "ht") for _ in range(2)]
        junk = kb.sb([128, D], BF16, "junk")
        xss = [kb.sb([128, D], BF16, "xs") for _ in range(2)]
        ss = kb.sb([128, NT], F32, "ss")
        rs = kb.sb([128, NT], F32, "rs")
        ptr = [kb.ps([128, 512], BF16, "ptr") for _ in range(2)]
        pmm = [kb.ps([128, 512], F32, "pmm") for _ in range(4)]
        hin = cx.h_in
        for t in range(NT if cx.stop >= 2 else 0):
            ht = hts[t % 2]
            xs = xss[t % 2]
            r0 = s * S + t * 128
            kb.dma("sp", ht[:], hin.t[r0:r0 + 128, :], reads=[hin], writes=[ht], sembuf=ht)
            kb.op("act", lambda e: e.activation(out=junk[:], in_=ht[:], func=AF.Square, accum_out=ss[:, t:t + 1]),
                  reads=[ht], writes=[junk, ss])
            kb.op("dve", lambda e: e.tensor_scalar(out=rs[:, t:t + 1], in0=ss[:, t:t + 1], scalar1=1.0 / D, scalar2=EPS,
                                                   op0=ALU.mult, op1=ALU.add), reads=[ss], writes=[rs])
            kb.op("act", lambda e: e.activation(out=rs[:, t:t + 1], in_=rs[:, t:t + 1], func=AF.Sqrt), reads=[rs], writes=[rs])
            kb.op("dve", lambda e: e.reciprocal(out=rs[:, t:t + 1], in_=rs[:, t:t + 1]), reads=[rs], writes=[rs])
            kb.op("dve", lambda e: e.scalar_tensor_tensor(out=xs[:], in0=ht[:], scalar=rs[:, t:t + 1], in1=gb[:],
                                                          op0=ALU.mult, op1=ALU.mult), reads=[ht, rs, gb], writes=[xs])
            for g4 in range(4):
                pt = ptr[g4 % 2]
                for j in range(4):
                    c = g4 * 4 + j
                    kb.op("pe", lambda e: e.transpose(out=pt[:, j * 128:(j + 1) * 128], in_=xs[:, c * 128:(c + 1) * 128],
                                                      identity=ident[:]), reads=[xs, ident], writes=[pt])
                dst = xnT[:, g4 * 4:(g4 + 1) * 4, t * 128:(t + 1) * 128]
                src = pt[:].rearrange("p (a b) -> p a b", a=4)
                if g4 % 2 == 0:
                    kb.op("dve", lambda e: e.tensor_copy(out=dst, in_=src), reads=[pt], writes=[xnT])
                else:
                    kb.op("act", lambda e: e.activation(out=dst, in_=src, func=AF.Copy), reads=[pt], writes=[xnT])

        W = cx.w_in
        wst = [kb.sb([128, 4, 512], F32, "wst") for _ in range(2)]
        wbfs = [kb.sb([128, 16, 512], BF16, "wbf") for _ in range(2)]
        stfm = [kb.sb([128, S], BF16, "stfm") for _ in range(2)]
        sttm = [kb.sb([128, 512], BF16, "sttm") for _ in range(3)]
        blocks = []
        for (wc, n), r in zip(FM_SEGS, [R_FQ, R_FK, R_DQ, R_DK, R_SX, R_SU]):
            for k in range(n // 512):
                blocks.append(("fm", wc + k * 512, r + k * 512))
        col = 0
        for (wc, n) in TM_SEGS:
            for k in range(n // 512):
                blocks.append(("tm", wc + k * 512, col))
                col += 512
        if cx.dbg_nblocks is not None:
            blocks = blocks[:cx.dbg_nblocks[0]] + [b for b in blocks if b[0] == "tm"][:cx.dbg_nblocks[1]]
        fmT, tm, sm = cx.fmT[s], cx.tm[s], cx.sm[s]
        state = {"k": 0, "pm": 0, "sf": 0, "st": 0}

        def load_block(i):
            kind, wc, dst = blocks[i]
            wbf = wbfs[i % 2]
            for piece in range(4):
                st = wst[state["k"] % 2]
                state["k"] += 1
                src = W.t[l, piece * 512:(piece + 1) * 512, wc:wc + 512].rearrange("(c p) n -> p c n", p=128)
                kb.dma("sp", st[:], src, reads=[W], writes=[st], sembuf=st)
                kb.op("pool", lambda e: e.tensor_copy(out=wbf[:, piece * 4:(piece + 1) * 4, :], in_=st[:]),
                      reads=[st], writes=[wbf])

        if cx.stop < 3:
            return
        wsm_f = kb.sb([128, 16, 12], F32, "wsmf")
        wsm = kb.sb([128, 16, 12], BF16, "wsm")
        smst = kb.sb([128, NT, 12], F32, "smst")
        for (wc, n, o) in ((O_FF, 4, 0), (O_SDT, 8, 4)):
            src = W.t[l, :, wc:wc + n].rearrange("(c p) n -> p c n", p=128)
            kb.dma("sp", wsm_f[:, :, o:o + n], src, reads=[W], writes=[wsm_f], sembuf=wsm_f, allow_slow_non_contiguous=True)
        kb.op("pool", lambda e: e.tensor_copy(out=wsm[:], in_=wsm_f[:]), reads=[wsm_f], writes=[wsm])
        for t in range(NT):
            pm = pmm[state["pm"] % 4]
            state["pm"] += 1
            kb.mm(pm[:, 0:12], [(xnT[:, c, t * 128:(t + 1) * 128], wsm[:, c, :]) for c in range(16)],
                  reads=[xnT, wsm], writes=[pm])
            kb.op("act", lambda e: e.activation(out=smst[:, t, :], in_=pm[:, 0:12], func=AF.Copy), reads=[pm], writes=[smst])
        kb.dma("act", sm.t[:, 0:12].rearrange("(t p) n -> p t n", p=128), smst[:], reads=[smst], writes=[sm], sembuf=smst,
               allow_slow_non_contiguous=True)

        if cx.stop < 4:
            return
        load_block(0)
        for i, (kind, wc, dst) in enumerate(blocks):
            if i + 1 < len(blocks):
                load_block(i + 1)
            wbf = wbfs[i % 2]
            if kind == "fm":
                for j in range(4):
                    sf = stfm[state["sf"] % 2]
                    state["sf"] += 1
                    for tb in range(4):
                        pm = pmm[state["pm"] % 4]
                        state["pm"] += 1
                        kb.mm(pm[:], [(wbf[:, c, j * 128:(j + 1) * 128], xnT[:, c, tb * 512:(tb + 1) * 512]) for c in range(16)],
                              reads=[xnT, wbf], writes=[pm])
                        kb.op("act", lambda e: e.activation(out=sf[:, tb * 512:(tb + 1) * 512], in_=pm[:], func=AF.Copy),
                              reads=[pm], writes=[sf])
                    kb.dma("act", fmT.t[dst + j * 128:dst + (j + 1) * 128, :], sf[:], reads=[sf], writes=[fmT], sembuf=sf)
            else:
                for t in range(NT):
                    pm = pmm[state["pm"] % 4]
                    state["pm"] += 1
                    stt = sttm[state["st"] % 3]
                    state["st"] += 1
                    kb.mm(pm[:], [(xnT[:, c, t * 128:(t + 1) * 128], wbf[:, c, :]) for c in range(16)],
                          reads=[xnT, wbf], writes=[pm])
                    if t % 2 == 0:
                        kb.op("act", lambda e: e.activation(out=stt[:], in_=pm[:], func=AF.Copy), reads=[pm], writes=[stt])
                    else:
                        kb.op("dve", lambda e: e.tensor_copy(out=stt[:], in_=pm[:]), reads=[pm], writes=[stt])
                    kb.dma("act", tm.t[t * 128:(t + 1) * 128, dst:dst + 512], stt[:], reads=[stt], writes=[tm], sembuf=stt)


BF = ml_dtypes.bfloat16


def make_consts():
    c = {}
    c["ident_bf"] = np.eye(128, dtype=np.float32).astype(BF)
    c["ident_f"] = np.eye(128, dtype=np.float32)
    k = np.arange(128)[:, None]
    q = np.arange(128)[None, :]
    c["mask_le_bf"] = (k <= q).astype(np.float32).astype(BF)
    c["tri_le_f"] = (k <= q).astype(np.float32)
    c["gt_f"] = (k > q).astype(np.float32)
    c["negmask_f"] = np.where(q < k, -30000.0, 0.0).astype(np.float32)
    c["ones_f"] = np.ones((128, 128), np.float32)
    d = np.arange(128) % 32
    inv = (10000.0 ** (-d.astype(np.float64) / 32.0))
    ang = inv[:, None] * np.arange(S, dtype=np.float64)[None, :]
    c["rope_cos"] = np.cos(ang).astype(np.float32)
    c["rope_sin"] = np.sin(ang).astype(np.float32)
    RT = np.zeros((128, 128), np.float32)
    for p in range(128):
        dd = p % 64
        if dd < 32:
            RT[p + 32, p] = -1.0
        else:
            RT[p - 32, p] = 1.0
    c["rope_rt"] = RT.astype(BF)
    gm = np.zeros((128, 8), np.float32)
    for r in range(128):
        gm[r, r // 16] = 1.0
    c["gmask"] = gm
    return c


CONST_DT = {"ident_bf": BF16, "mask_le_bf": BF16, "rope_rt": BF16}

PARAMS = ["w_in", "b_fox_f", "pre_norm_g", "post_norm_g", "diff_lambda", "diff_subln_g", "ssd_conv_w", "ssd_conv_b",
          "ssd_dt_bias", "ssd_a_log", "ssd_d", "ssd_norm_g", "s5_a_re", "s5_a_im", "s5_b_re", "s5_b_im", "s5_c_re",
          "s5_c_im", "s5_d", "s5_log_dt", "s5_w_glu", "w_branch", "w_out"]


def build(shapes, layers=range(DEPTH), phases="ABCDEF", dbg=None, dbg_nblocks=None, stop=99, stop2=0):
    dbg = dbg or {}
    nc = bass.Bass("TRN2", target_bir_lowering=False)
    es = contextlib.ExitStack()
    kb = KB(nc, es)
    cx = Ctx()
    cx.dbg_nblocks = dbg_nblocks
    cx.stop = stop
    cx.stop2 = stop2
    x = nc.dram_tensor("x", [NSEQ * S, D], F32, kind="ExternalInput")
    out = nc.dram_tensor("out", [NSEQ * S, D], F32, kind="ExternalOutput")
    cx.cst = {}
    for k, v in make_consts().items():
        cx.cst[k] = kb.dram(k, nc.dram_tensor("c_" + k, list(v.shape), CONST_DT.get(k, F32), kind="ExternalInput"))
    for p in PARAMS:
        setattr(cx, p, kb.dram(p, nc.dram_tensor(p, list(shapes[p]), F32, kind="ExternalInput")))
    cx.pre_g = cx.pre_norm_g

    def scratch(name, shape, dt):
        kind = {"in": "ExternalInput", "out": "ExternalOutput"}.get(dbg.get(name), "Internal")
        return kb.dram(name, nc.dram_tensor(name, shape, dt, kind=kind))

    cx.fmT = [scratch(f"fmT{s}", [FM_ROWS, S], BF16) for s in range(NSEQ)]
    cx.tm = [scratch(f"tm{s}", [S, TM_COLS], BF16) for s in range(NSEQ)]
    cx.sm = [scratch(f"sm{s}", [S, 16], F32) for s in range(NSEQ)]
    cx.oT = [scratch(f"oT{s}", [4 * 512, S], BF16) for s in range(NSEQ)]
    cx.mTs = scratch("mTs", [D, NSEQ * S], BF16)
    cx.Etab = scratch("Etab", [16, 2, 128, S], F32)
    hA = scratch("hA", [NSEQ * S, D], F32)
    hB = scratch("hB", [NSEQ * S, D], F32)
    xb = kb.dram("x", x)
    ob = kb.dram("out", out)
    layers = list(layers)
    if "E" in phases:
        s5_alloc(kb, cx)
    for li, l in enumerate(layers):
        cx.h_in = xb if li == 0 else (hA if li % 2 == 1 else hB)
        cx.h_out = ob if li == len(layers) - 1 else (hA if li % 2 == 0 else hB)
        cx.l = l
        if "E" in phases:
            phase_s5_prep(kb, cx, l)
        for s in range(NSEQ):
            if "A" in phases:
                phase_A(kb, cx, l, s)
            if "B" in phases:
                phase_fox(kb, cx, l, s)
            if "C" in phases:
                phase_diff(kb, cx, l, s)
            if "D" in phases:
                phase_ssd(kb, cx, l, s)
            if "E" in phases:
                phase_s5(kb, cx, l, s)
        if "F" in phases:
            phase_out(kb, cx, l)
        if li + 1 < len(layers):
            kb.new_epoch()
    es.close()
    return nc


_CACHE = {}


def kernel(**inputs):
    x = np.ascontiguousarray(inputs["x"], dtype=np.float32)
    shapes = {p: inputs[p].shape for p in PARAMS}
    key = "full"
    if key not in _CACHE:
        _CACHE[key] = build(shapes)
    nc = _CACHE[key]
    consts = make_consts()
    base = {p: np.ascontiguousarray(inputs[p], dtype=np.float32) for p in PARAMS}
    for k, v in consts.items():
        base["c_" + k] = v
    in_maps = []
    for c in range(8):
        m = dict(base)
        m["x"] = x[c * NSEQ:(c + 1) * NSEQ].reshape(NSEQ * S, D)
        in_maps.append(m)
    res = run_bass_kernel_spmd(nc, in_maps, core_ids=list(range(8)))
    outs = [np.asarray(r["out"]).reshape(NSEQ, S, D) for r in res.results]
    return np.concatenate(outs, axis=0).astype(np.float32)


def attn_core(kb, cx, maps, nheads, vaug, finalize, scale, pools):
    sc_ps, pt_sb, po_ps, maskT = pools
    st = {"sc": 0, "po": 0}
    for h in range(nheads):
        for i in range(NT):
            pos = []
            for mi, mp in enumerate(maps):
                po = po_ps[st["po"] % len(po_ps)]
                st["po"] += 1
                pos.append(po)
                groups = [list(range(j0, min(j0 + 4, i + 1))) for j0 in range(0, i + 1, 4)]
                pend = None
                for gi, js in enumerate(groups):
                    sc = sc_ps[st["sc"] % len(sc_ps)]
                    pt = pt_sb[st["sc"] % len(pt_sb)]
                    st["sc"] += 1
                    for jj, j in enumerate(js):
                        kb.mm(sc[:, jj * 128:(jj + 1) * 128], [(mp["kT"](h, j), mp["qT"](h, i))], reads=mp["reads"], writes=[sc])
                    if mp["bias"] is None:
                        n = len(js) * 128
                        kb.op("act", lambda e: e.activation(out=pt[:, 0:n], in_=sc[:, 0:n], func=AF.Exp, scale=scale),
                              reads=[sc], writes=[pt])
                    else:
                        for jj, j in enumerate(js):
                            kb.op("act", lambda e: e.activation(out=pt[:, jj * 128:(jj + 1) * 128], in_=sc[:, jj * 128:(jj + 1) * 128],
                                                                func=AF.Exp, scale=scale, bias=mp["bias"](h, i, j)),
                                  reads=[sc] + mp["bias_reads"], writes=[pt])
                    if js[-1] == i:
                        jj = len(js) - 1
                        kb.op("dve", lambda e: e.tensor_tensor(out=pt[:, jj * 128:(jj + 1) * 128], in0=pt[:, jj * 128:(jj + 1) * 128],
                                                               in1=maskT[:], op=ALU.mult), reads=[pt, maskT], writes=[pt])
                    if pend is not None:
                        _pv(kb, pend, po, vaug, h, i)
                    pend = (js, pt)
                _pv(kb, pend, po, vaug, h, i)
            finalize(h, i, pos)


def _pv(kb, pend, po, vaug, h, i):
    js, pt = pend
    for jj, j in enumerate(js):
        kb.nc.tensor
        kb._wait("pe", kb._deps([pt, vaug], [po] if j == 0 else []))
        ins = kb.nc.tensor.matmul(po[:, 0:129], pt[:, jj * 128:(jj + 1) * 128], vaug[:, j, h, :], start=(j == 0), stop=(j == i))
        s = kb.sem["pe"]
        kb.semval[s] += 1
        ins.then_inc(kb.semobj[s], 1)
        tok = (s, kb.semval[s])
        pt.rd.append(tok)
        vaug.rd.append(tok)
        if j == 0:
            po.rd = []
        po.lw = [tok]


def load_vaug(kb, cx, s, col0, name):
    vaug = kb.sb([128, NT, 4, 129], BF16, name)
    kb.op("pool", lambda e: e.memset(vaug[:, :, :, 128:129], 1.0), reads=[], writes=[vaug])
    tm = cx.tm[s]
    for t in range(NT):
        src = tm.t[t * 128:(t + 1) * 128, col0:col0 + 512].rearrange("p (h d) -> p h d", h=4)
        kb.dma("sp", vaug[:, t, :, 0:128], src, reads=[tm], writes=[vaug], sembuf=vaug)
    return vaug


def transpose_out(kb, cx, o_tm, ident, ptr, oTs, i):
    for c in range(4):
        kb.op("pe", lambda e: e.transpose(out=ptr[:, c * 128:(c + 1) * 128], in_=o_tm[:, c * 128:(c + 1) * 128], identity=ident[:]),
              reads=[o_tm, ident], writes=[ptr])
    kb.op("act", lambda e: e.activation(out=oTs[:, :, i * 128:(i + 1) * 128], in_=ptr[:].rearrange("p (a b) -> p a b", a=4), func=AF.Copy),
          reads=[ptr], writes=[oTs])


def phase_fox(kb, cx, l, s):
    nc = kb.nc
    with kb.phase():
        ident = load_const(kb, cx, "ident_bf", [128, 128], BF16)
        maskT = load_const(kb, cx, "mask_le_bf", [128, 128], BF16)
        tri = load_const(kb, cx, "tri_le_f", [128, 128], F32)
        ones = load_const(kb, cx, "ones_f", [128, 128], F32)
        fmT, tm, sm = cx.fmT[s], cx.tm[s], cx.sm[s]
        qT = kb.sb([128, 4, S], BF16, "qT")
        kT = kb.sb([128, 4, S], BF16, "kT")
        kb.dma("sp", qT[:], fmT.t[R_FQ:R_FQ + 512, :].rearrange("(h p) t -> p h t", p=128), reads=[fmT], writes=[qT], sembuf=qT)
        kb.dma("sp", kT[:], fmT.t[R_FK:R_FK + 512, :].rearrange("(h p) t -> p h t", p=128), reads=[fmT], writes=[kT], sembuf=kT)
        vaug = load_vaug(kb, cx, s, C_FV, "vaugf")
        ff = kb.sb([128, NT, 4], F32, "ff")
        kb.dma("sp", ff[:], sm.t[:, 0:4].rearrange("(t p) n -> p t n", p=128), reads=[sm], writes=[ff], sembuf=ff,
               allow_slow_non_contiguous=True)
        bb = kb.sb([128, 4], F32, "bb")
        kb.dma("sp", bb[:], cx.b_fox_f.t[l:l + 1, :].broadcast_to([128, 4]), reads=[cx.b_fox_f], writes=[bb], sembuf=bb)
        lf = kb.sb([128, NT, 4], F32, "lf")
        kb.op("dve", lambda e: e.tensor_tensor(out=lf[:], in0=ff[:], in1=bb[:].unsqueeze(1).broadcast_to([128, NT, 4]), op=ALU.add),
              reads=[ff, bb], writes=[lf])
        kb.op("act", lambda e: e.activation(out=lf[:], in_=lf[:], func=AF.Exp, scale=-1.0), reads=[lf], writes=[lf])
        kb.op("act", lambda e: e.activation(out=lf[:], in_=lf[:], func=AF.Ln, bias=1.0), reads=[lf], writes=[lf])
        sc_ps = [kb.ps([128, 512], F32, "sc") for _ in range(3)]
        pF, pT = sc_ps[0], sc_ps[1]
        lf2 = lf[:].rearrange("p t h -> p (t h)")
        kb.mm(pF[:, 0:64], [(tri[:], lf2)], reads=[tri, lf], writes=[pF])
        kb.mm(pT[:, 0:64], [(ones[:], lf2)], reads=[ones, lf], writes=[pT])
        Lloc = kb.sb([128, NT, 4], F32, "Lloc")
        tot = kb.sb([128, NT, 4], F32, "tot")
        Lend = kb.sb([128, NT, 4], F32, "Lend")
        Lc = kb.sb([128, NT, 4], F32, "Lc")
        kb.op("dve", lambda e: e.tensor_copy(out=Lloc[:].rearrange("p t h -> p (t h)"), in_=pF[:, 0:64]), reads=[pF], writes=[Lloc])
        kb.op("dve", lambda e: e.tensor_copy(out=tot[:].rearrange("p t h -> p (t h)"), in_=pT[:, 0:64]), reads=[pT], writes=[tot])
        kb.op("dve", lambda e: e.tensor_copy(out=Lend[:, 0, :], in_=tot[:, 0, :]), reads=[tot], writes=[Lend])
        for t in range(1, NT):
            kb.op("dve", lambda e: e.tensor_tensor(out=Lend[:, t, :], in0=Lend[:, t - 1, :], in1=tot[:, t, :], op=ALU.add),
                  reads=[Lend, tot], writes=[Lend])
        kb.op("dve", lambda e: e.tensor_tensor(out=Lc[:], in0=Lend[:], in1=tot[:], op=ALU.subtract), reads=[Lend, tot], writes=[Lc])
        kb.op("dve", lambda e: e.tensor_tensor(out=Lc[:], in0=Lc[:], in1=Lloc[:], op=ALU.add), reads=[Lc, Lloc], writes=[Lc])
        bm = kb.sb([128, 4, NT, NT], F32, "bm")
        for h in range(4):
            for i in range(NT):
                kb.op("dve", lambda e: e.tensor_scalar(out=bm[:, h, i, 0:i + 1], in0=Lc[:, 0:i + 1, h], scalar1=Lend[:, i, h:h + 1],
                                                       scalar2=None, op0=ALU.subtract), reads=[Lc, Lend], writes=[bm])
        pt_sb = [kb.sb([128, 512], BF16, "pt") for _ in range(3)]
        po_ps = [kb.ps([128, 512], F32, "po") for _ in range(4)]
        ptr = kb.ps([128, 512], BF16, "ptr")
        oTs = kb.sb([128, 4, S], BF16, "oTs")
        otm = kb.sb([128, NT, 512], BF16, "otm")
        sg = kb.sb([128, NT, 512], F32, "sg")
        gin = kb.sb([128, NT, 512], BF16, "gin")
        kb.dma("sp", gin[:], tm.t[:, C_FG:C_FG + 512].rearrange("(t p) n -> p t n", p=128), reads=[tm], writes=[gin], sembuf=gin)
        kb.op("act", lambda e: e.activation(out=sg[:], in_=gin[:], func=AF.Silu), reads=[gin], writes=[sg])
        rec = kb.sb([128, 4 * NT], F32, "rec")

        def fin(h, i, pos):
            po = pos[0]
            r = rec[:, h * NT + i:h * NT + i + 1]
            kb.op("dve", lambda e: e.reciprocal(out=r, in_=po[:, 128:129]), reads=[po], writes=[rec])
            kb.op("dve", lambda e: e.scalar_tensor_tensor(out=otm[:, i, h * 128:(h + 1) * 128], in0=po[:, 0:128], scalar=r,
                                                          in1=sg[:, i, h * 128:(h + 1) * 128], op0=ALU.mult, op1=ALU.mult),
                  reads=[po, rec, sg], writes=[otm])

        maps = [dict(kT=lambda h, j: kT[:, h, j * 128:(j + 1) * 128], qT=lambda h, i: qT[:, h, i * 128:(i + 1) * 128],
                     bias=lambda h, i, j: bm[:, h, i, j:j + 1], bias_reads=[bm], reads=[kT, qT])]
        attn_core(kb, cx, maps, 4, vaug, fin, 128 ** -0.5, (sc_ps, pt_sb, po_ps, maskT))
        for i in range(NT):
            transpose_out(kb, cx, otm[:, i, :], ident, ptr, oTs, i) if False else None
        for i in range(NT):
            for c in range(4):
                kb.op("pe", lambda e: e.transpose(out=ptr[:, c * 128:(c + 1) * 128], in_=otm[:, i, c * 128:(c + 1) * 128], identity=ident[:]),
                      reads=[otm, ident], writes=[ptr])
            kb.op("act", lambda e: e.activation(out=oTs[:, :, i * 128:(i + 1) * 128], in_=ptr[:].rearrange("p (a b) -> p a b", a=4),
                                                func=AF.Copy), reads=[ptr], writes=[oTs])
        oT = cx.oT[s]
        kb.dma("sp", oT.t[0:512, :].rearrange("(c p) t -> p c t", p=128), oTs[:], reads=[oTs], writes=[oT], sembuf=oTs)


def rstd_from_ss(kb, ss_ap, out_ap, n, ssb, outb):
    kb.op("dve", lambda e: e.tensor_scalar(out=out_ap, in0=ss_ap, scalar1=1.0 / n, scalar2=EPS, op0=ALU.mult, op1=ALU.add),
          reads=[ssb], writes=[outb])
    kb.op("act", lambda e: e.activation(out=out_ap, in_=out_ap, func=AF.Ln), reads=[outb], writes=[outb])
    kb.op("act", lambda e: e.activation(out=out_ap, in_=out_ap, func=AF.Exp, scale=-0.5), reads=[outb], writes=[outb])


def phase_diff(kb, cx, l, s):
    lambda_init = 0.8 - 0.6 * math.exp(-0.3 * l)
    with kb.phase():
        ident = load_const(kb, cx, "ident_bf", [128, 128], BF16)
        maskT = load_const(kb, cx, "mask_le_bf", [128, 128], BF16)
        rt = load_const(kb, cx, "rope_rt", [128, 128], BF16)
        rcos = load_const(kb, cx, "rope_cos", [128, S], F32)
        rsin = load_const(kb, cx, "rope_sin", [128, S], F32)
        fmT, tm = cx.fmT[s], cx.tm[s]
        sc_ps = [kb.ps([128, 512], F32, "sc") for _ in range(3)]
        pt_sb = [kb.sb([128, 512], BF16, "pt") for _ in range(3)]
        po_ps = [kb.ps([128, 512], F32, "po") for _ in range(4)]
        ptr = kb.ps([128, 512], BF16, "ptr")
        raw = kb.sb([128, 4, S], BF16, "raw")
        qz = [kb.sb([128, 4, S], BF16, f"qz{m}") for m in range(2)]
        for m in range(2):
            kb.op("pool", lambda e: e.memset(qz[m][:], 0.0), reads=[], writes=[qz[m]])
        qr = None
        kr = kb.sb([128, 4, S], BF16, "kr")
        t1s = [kb.sb([128, 512], F32, "t1") for _ in range(2)]
        t2s = [kb.sb([128, 512], F32, "t2") for _ in range(2)]
        n = 0
        for (r0, dst) in ((R_DQ, qr), (R_DK, kr)):
            kb.dma("sp", raw[:], fmT.t[r0:r0 + 512, :].rearrange("(h p) t -> p h t", p=128), reads=[fmT], writes=[raw], sembuf=raw)
            for h in range(4):
                for tb in range(4):
                    ps = sc_ps[n % 2]
                    t1 = t1s[n % 2]
                    t2 = t2s[n % 2]
                    n += 1
                    cs = slice(tb * 512, (tb + 1) * 512)
                    kb.mm(ps[:], [(rt[:], raw[:, h, cs])], reads=[rt, raw], writes=[ps])
                    kb.op("dve", lambda e: e.tensor_tensor(out=t1[:], in0=ps[:], in1=rsin[:, cs], op=ALU.mult), reads=[ps, rsin], writes=[t1])
                    kb.op("pool", lambda e: e.tensor_tensor(out=t2[:], in0=raw[:, h, cs], in1=rcos[:, cs], op=ALU.mult),
                          reads=[raw, rcos], writes=[t2])
                    if dst is None:
                        for m in range(2):
                            ps_ = slice(m * 64, (m + 1) * 64)
                            kb.op("dve", lambda e: e.tensor_tensor(out=qz[m][ps_, h, cs], in0=t1[ps_, :], in1=t2[ps_, :], op=ALU.add),
                                  reads=[t1, t2], writes=[qz[m]])
                    else:
                        kb.op("dve", lambda e: e.tensor_tensor(out=dst[:, h, cs], in0=t1[:], in1=t2[:], op=ALU.add), reads=[t1, t2], writes=[dst])
        vaug = load_vaug(kb, cx, s, C_DV, "vaugd")
        lpb = kb.sb([128, 256], F32, "lpb")
        kb.dma("sp", lpb[:], cx.diff_lambda.t[l:l + 1].rearrange("o a b -> o (a b)").broadcast_to([128, 256]), reads=[cx.diff_lambda],
               writes=[lpb], sembuf=lpb)
        lt = kb.sb([128, 8], F32, "lt")
        lj = kb.sb([128, 64], F32, "lj")
        for k in range(2):
            kb.op("dve", lambda e: e.scalar_tensor_tensor(out=lj[:], in0=lpb[:, k * 128:k * 128 + 64], scalar=1.0, in1=lpb[:, k * 128 + 64:k * 128 + 128],
                                                          op0=ALU.mult, op1=ALU.mult, accum_out=lt[:, k:k + 1]), reads=[lpb], writes=[lj, lt])
        kb.op("act", lambda e: e.activation(out=lt[:, 2:4], in_=lt[:, 0:2], func=AF.Exp), reads=[lt], writes=[lt])
        kb.op("dve", lambda e: e.tensor_tensor(out=lt[:, 4:5], in0=lt[:, 3:4], in1=lt[:, 2:3], op=ALU.subtract), reads=[lt], writes=[lt])
        kb.op("dve", lambda e: e.tensor_scalar(out=lt[:, 5:6], in0=lt[:, 4:5], scalar1=-lambda_init, scalar2=None, op0=ALU.add), reads=[lt], writes=[lt])
        nlam = lt[:, 5:6]
        gsb = kb.sb([128, 128], F32, "gsb")
        kb.dma("sp", gsb[:], cx.diff_subln_g.t[l:l + 1, :].broadcast_to([128, 128]), reads=[cx.diff_subln_g], writes=[gsb], sembuf=gsb)
        kb.op("dve", lambda e: e.tensor_scalar(out=gsb[:], in0=gsb[:], scalar1=1.0 - lambda_init, scalar2=None, op0=ALU.mult), reads=[gsb], writes=[gsb])
        oTs = kb.sb([128, 4, S], BF16, "oTs")
        otm = kb.sb([128, NT, 512], BF16, "otm")
        sg = kb.sb([128, NT, 512], F32, "sg")
        gin_v = raw[:].rearrange("p h t -> p (h t)").rearrange("p (t n) -> p t n", n=512)
        kb.dma("sp", gin_v, tm.t[:, C_DG:C_DG + 512].rearrange("(t p) n -> p t n", p=128), reads=[tm], writes=[raw], sembuf=raw)
        kb.op("act", lambda e: e.activation(out=sg[:], in_=gin_v, func=AF.Silu), reads=[raw], writes=[sg])
        sm4 = [kb.sb([128, 8], F32, "sm4") for _ in range(2)]
        o2s = [kb.sb([128, 128], F32, "o2s") for _ in range(2)]
        ob = [kb.sb([128, 128], F32, "ob") for _ in range(2)]
        jk = kb.sb([128, 128], F32, "jk")
        cnt = {"n": 0}

        def fin(h, i, pos):
            k = cnt["n"] % 2
            cnt["n"] += 1
            po1, po2 = pos
            v = sm4[k]
            kb.op("dve", lambda e: e.reciprocal(out=v[:, 0:1], in_=po1[:, 128:129]), reads=[po1], writes=[v])
            kb.op("dve", lambda e: e.reciprocal(out=v[:, 1:2], in_=po2[:, 128:129]), reads=[po2], writes=[v])
            kb.op("dve", lambda e: e.tensor_tensor(out=v[:, 2:3], in0=v[:, 1:2], in1=nlam, op=ALU.mult), reads=[v, lt], writes=[v])
            kb.op("dve", lambda e: e.tensor_scalar(out=o2s[k][:], in0=po2[:, 0:128], scalar1=v[:, 2:3], scalar2=None, op0=ALU.mult),
                  reads=[po2, v], writes=[o2s[k]])
            kb.op("dve", lambda e: e.scalar_tensor_tensor(out=ob[k][:], in0=po1[:, 0:128], scalar=v[:, 0:1], in1=o2s[k][:],
                                                          op0=ALU.mult, op1=ALU.add), reads=[po1, v, o2s[k]], writes=[ob[k]])
            kb.op("dve", lambda e: e.scalar_tensor_tensor(out=jk[:], in0=ob[k][:], scalar=1.0, in1=ob[k][:], op0=ALU.mult, op1=ALU.mult,
                                                          accum_out=v[:, 3:4]), reads=[ob[k]], writes=[jk, v])
            rstd_from_ss(kb, v[:, 3:4], v[:, 4:5], 128, v, v)
            kb.op("dve", lambda e: e.scalar_tensor_tensor(out=ob[k][:], in0=ob[k][:], scalar=v[:, 4:5], in1=gsb[:], op0=ALU.mult, op1=ALU.mult),
                  reads=[ob[k], v, gsb], writes=[ob[k]])
            kb.op("dve", lambda e: e.tensor_tensor(out=otm[:, i, h * 128:(h + 1) * 128], in0=ob[k][:], in1=sg[:, i, h * 128:(h + 1) * 128],
                                                   op=ALU.mult), reads=[ob[k], sg], writes=[otm])

        maps = []
        for m in range(2):
            maps.append(dict(kT=(lambda h, j, m=m: kr[:, h, j * 128:(j + 1) * 128]),
                             qT=(lambda h, i, m=m: qz[m][:, h, i * 128:(i + 1) * 128]),
                             bias=None, reads=[kr, qz[m]]))
        attn_core(kb, cx, maps, 4, vaug, fin, 64 ** -0.5, (sc_ps, pt_sb, po_ps, maskT))
        for i in range(NT):
            for c in range(4):
                kb.op("pe", lambda e: e.transpose(out=ptr[:, c * 128:(c + 1) * 128], in_=otm[:, i, c * 128:(c + 1) * 128], identity=ident[:]),
                      reads=[otm, ident], writes=[ptr])
            kb.op("act", lambda e: e.activation(out=oTs[:, :, i * 128:(i + 1) * 128], in_=ptr[:].rearrange("p (a b) -> p a b", a=4),
                                                func=AF.Copy), reads=[ptr], writes=[oTs])
        oT = cx.oT[s]
        kb.dma("sp", oT.t[512:1024, :].rearrange("(c p) t -> p c t", p=128), oTs[:], reads=[oTs], writes=[oT], sembuf=oTs)


def bc3(ap2, n):
    a = ap2.shape[1]
    return ap2.unsqueeze(2).broadcast_to([ap2.shape[0], a, n])


def phase_ssd(kb, cx, l, s):
    with kb.phase():
        ident = load_const(kb, cx, "ident_bf", [128, 128], BF16)
        identf = load_const(kb, cx, "ident_f", [128, 128], F32)
        tri = load_const(kb, cx, "tri_le_f", [128, 128], F32)
        gtf = load_const(kb, cx, "gt_f", [128, 128], F32)
        negm = load_const(kb, cx, "negmask_f", [128, 128], F32)
        ones = load_const(kb, cx, "ones_f", [128, 128], F32)
        fmT, tm, sm = cx.fmT[s], cx.tm[s], cx.sm[s]
        pG = kb.ps([128, 2, 128], F32, "pG")
        pD = [kb.ps([128, 128], F32, "pD") for _ in range(2)]
        pY = kb.ps([128, 512], F32, "pY")
        pY2 = kb.ps([128, 512], F32, "pY2")
        pS = kb.ps([128, 512], F32, "pS")
        ptr = kb.ps([128, 512], BF16, "ptr")
        ptr2 = kb.ps([128, 512], BF16, "ptr2")
        cw4 = kb.sb([4, 1024], F32, "cw4")
        kb.dma("sp", cw4[:], cx.ssd_conv_w.t[l], reads=[cx.ssd_conv_w], writes=[cw4], sembuf=cw4)
        cb8 = kb.sb([8, 128], F32, "cb8")
        kb.dma("sp", cb8[:], cx.ssd_conv_b.t[l:l + 1, :].rearrange("o (c p) -> (o c) p", p=128), reads=[cx.ssd_conv_b], writes=[cb8], sembuf=cb8)
        par = kb.sb([128, 40], F32, "par")
        for c in range(8):
            kb.op("pe", lambda e: e.transpose(out=pD[0][:, c * 4:(c + 1) * 4], in_=cw4[0:4, c * 128:(c + 1) * 128], identity=identf[0:4, 0:4]),
                  reads=[cw4, identf], writes=[pD[0]])
        kb.op("pe", lambda e: e.transpose(out=pD[0][:, 32:40], in_=cb8[0:8, :], identity=identf[0:8, 0:8]), reads=[cb8, identf], writes=[pD[0]])
        kb.op("dve", lambda e: e.tensor_copy(out=par[:], in_=pD[0][:, 0:40]), reads=[pD[0]], writes=[par])
        if cx.stop < 11:
            return
        dtb = kb.sb([128, 8], F32, "dtb")
        kb.dma("sp", dtb[:], cx.ssd_dt_bias.t[l:l + 1, :].broadcast_to([128, 8]), reads=[cx.ssd_dt_bias], writes=[dtb], sembuf=dtb)
        negA = kb.sb([128, 8], F32, "negA")
        kb.dma("sp", negA[:], cx.ssd_a_log.t[l:l + 1, :].broadcast_to([128, 8]), reads=[cx.ssd_a_log], writes=[negA], sembuf=negA)
        kb.op("act", lambda e: e.activation(out=negA[:], in_=negA[:], func=AF.Exp), reads=[negA], writes=[negA])
        kb.op("dve", lambda e: e.tensor_scalar(out=negA[:], in0=negA[:], scalar1=-1.0, scalar2=None, op0=ALU.mult), reads=[negA], writes=[negA])
        dsk = kb.sb([128, 8], F32, "dsk")
        kb.dma("sp", dsk[:], cx.ssd_d.t[l:l + 1, :].broadcast_to([128, 8]), reads=[cx.ssd_d], writes=[dsk], sembuf=dsk)
        ng = kb.sb([128, 512], F32, "ng")
        kb.dma("sp", ng[:], cx.ssd_norm_g.t[l:l + 1, :].broadcast_to([128, 512]), reads=[cx.ssd_norm_g], writes=[ng], sembuf=ng)
        xbc = kb.sb([128, 8, S], BF16, "xbc")
        kb.dma("sp", xbc[:], fmT.t[R_SX:R_SX + 1024, :].rearrange("(c p) t -> p c t", p=128), reads=[fmT], writes=[xbc], sembuf=xbc)
        xc = kb.sb([128, 8, S], BF16, "xc")
        accs = [kb.sb([128, S], F32, "acc") for _ in range(2)]
        for c in range(8):
            acc = accs[c % 2]
            w = lambda j: par[:, c * 4 + j:c * 4 + j + 1]
            kb.op("dve", lambda e: e.tensor_scalar(out=acc[:], in0=xbc[:, c, :], scalar1=w(3), scalar2=None, op0=ALU.mult), reads=[xbc, par], writes=[acc])
            for sh in (1, 2, 3):
                kb.op("dve", lambda e: e.scalar_tensor_tensor(out=acc[:, sh:], in0=xbc[:, c, 0:S - sh], scalar=w(3 - sh), in1=acc[:, sh:],
                                                              op0=ALU.mult, op1=ALU.add), reads=[xbc, par, acc], writes=[acc])
            kb.op("act", lambda e: e.activation(out=xc[:, c, :], in_=acc[:], func=AF.Silu, bias=par[:, 32 + c:33 + c]), reads=[acc, par], writes=[xc])
        if cx.stop < 12:
            return
        dt = kb.sb([128, NT, 8], F32, "dt")
        kb.dma("sp", dt[:], sm.t[:, 4:12].rearrange("(t p) n -> p t n", p=128), reads=[sm], writes=[dt], sembuf=dt, allow_slow_non_contiguous=True)
        kb.op("dve", lambda e: e.tensor_tensor(out=dt[:], in0=dt[:], in1=dtb[:].unsqueeze(1).broadcast_to([128, NT, 8]), op=ALU.add), reads=[dt, dtb], writes=[dt])
        if cx.stop2 == 1:
            return
        kb.op("act", lambda e: e.activation(out=dt[:], in_=dt[:], func=AF.Exp), reads=[dt], writes=[dt])
        kb.op("act", lambda e: e.activation(out=dt[:], in_=dt[:], func=AF.Ln, bias=1.0), reads=[dt], writes=[dt])
        if cx.stop2 == 2:
            return
        av = kb.sb([128, NT, 8], F32, "av")
        kb.op("dve", lambda e: e.tensor_tensor(out=av[:], in0=dt[:], in1=negA[:].unsqueeze(1).broadcast_to([128, NT, 8]), op=ALU.mult), reads=[dt, negA], writes=[av])
        if cx.stop2 == 3:
            return
        av2 = av[:].rearrange("p t h -> p (t h)")
        kb.mm(pY[:, 0:128], [(tri[:], av2)], reads=[tri, av], writes=[pY])
        kb.mm(pY2[:, 0:128], [(ones[:], av2)], reads=[ones, av], writes=[pY2])
        if cx.stop2 == 4:
            return
        Acum = kb.sb([128, NT, 8], F32, "Acum")
        eA = kb.sb([128, NT, 8], F32, "eA")
        eAt = kb.sb([128, NT, 8], F32, "eAt")
        dsv = kb.sb([128, NT, 8], F32, "dsv")
        f2 = lambda b: b[:].rearrange("p t h -> p (t h)")
        kb.op("dve", lambda e: e.tensor_copy(out=f2(Acum), in_=pY[:, 0:128]), reads=[pY], writes=[Acum])
        kb.op("dve", lambda e: e.tensor_tensor(out=f2(dsv), in0=pY2[:, 0:128], in1=f2(Acum), op=ALU.subtract), reads=[pY2, Acum], writes=[dsv])
        if cx.stop2 == 5:
            return
        kb.op("dve", lambda e: e.tensor_copy(out=f2(eAt), in_=pY2[:, 0:128]), reads=[pY2], writes=[eAt])
        kb.op("act", lambda e: e.activation(out=f2(eAt), in_=f2(eAt), func=AF.Exp), reads=[eAt], writes=[eAt])
        if cx.stop2 == 6:
            return
        kb.op("act", lambda e: e.activation(out=f2(eA), in_=f2(Acum), func=AF.Exp), reads=[Acum], writes=[eA])
        if cx.stop2 == 7:
            return
        dsv0 = dsv
        dsv = kb.sb([128, NT, 8], F32, "dsv2")
        kb.op("act", lambda e: e.activation(out=f2(dsv), in_=f2(dsv0), func=AF.Exp), reads=[dsv0], writes=[dsv])
        if cx.stop < 13:
            return
        xs_tm = kb.sb([128, NT, 512], BF16, "xs_tm")
        B_tm = kb.sb([128, NT, 256], BF16, "B_tm")
        for t in range(NT):
            ts_ = slice(t * 128, (t + 1) * 128)
            for c in range(4):
                kb.op("pe", lambda e: e.transpose(out=ptr[:, c * 128:(c + 1) * 128], in_=xc[:, c, ts_], identity=ident[:]), reads=[xc, ident], writes=[ptr])
            kb.op("act", lambda e: e.activation(out=xs_tm[:, t, :], in_=ptr[:, 0:512], func=AF.Copy), reads=[ptr], writes=[xs_tm])
            for c in range(2):
                kb.op("pe", lambda e: e.transpose(out=ptr2[:, c * 128:(c + 1) * 128], in_=xc[:, 4 + c, ts_], identity=ident[:]), reads=[xc, ident], writes=[ptr2])
            kb.op("dve", lambda e: e.tensor_copy(out=B_tm[:, t, :], in_=ptr2[:, 0:256]), reads=[ptr2], writes=[B_tm])
        if cx.stop < 14:
            return
        zin = kb.sb([128, NT, 512], BF16, "zin")
        kb.dma("sp", zin[:], tm.t[:, C_SZ:C_SZ + 512].rearrange("(t p) n -> p t n", p=128), reads=[tm], writes=[zin], sembuf=zin)
        oTs = kb.sb([128, 4, S], BF16, "oTs")
        ST = kb.sb([128, 512], F32, "ST")
        STb = kb.sb([128, 512], BF16, "STb")
        tmpS = kb.sb([128, 512], F32, "tmpS")
        kb.op("dve", lambda e: e.memset(ST[:], 0.0), reads=[], writes=[ST])
        Xd = [kb.sb([128, 512], BF16, "Xd") for _ in range(2)]
        Xds = [kb.sb([128, 512], BF16, "Xds") for _ in range(2)]
        adg8 = [kb.sb([128, 128], F32, "adg") for _ in range(8)]
        Eb8 = [kb.sb([128, 128], F32, "Eb") for _ in range(8)]
        Gm = [kb.sb([128, 2, 128], F32, "Gm") for _ in range(2)]
        maskle = load_const(kb, cx, "tri_le_f", [128, 128], F32) if False else tri
        MT = [kb.sb([128, 128], BF16, "MT") for _ in range(2)]
        ysb = kb.sb([128, 512], F32, "ysb")
        y2 = kb.sb([128, 512], F32, "y2")
        szt = kb.sb([128, 512], F32, "szt")
        jk = kb.sb([128, 512], F32, "jk")
        otm = [kb.sb([128, 512], BF16, "otm") for _ in range(2)]
        sv = kb.sb([128, NT, 4], F32, "sv")
        v3 = lambda b: b[:].rearrange("p (h d) -> p h d", h=8)
        n = 0
        for t in range(NT if cx.stop >= 20 else max(0, cx.stop - 14)):
            ts_ = slice(t * 128, (t + 1) * 128)
            xd, xds = Xd[t % 2], Xds[t % 2]
            kb.op("dve", lambda e: e.tensor_tensor(out=v3(xd), in0=xs_tm[:, t, :].rearrange("p (h d) -> p h d", h=8), in1=bc3(dt[:, t, :], 64),
                                                   op=ALU.mult), reads=[xs_tm, dt], writes=[xd])
            kb.op("dve", lambda e: e.tensor_tensor(out=v3(xds), in0=v3(xd), in1=bc3(dsv[:, t, :], 64), op=ALU.mult), reads=[xd, dsv], writes=[xds])
            for g in range(2):
                kb.mm(pG[:, g, :], [(xc[:, 4 + g, ts_], xc[:, 6 + g, ts_])], reads=[xc], writes=[pG])
            gm = Gm[t % 2]
            kb.op("dve", lambda e: e.tensor_tensor(out=gm[:], in0=pG[:], in1=maskle[:].unsqueeze(1).broadcast_to([128, 2, 128]), op=ALU.mult),
                  reads=[pG, maskle], writes=[gm])
            for h in range(8):
                a_ = adg8[h]
                kb.op("dve", lambda e: e.tensor_scalar(out=a_[:], in0=tri[:], scalar1=av[:, t, h:h + 1], scalar2=None, op0=ALU.mult),
                      reads=[tri, av], writes=[a_])
            for h in range(8):
                pd = pD[h % 2]
                kb.mm(pd[:], [(gtf[:], adg8[h][:])], reads=[gtf, adg8[h]], writes=[pd])
                kb.op("act", lambda e: e.activation(out=Eb8[h][:], in_=pd[:], func=AF.Exp), reads=[pd], writes=[Eb8[h]])
            for h in range(8):
                g = h // 4
                m_ = MT[h % 2]
                kb.op("dve", lambda e: e.tensor_tensor(out=m_[:], in0=gm[:, g, :], in1=Eb8[h][:], op=ALU.mult), reads=[gm, Eb8[h]], writes=[m_])
                hs = slice(h * 64, (h + 1) * 64)
                kb.mm(pY[:, hs], [(m_[:], xd[:, hs])], reads=[m_, xd], writes=[pY])
                if t > 0:
                    kb.mm(pY2[:, hs], [(xc[:, 6 + g, ts_], STb[:, hs])], reads=[xc, STb], writes=[pY2])
                kb.mm(pS[:, hs], [(B_tm[:, t, g * 128:(g + 1) * 128], xds[:, hs])], reads=[B_tm, xds], writes=[pS])
            kb.op("act", lambda e: e.activation(out=ysb[:], in_=pY[:], func=AF.Copy), reads=[pY], writes=[ysb])
            if t > 0:
                kb.op("dve", lambda e: e.tensor_tensor(out=v3(y2), in0=pY2[:].rearrange("p (h d) -> p h d", h=8), in1=bc3(eA[:, t, :], 64), op=ALU.mult),
                      reads=[pY2, eA], writes=[y2])
                kb.op("dve", lambda e: e.tensor_tensor(out=ysb[:], in0=ysb[:], in1=y2[:], op=ALU.add), reads=[ysb, y2], writes=[ysb])
            if t < NT - 1:
                kb.op("dve", lambda e: e.tensor_tensor(out=v3(tmpS), in0=v3(ST), in1=bc3(eAt[:, t, :], 64), op=ALU.mult), reads=[ST, eAt], writes=[tmpS])
                kb.op("dve", lambda e: e.tensor_tensor(out=ST[:], in0=tmpS[:], in1=pS[:], op=ALU.add), reads=[tmpS, pS], writes=[ST])
                kb.op("act", lambda e: e.activation(out=STb[:], in_=ST[:], func=AF.Copy), reads=[ST], writes=[STb])
            kb.op("dve", lambda e: e.tensor_tensor(out=v3(y2), in0=xs_tm[:, t, :].rearrange("p (h d) -> p h d", h=8), in1=bc3(dsk[:], 64), op=ALU.mult),
                  reads=[xs_tm, dsk], writes=[y2])
            kb.op("dve", lambda e: e.tensor_tensor(out=ysb[:], in0=ysb[:], in1=y2[:], op=ALU.add), reads=[ysb, y2], writes=[ysb])
            kb.op("act", lambda e: e.activation(out=szt[:], in_=zin[:, t, :], func=AF.Silu), reads=[zin], writes=[szt])
            kb.op("dve", lambda e: e.tensor_tensor(out=ysb[:], in0=ysb[:], in1=szt[:], op=ALU.mult), reads=[ysb, szt], writes=[ysb])
            kb.op("dve", lambda e: e.scalar_tensor_tensor(out=jk[:], in0=ysb[:], scalar=1.0, in1=ysb[:], op0=ALU.mult, op1=ALU.mult,
                                                          accum_out=sv[:, t, 0:1]), reads=[ysb], writes=[jk, sv])
            rstd_from_ss(kb, sv[:, t, 0:1], sv[:, t, 1:2], 512, sv, sv)
            om = otm[t % 2]
            kb.op("dve", lambda e: e.scalar_tensor_tensor(out=om[:], in0=ysb[:], scalar=sv[:, t, 1:2], in1=ng[:], op0=ALU.mult, op1=ALU.mult),
                  reads=[ysb, sv, ng], writes=[om])
            for c in range(4):
                kb.op("pe", lambda e: e.transpose(out=ptr[:, c * 128:(c + 1) * 128], in_=om[:, c * 128:(c + 1) * 128], identity=ident[:]),
                      reads=[om, ident], writes=[ptr])
            kb.op("act", lambda e: e.activation(out=oTs[:, :, ts_], in_=ptr[:, 0:512].rearrange("p (a b) -> p a b", a=4), func=AF.Copy),
                  reads=[ptr], writes=[oTs])
        oT = cx.oT[s]
        kb.dma("sp", oT.t[1024:1536, :].rearrange("(c p) t -> p c t", p=128), oTs[:], reads=[oTs], writes=[oT], sembuf=oTs)


def s5_alloc(kb, cx):
    def P(shape, dt, name):
        t = kb.es.enter_context(kb.nc.sbuf_tensor("s5p_" + name, list(shape), dt))
        return Buf(kb, name, t)
    cx.s5 = dict(BT=P([128, 16, 2, 128], BF16, "BT"), CP=P([128, 16, 2, 128], BF16, "CP"),
                 PWr=P([128, 11, 16], F32, "PWr"), PWi=P([128, 11, 16], F32, "PWi"), NPWi=P([128, 11, 16], F32, "NPWi"),
                 dsk=P([128, 4], F32, "dsk"), Ur=P([128, 11, 16], F32, "Ur"), Ui=P([128, 11, 16], F32, "Ui"),
                 NUi=P([128, 11, 16], F32, "NUi"), R=P([128, 16], F32, "R"))


def phase_s5_prep(kb, cx, l):
    TWO_PI = 2.0 * math.pi
    s5 = cx.s5
    with kb.phase():
        identf = load_const(kb, cx, "ident_f", [128, 128], F32)
        ones = load_const(kb, cx, "ones_f", [128, 128], F32)
        gmask = load_const(kb, cx, "gmask", [128, 8], F32)
        pt = kb.ps([128, 128], F32, "pt")
        pts = [kb.ps([128, 128], F32, "pts") for _ in range(2)]
        p3 = kb.sb([16, 3, 128], F32, "p3")
        kb.dma("sp", p3[:, 0, :], cx.s5_a_re.t[l].rearrange("(st g) p -> st (g p)", g=2), reads=[cx.s5_a_re], writes=[p3], sembuf=p3)
        kb.dma("sp", p3[:, 1, :], cx.s5_a_im.t[l].rearrange("(st g) p -> st (g p)", g=2), reads=[cx.s5_a_im], writes=[p3], sembuf=p3)
        ld = kb.sb([16, 2], F32, "ld")
        kb.dma("sp", ld[:], cx.s5_log_dt.t[l:l + 1, :].rearrange("o (st g) -> (o st) g", g=2), reads=[cx.s5_log_dt], writes=[ld], sembuf=ld)
        for g in range(2):
            kb.op("dve", lambda e: e.tensor_scalar(out=p3[:, 2, g * 64:(g + 1) * 64], in0=ones[0:16, 0:64], scalar1=ld[:, g:g + 1], scalar2=None,
                                                   op0=ALU.mult), reads=[ones, ld], writes=[p3])
        d4 = kb.sb([4, 128], F32, "d4")
        kb.dma("sp", d4[:], cx.s5_d.t[l].rearrange("(cc gl) h -> cc (gl h)", gl=8), reads=[cx.s5_d], writes=[d4], sembuf=d4)
        for k in range(3):
            kb.op("pe", lambda e: e.transpose(out=pt[:, k * 16:(k + 1) * 16], in_=p3[0:16, k, :], identity=identf[0:16, 0:16]),
                  reads=[p3, identf], writes=[pt])
        kb.op("pe", lambda e: e.transpose(out=pt[:, 48:52], in_=d4[0:4, :], identity=identf[0:4, 0:4]), reads=[d4, identf], writes=[pt])
        prm = kb.sb([128, 3, 16], F32, "prm")
        kb.op("dve", lambda e: e.tensor_copy(out=prm[:].rearrange("p a b -> p (a b)"), in_=pt[:, 0:48]), reads=[pt], writes=[prm])
        kb.op("dve", lambda e: e.tensor_copy(out=s5["dsk"][:], in_=pt[:, 48:52]), reads=[pt], writes=[s5["dsk"]])
        aR, aI, LD = prm[:, 0, :], prm[:, 1, :], prm[:, 2, :]
        w = kb.sb([128, 24, 16], F32, "w")
        W = lambda i: w[:, i, :]
        dve = lambda fn, rd=(), wr=(): kb.op("dve", fn, reads=list(rd) or [w, prm], writes=list(wr) or [w])
        act = lambda fn, rd=(), wr=(): kb.op("act", fn, reads=list(rd) or [w, prm], writes=list(wr) or [w])
        TT = lambda o, a, b, op: dve(lambda e: e.tensor_tensor(out=o, in0=a, in1=b, op=op))
        act(lambda e: e.activation(out=W(0), in_=LD, func=AF.Exp))
        TT(W(1), aR, W(0), ALU.mult)
        act(lambda e: e.activation(out=W(1), in_=W(1), func=AF.Exp))
        TT(W(2), aI, W(0), ALU.mult)
        dve(lambda e: e.tensor_scalar(out=W(2), in0=W(2), scalar1=1.0 / TWO_PI, scalar2=None, op0=ALU.mult))
        wi = kb.sb([128, 16], mybir.dt.int32, "wi")
        dve(lambda e: e.tensor_copy(out=wi[:], in_=W(2)), wr=[wi])
        dve(lambda e: e.tensor_copy(out=W(3), in_=wi[:]), rd=[wi])
        TT(W(3), W(2), W(3), ALU.subtract)
        dve(lambda e: e.tensor_scalar(out=W(4), in0=W(3), scalar1=0.25, scalar2=None, op0=ALU.add))

        def wrap(src, dst, t1, t2):
            dve(lambda e: e.tensor_single_scalar(out=t1, in_=src, scalar=0.5, op=ALU.is_gt))
            dve(lambda e: e.tensor_single_scalar(out=t2, in_=src, scalar=-0.5, op=ALU.is_lt))
            TT(dst, src, t1, ALU.subtract)
            TT(dst, dst, t2, ALU.add)
        wrap(W(3), W(5), W(6), W(7))
        wrap(W(4), W(8), W(6), W(7))
        act(lambda e: e.activation(out=W(9), in_=W(5), func=AF.Sin, scale=TWO_PI))
        act(lambda e: e.activation(out=W(10), in_=W(8), func=AF.Sin, scale=TWO_PI))
        PWr, PWi, NPWi = s5["PWr"], s5["PWi"], s5["NPWi"]
        kb.op("dve", lambda e: e.tensor_tensor(out=PWr[:, 0, :], in0=W(1), in1=W(10), op=ALU.mult), reads=[w], writes=[PWr])
        kb.op("dve", lambda e: e.tensor_tensor(out=PWi[:, 0, :], in0=W(1), in1=W(9), op=ALU.mult), reads=[w], writes=[PWi])
        for k in range(10):
            kb.op("dve", lambda e: e.tensor_tensor(out=W(11), in0=PWr[:, k, :], in1=PWr[:, k, :], op=ALU.mult), reads=[PWr], writes=[w])
            kb.op("dve", lambda e: e.tensor_tensor(out=W(12), in0=PWi[:, k, :], in1=PWi[:, k, :], op=ALU.mult), reads=[PWi], writes=[w])
            kb.op("dve", lambda e: e.tensor_tensor(out=PWr[:, k + 1, :], in0=W(11), in1=W(12), op=ALU.subtract), reads=[w], writes=[PWr])
            kb.op("dve", lambda e: e.scalar_tensor_tensor(out=PWi[:, k + 1, :], in0=PWr[:, k, :], scalar=2.0, in1=PWi[:, k, :], op0=ALU.mult, op1=ALU.mult),
                  reads=[PWr, PWi], writes=[PWi])
        kb.op("dve", lambda e: e.tensor_scalar(out=NPWi[:], in0=PWi[:], scalar1=-1.0, scalar2=None, op0=ALU.mult), reads=[PWi], writes=[NPWi])
        Ur, Ui, NUi, Rm = s5["Ur"], s5["Ui"], s5["NUi"], s5["R"]
        kb.op("dve", lambda e: e.tensor_copy(out=Ur[:, 0, :], in_=W(10)), reads=[w], writes=[Ur])
        kb.op("dve", lambda e: e.tensor_copy(out=Ui[:, 0, :], in_=W(9)), reads=[w], writes=[Ui])
        kb.op("dve", lambda e: e.tensor_copy(out=Rm[:], in_=W(1)), reads=[w], writes=[Rm])
        for k in range(10):
            kb.op("dve", lambda e: e.tensor_tensor(out=W(19), in0=Ur[:, k, :], in1=Ur[:, k, :], op=ALU.mult), reads=[Ur], writes=[w])
            kb.op("dve", lambda e: e.tensor_tensor(out=W(20), in0=Ui[:, k, :], in1=Ui[:, k, :], op=ALU.mult), reads=[Ui], writes=[w])
            kb.op("dve", lambda e: e.tensor_tensor(out=Ur[:, k + 1, :], in0=W(19), in1=W(20), op=ALU.subtract), reads=[w], writes=[Ur])
            kb.op("dve", lambda e: e.scalar_tensor_tensor(out=Ui[:, k + 1, :], in0=Ur[:, k, :], scalar=2.0, in1=Ui[:, k, :], op0=ALU.mult, op1=ALU.mult),
                  reads=[Ur, Ui], writes=[Ui])
        kb.op("dve", lambda e: e.tensor_scalar(out=NUi[:], in0=Ui[:], scalar1=-1.0, scalar2=None, op0=ALU.mult), reads=[Ui], writes=[NUi])
        Ets = [kb.sb([128, 2, S], F32, "Et") for _ in range(2)]
        for st in range(16):
            E = Ets[st % 2]
            kb.op("dve", lambda e: e.memset(E[:, 0, 0:1], 1.0), reads=[], writes=[E])
            kb.op("dve", lambda e: e.memset(E[:, 1, 0:1], 0.0), reads=[], writes=[E])
            for k in range(11):
                sh = 1 << k
                ur, ui, nui = Ur[:, k, st:st + 1], Ui[:, k, st:st + 1], NUi[:, k, st:st + 1]
                kb.op("dve", lambda e: e.tensor_scalar(out=E[:, 0, sh:2 * sh], in0=E[:, 0, 0:sh], scalar1=ur, scalar2=None, op0=ALU.mult),
                      reads=[E, Ur], writes=[E])
                kb.op("dve", lambda e: e.scalar_tensor_tensor(out=E[:, 0, sh:2 * sh], in0=E[:, 1, 0:sh], scalar=nui, in1=E[:, 0, sh:2 * sh],
                                                              op0=ALU.mult, op1=ALU.add), reads=[E, NUi], writes=[E])
                kb.op("dve", lambda e: e.tensor_scalar(out=E[:, 1, sh:2 * sh], in0=E[:, 0, 0:sh], scalar1=ui, scalar2=None, op0=ALU.mult),
                      reads=[E, Ui], writes=[E])
                kb.op("dve", lambda e: e.scalar_tensor_tensor(out=E[:, 1, sh:2 * sh], in0=E[:, 1, 0:sh], scalar=ur, in1=E[:, 1, sh:2 * sh],
                                                              op0=ALU.mult, op1=ALU.add), reads=[E, Ur], writes=[E])
            kb.dma("sp", cx.Etab.t[st].rearrange("a p t -> p a t"), E[:], reads=[E], writes=[cx.Etab], sembuf=E)
        ar, ai = PWr[:, 0, :], PWi[:, 0, :]
        rdP = [w, prm, PWr, PWi]
        dveP = lambda fn: kb.op("dve", fn, reads=rdP, writes=[w])
        dveP(lambda e: e.tensor_scalar(out=W(13), in0=ar, scalar1=-1.0, scalar2=None, op0=ALU.add))
        dveP(lambda e: e.tensor_tensor(out=W(14), in0=aR, in1=aR, op=ALU.mult))
        dveP(lambda e: e.tensor_tensor(out=W(15), in0=aI, in1=aI, op=ALU.mult))
        dveP(lambda e: e.tensor_tensor(out=W(14), in0=W(14), in1=W(15), op=ALU.add))
        dveP(lambda e: e.reciprocal(out=W(14), in_=W(14)))
        dveP(lambda e: e.tensor_tensor(out=W(15), in0=W(13), in1=aR, op=ALU.mult))
        dveP(lambda e: e.tensor_tensor(out=W(16), in0=ai, in1=aI, op=ALU.mult))
        dveP(lambda e: e.tensor_tensor(out=W(15), in0=W(15), in1=W(16), op=ALU.add))
        dveP(lambda e: e.tensor_tensor(out=W(17), in0=W(15), in1=W(14), op=ALU.mult))
        dveP(lambda e: e.tensor_tensor(out=W(15), in0=ai, in1=aR, op=ALU.mult))
        dveP(lambda e: e.tensor_tensor(out=W(16), in0=W(13), in1=aI, op=ALU.mult))
        dveP(lambda e: e.tensor_tensor(out=W(15), in0=W(15), in1=W(16), op=ALU.subtract))
        dveP(lambda e: e.tensor_tensor(out=W(18), in0=W(15), in1=W(14), op=ALU.mult))
        bq = kb.sb([128, 2, 16, 16], F32, "bq")
        for ri, src in enumerate((cx.s5_b_re, cx.s5_b_im)):
            v = src.t[l].rearrange("(st g) p h -> g p st h", g=2)
            for g in range(2):
                kb.dma("sp", bq[g * 64:(g + 1) * 64, ri, :, :], v[g], reads=[src], writes=[bq], sembuf=bq, allow_slow_non_contiguous=True)
        bb = kb.sb([128, 2, 16, 16], F32, "bb")
        tq = kb.sb([128, 2, 16, 16], F32, "tq")
        gr, gi = bc3(W(17), 16), bc3(W(18), 16)
        rdB = [bq, w, tq, bb]
        kb.op("dve", lambda e: e.tensor_tensor(out=tq[:, 0], in0=bq[:, 0], in1=gr, op=ALU.mult), reads=rdB, writes=[tq])
        kb.op("dve", lambda e: e.tensor_tensor(out=tq[:, 1], in0=bq[:, 1], in1=gi, op=ALU.mult), reads=rdB, writes=[tq])
        kb.op("dve", lambda e: e.tensor_tensor(out=bb[:, 0], in0=tq[:, 0], in1=tq[:, 1], op=ALU.subtract), reads=rdB, writes=[bb])
        kb.op("dve", lambda e: e.tensor_tensor(out=tq[:, 0], in0=bq[:, 1], in1=gr, op=ALU.mult), reads=rdB, writes=[tq])
        kb.op("dve", lambda e: e.tensor_tensor(out=tq[:, 1], in0=bq[:, 0], in1=gi, op=ALU.mult), reads=rdB, writes=[tq])
        kb.op("dve", lambda e: e.tensor_tensor(out=bb[:, 1], in0=tq[:, 0], in1=tq[:, 1], op=ALU.add), reads=rdB, writes=[bb])
        bpad = kb.sb([128, 16, 2, 128], F32, "bpad")
        kb.op("pool", lambda e: e.memset(bpad[:], 0.0), reads=[], writes=[bpad])
        for st4 in range(4):
            for g in range(2):
                c0 = (st4 * 2 + g) * 16
                for ri in range(2):
                    kb.op("dve", lambda e: e.tensor_copy(out=bpad[g * 64:(g + 1) * 64, st4::4, ri, c0:c0 + 16], in_=bb[g * 64:(g + 1) * 64, ri, st4::4, :]),
                          reads=[bb], writes=[bpad])
        BT, CP = s5["BT"], s5["CP"]
        n = 0
        for st in range(16):
            for ri in range(2):
                p = pts[n % 2]
                n += 1
                kb.op("pe", lambda e: e.transpose(out=p[:], in_=bpad[:, st, ri, :], identity=identf[:]), reads=[bpad, identf], writes=[p])
                kb.op("act", lambda e: e.activation(out=BT[:, st, ri, :], in_=p[:], func=AF.Copy), reads=[p], writes=[BT])
        cn = kb.sb([128, 2, 4, 64], F32, "cn")
        for ri, src in enumerate((cx.s5_c_re, cx.s5_c_im)):
            for cc in range(4):
                kb.dma("sp", cn[:, ri, cc, :], src.t[l, cc * 8:(cc + 1) * 8].rearrange("g h p -> (g h) p"), reads=[src], writes=[cn], sembuf=cn)
        mts = [kb.sb([128, 128], F32, "mt") for _ in range(2)]
        for st in range(16):
            cc, gl0 = st // 4, (st % 4) * 2
            for ri in range(2):
                mt = mts[n % 2]
                p = pts[n % 2]
                n += 1
                for g in range(2):
                    kb.op("dve", lambda e: e.tensor_scalar(out=mt[:, g * 64:(g + 1) * 64], in0=cn[:, ri, cc, :], scalar1=gmask[:, gl0 + g:gl0 + g + 1],
                                                           scalar2=None, op0=ALU.mult), reads=[cn, gmask], writes=[mt])
                kb.op("pe", lambda e: e.transpose(out=p[:], in_=mt[:], identity=identf[:]), reads=[mt, identf], writes=[p])
                sgn = 1.0 if ri == 0 else -1.0
                kb.op("dve", lambda e: e.tensor_scalar(out=CP[:, st, ri, :], in0=p[:], scalar1=sgn, scalar2=None, op0=ALU.mult), reads=[p], writes=[CP])


def phase_s5(kb, cx, l, s):
    s5 = cx.s5
    BT, CP, PWr, PWi, NPWi, dsk = s5["BT"], s5["CP"], s5["PWr"], s5["PWi"], s5["NPWi"], s5["dsk"]
    with kb.phase():
        ident = load_const(kb, cx, "ident_bf", [128, 128], BF16)
        fmT, tm = cx.fmT[s], cx.tm[s]
        pYo = [kb.ps([128, 512], F32, "pYo") for _ in range(4)]
        pB = [kb.ps([128, 512], F32, "pB") for _ in range(2)]
        ptr = kb.ps([128, 512], BF16, "ptr")
        uT = kb.sb([128, 4, S], BF16, "uT")
        kb.dma("sp", uT[:], fmT.t[R_SU:R_SU + 512, :].rearrange("(c p) t -> p c t", p=128), reads=[fmT], writes=[uT], sembuf=uT)
        bp = [kb.sb([128, S], F32, f"bp{b}") for b in range(2)]
        zz = [kb.sb([128, S], F32, f"zz{b}") for b in range(2)]
        xb = [kb.sb([128, S], BF16, f"xb{b}") for b in range(2)]
        Etl = [kb.sb([128, 2, S], F32, "Etl") for _ in range(2)]
        tA = [kb.sb([128, 512], F32, "tA") for _ in range(4)]
        tB = [kb.sb([128, 512], F32, "tB") for _ in range(4)]
        gT = kb.sb([128, 4, S], BF16, "gT")
        wss = [kb.sb([128, 1024], F32, "ws") for _ in range(2)]
        wg = kb.sb([128, 4, 1024], BF16, "wg")
        wv = cx.s5_w_glu.t[l].rearrange("(c p) n -> p c n", p=128)
        for c in range(4):
            kb.dma("sp", wss[c % 2][:], wv[:, c, :], reads=[cx.s5_w_glu], writes=[wss[c % 2]], sembuf=wss[c % 2])
            kb.op("pool", lambda e: e.tensor_copy(out=wg[:, c, :], in_=wss[c % 2][:]), reads=[wss[c % 2]], writes=[wg])
        sgin = [kb.sb([128, 512], BF16, "sgin") for _ in range(2)]
        yb = [kb.sb([128, 512], F32, "yb") for _ in range(2)]
        q1 = [kb.sb([128, 512], F32, "q1") for _ in range(2)]
        Rm = s5["R"]
        nt = 0
        for st in range(16):
            cc = st // 4
            Et = Etl[st % 2]
            kb.dma("sp", Et[:], cx.Etab.t[st].rearrange("a p t -> p a t"), reads=[cx.Etab], writes=[Et], sembuf=Et)
            for tb in range(4):
                cs = slice(tb * 512, (tb + 1) * 512)
                a_, b_ = tA[(2 * nt) % 4], tB[(2 * nt) % 4]
                a2_, b2_ = tA[(2 * nt + 1) % 4], tB[(2 * nt + 1) % 4]
                nt += 1
                kb.mm(pB[0][:], [(BT[:, st, 0, :], uT[:, cc, cs])], reads=[BT, uT], writes=[pB[0]])
                kb.mm(pB[1][:], [(BT[:, st, 1, :], uT[:, cc, cs])], reads=[BT, uT], writes=[pB[1]])
                kb.op("act", lambda e: e.activation(out=zz[0][:, cs], in_=pB[0][:], func=AF.Copy), reads=[pB[0]], writes=[zz[0]])
                kb.op("act", lambda e: e.activation(out=zz[1][:, cs], in_=pB[1][:], func=AF.Copy), reads=[pB[1]], writes=[zz[1]])
                kb.op("dve", lambda e: e.tensor_tensor(out=a_[:], in0=zz[0][:, cs], in1=Et[:, 0, cs], op=ALU.mult), reads=[zz[0], Et], writes=[a_])
                kb.op("pool", lambda e: e.tensor_tensor(out=b_[:], in0=zz[1][:, cs], in1=Et[:, 1, cs], op=ALU.mult), reads=[zz[1], Et], writes=[b_])
                kb.op("pool", lambda e: e.tensor_tensor(out=b2_[:], in0=zz[0][:, cs], in1=Et[:, 1, cs], op=ALU.mult), reads=[zz[0], Et], writes=[b2_])
                kb.op("dve", lambda e: e.tensor_tensor(out=a2_[:], in0=zz[1][:, cs], in1=Et[:, 0, cs], op=ALU.mult), reads=[zz[1], Et], writes=[a2_])
                kb.op("dve", lambda e: e.tensor_tensor(out=bp[0][:, cs], in0=a_[:], in1=b_[:], op=ALU.add), reads=[a_, b_], writes=[bp[0]])
                kb.op("dve", lambda e: e.tensor_tensor(out=bp[1][:, cs], in0=a2_[:], in1=b2_[:], op=ALU.subtract), reads=[a2_, b2_], writes=[bp[1]])
            rb = Rm[:, st:st + 1].broadcast_to([128, S])
            for ri in range(2):
                kb.op("dve", lambda e: e.tensor_tensor_scan(out=zz[ri][:], data0=rb, data1=bp[ri][:], initial=0.0, op0=ALU.mult, op1=ALU.add),
                      reads=[Rm, bp[ri], zz[ri]], writes=[zz[ri]])
            for tb in range(4):
                cs = slice(tb * 512, (tb + 1) * 512)
                a_, b_ = tA[(2 * nt) % 4], tB[(2 * nt) % 4]
                a2_, b2_ = tA[(2 * nt + 1) % 4], tB[(2 * nt + 1) % 4]
                nt += 1
                kb.op("dve", lambda e: e.tensor_tensor(out=a_[:], in0=zz[0][:, cs], in1=Et[:, 0, cs], op=ALU.mult), reads=[zz[0], Et], writes=[a_])
                kb.op("pool", lambda e: e.tensor_tensor(out=b_[:], in0=zz[1][:, cs], in1=Et[:, 1, cs], op=ALU.mult), reads=[zz[1], Et], writes=[b_])
                kb.op("pool", lambda e: e.tensor_tensor(out=b2_[:], in0=zz[1][:, cs], in1=Et[:, 0, cs], op=ALU.mult), reads=[zz[1], Et], writes=[b2_])
                kb.op("dve", lambda e: e.tensor_tensor(out=a2_[:], in0=zz[0][:, cs], in1=Et[:, 1, cs], op=ALU.mult), reads=[zz[0], Et], writes=[a2_])
                kb.op("dve", lambda e: e.tensor_tensor(out=xb[0][:, cs], in0=a_[:], in1=b_[:], op=ALU.subtract), reads=[a_, b_], writes=[xb[0]])
                kb.op("dve", lambda e: e.tensor_tensor(out=xb[1][:, cs], in0=a2_[:], in1=b2_[:], op=ALU.add), reads=[a2_, b2_], writes=[xb[1]])
            for tb in range(4):
                cs = slice(tb * 512, (tb + 1) * 512)
                py = pYo[tb]
                first, last = (st % 4 == 0), (st % 4 == 3)
                kb._wait("pe", kb._deps([CP, xb[0], xb[1]], [py] if first else []))
                kb.nc.tensor.matmul(py[:], CP[:, st, 0, :], xb[0][:, cs], start=first, stop=False)
                ins = kb.nc.tensor.matmul(py[:], CP[:, st, 1, :], xb[1][:, cs], start=False, stop=last)
                sname = kb.sem["pe"]
                kb.semval[sname] += 1
                ins.then_inc(kb.semobj[sname], 1)
                tok = (sname, kb.semval[sname])
                xb[0].rd.append(tok)
                xb[1].rd.append(tok)
                if first:
                    py.rd = []
                py.lw = [tok]
            if st % 4 == 3:
                for tb in range(4):
                    cs = slice(tb * 512, (tb + 1) * 512)
                    y, q = yb[tb % 2], q1[tb % 2]
                    kb.op("dve", lambda e: e.scalar_tensor_tensor(out=y[:], in0=uT[:, cc, cs], scalar=dsk[:, cc:cc + 1], in1=pYo[tb][:], op0=ALU.mult, op1=ALU.add),
                          reads=[uT, dsk, pYo[tb]], writes=[y])
                    kb.op("dve", lambda e: e.tensor_tensor(out=q[:], in0=y[:], in1=y[:], op=ALU.mult), reads=[y], writes=[q])
                    kb.op("dve", lambda e: e.tensor_scalar(out=q[:], in0=q[:], scalar1=0.044715, scalar2=1.0, op0=ALU.mult, op1=ALU.add), reads=[q], writes=[q])
                    kb.op("dve", lambda e: e.tensor_tensor(out=q[:], in0=q[:], in1=y[:], op=ALU.mult), reads=[q, y], writes=[q])
                    kb.op("act", lambda e: e.activation(out=q[:], in_=q[:], func=AF.Sigmoid, scale=1.5957691216057308), reads=[q], writes=[q])
                    kb.op("dve", lambda e: e.tensor_tensor(out=gT[:, cc, cs], in0=q[:], in1=y[:], op=ALU.mult), reads=[q, y], writes=[gT])
        oTs = kb.sb([128, 4, S], BF16, "oTs")
        sgs = [kb.sb([128, 512], F32, "sgs") for _ in range(2)]
        sig = [kb.sb([128, 512], F32, "sig") for _ in range(2)]
        od = [kb.sb([128, 512], F32, "od") for _ in range(2)]
        om = [kb.sb([128, 512], BF16, "om") for _ in range(2)]
        for t in range(NT):
            ts_ = slice(t * 128, (t + 1) * 128)
            k = t % 2
            kb.mm(pB[0][:], [(gT[:, c, ts_], wg[:, c, 0:512]) for c in range(4)], reads=[gT, wg], writes=[pB[0]])
            kb.mm(pB[1][:], [(gT[:, c, ts_], wg[:, c, 512:1024]) for c in range(4)], reads=[gT, wg], writes=[pB[1]])
            kb.op("act", lambda e: e.activation(out=sig[k][:], in_=pB[1][:], func=AF.Sigmoid), reads=[pB[1]], writes=[sig[k]])
            kb.dma("sp", sgin[k][:], tm.t[ts_, C_SG:C_SG + 512], reads=[tm], writes=[sgin[k]], sembuf=sgin[k])
            kb.op("act", lambda e: e.activation(out=sgs[k][:], in_=sgin[k][:], func=AF.Silu), reads=[sgin[k]], writes=[sgs[k]])
            kb.op("dve", lambda e: e.tensor_tensor(out=od[k][:], in0=pB[0][:], in1=sig[k][:], op=ALU.mult), reads=[pB[0], sig[k]], writes=[od[k]])
            kb.op("dve", lambda e: e.tensor_tensor(out=om[k][:], in0=od[k][:], in1=sgs[k][:], op=ALU.mult), reads=[od[k], sgs[k]], writes=[om[k]])
            for c in range(4):
                kb.op("pe", lambda e: e.transpose(out=ptr[:, c * 128:(c + 1) * 128], in_=om[k][:, c * 128:(c + 1) * 128], identity=ident[:]),
                      reads=[om[k], ident], writes=[ptr])
            kb.op("act", lambda e: e.activation(out=oTs[:, :, ts_], in_=ptr[:].rearrange("p (a b) -> p a b", a=4), func=AF.Copy), reads=[ptr], writes=[oTs])
        oT = cx.oT[s]
        kb.dma("sp", oT.t[1536:2048, :].rearrange("(c p) t -> p c t", p=128), oTs[:], reads=[oTs], writes=[oT], sembuf=oTs)


def load_w_bf16(kb, dst, src_ap2d, srcbuf, stg):
    v = src_ap2d.rearrange("(c p) n -> p c n", p=128)
    for c in range(16):
        st = stg[c % 2]
        kb.dma("sp", st[:], v[:, c, :], reads=[srcbuf], writes=[st], sembuf=st)
        eng = ("act", "dve", "pool")[c % 3]
        if eng == "act":
            kb.op("act", lambda e: e.activation(out=dst[:, c, :], in_=st[:], func=AF.Copy), reads=[st], writes=[dst])
        else:
            kb.op(eng, lambda e: e.tensor_copy(out=dst[:, c, :], in_=st[:]), reads=[st], writes=[dst])


def phase_out(kb, cx, l):
    mTs = cx.mTs
    with kb.phase():
        ident = load_const(kb, cx, "ident_bf", [128, 128], BF16)
        Wb = kb.sb([128, 16, D], BF16, "Wb")
        stg = [kb.sb([128, D], F32, "stg") for _ in range(2)]
        load_w_bf16(kb, Wb, cx.w_branch.t[l].rearrange("n k d -> (n k) d"), cx.w_branch, stg)
        oTt = [kb.sb([128, 16, 128], BF16, "oTt") for _ in range(2)]
        mgt = [kb.sb([128, 4 * D], BF16, "mgt") for _ in range(2)]
        sgm = [kb.sb([128, D], F32, "sgm") for _ in range(2)]
        mergedb = [[kb.sb([128, 512], F32, "merged") for _ in range(4)] for _ in range(2)]
        tmp = [kb.sb([128, 512], F32, "tmp") for _ in range(4)]
        mbf = kb.sb([128, D], BF16, "mbf")
        mst = [kb.sb([128, 16, 128], BF16, "mst") for _ in range(2)]
        pm = [kb.ps([128, 512], F32, "pm") for _ in range(4)]
        ptr = [kb.ps([128, 512], BF16, "ptr") for _ in range(2)]
        n_pm = 0
        n_sg = 0
        it = 0
        for s in range(NSEQ):
            oT, tm = cx.oT[s], cx.tm[s]
            for t in range(NT):
                k = it % 2
                it += 1
                ts_ = slice(t * 128, (t + 1) * 128)
                kb.dma("sp", oTt[k][:], oT.t[:, ts_].rearrange("(c p) t -> p c t", p=128), reads=[oT], writes=[oTt[k]], sembuf=oTt[k])
                kb.dma("sp", mgt[k][:], tm.t[ts_, C_MG:C_MG + 4 * D], reads=[tm], writes=[mgt[k]], sembuf=mgt[k])
                for n in range(4):
                    sg = sgm[n_sg % 2]
                    n_sg += 1
                    kb.op("act", lambda e: e.activation(out=sg[:], in_=mgt[k][:, n * D:(n + 1) * D], func=AF.Sigmoid), reads=[mgt[k]], writes=[sg])
                    for cb in range(4):
                        cs = slice(cb * 512, (cb + 1) * 512)
                        p = pm[n_pm % 4]
                        tp = tmp[n_pm % 4]
                        n_pm += 1
                        kb.mm(p[:], [(oTt[k][:, n * 4 + c, :], Wb[:, n * 4 + c, cs]) for c in range(4)], reads=[oTt[k], Wb], writes=[p])
                        mg_ = mergedb[k][cb]
                        if n == 0:
                            kb.op("dve", lambda e: e.tensor_tensor(out=mg_[:], in0=p[:], in1=sg[:, cs], op=ALU.mult), reads=[p, sg], writes=[mg_])
                        else:
                            kb.op("dve", lambda e: e.tensor_tensor(out=tp[:], in0=p[:], in1=sg[:, cs], op=ALU.mult), reads=[p, sg], writes=[tp])
                            kb.op("pool", lambda e: e.tensor_tensor(out=mg_[:], in0=mg_[:], in1=tp[:], op=ALU.add), reads=[mg_, tp], writes=[mg_])
                for cb in range(4):
                    kb.op("act", lambda e: e.activation(out=mbf[:, cb * 512:(cb + 1) * 512], in_=mergedb[k][cb][:], func=AF.Copy),
                          reads=[mergedb[k][cb]], writes=[mbf])
                ms = mst[k]
                for g4 in range(4):
                    pt = ptr[g4 % 2]
                    for j in range(4):
                        c = g4 * 4 + j
                        kb.op("pe", lambda e: e.transpose(out=pt[:, j * 128:(j + 1) * 128], in_=mbf[:, c * 128:(c + 1) * 128], identity=ident[:]),
                              reads=[mbf, ident], writes=[pt])
                    kb.op("act", lambda e: e.activation(out=ms[:, g4 * 4:(g4 + 1) * 4, :], in_=pt[:].rearrange("p (a b) -> p a b", a=4), func=AF.Copy),
                          reads=[pt], writes=[ms])
                c0 = s * S + t * 128
                kb.dma("act", mTs.t[:, c0:c0 + 128].rearrange("(c p) t -> p c t", p=128), ms[:], reads=[ms], writes=[mTs], sembuf=ms)
    with kb.phase():
        Wo = kb.sb([128, 16, D], BF16, "Wo")
        stg = [kb.sb([128, D], F32, "stg") for _ in range(2)]
        load_w_bf16(kb, Wo, cx.w_out.t[l], cx.w_out, stg)
        gp = kb.sb([128, D], F32, "gp")
        kb.dma("sp", gp[:], cx.post_norm_g.t[l:l + 1, :].broadcast_to([128, D]), reads=[cx.post_norm_g], writes=[gp], sembuf=gp)
        mT = [kb.sb([128, 16, 128], BF16, "mT") for _ in range(2)]
        ht = [kb.sb([128, D], F32, "ht") for _ in range(2)]
        hn = [kb.sb([128, D], F32, "hn") for _ in range(2)]
        t1 = [kb.sb([128, 512], F32, "t1") for _ in range(2)]
        jk = kb.sb([128, 512], BF16, "jk")
        sq = kb.sb([128, 2 * NT * NSEQ, 8], F32, "sq")
        py = [kb.ps([128, 512], F32, "py") for _ in range(8)]
        hin, hout = cx.h_in, cx.h_out
        it = 0
        for s in range(NSEQ):
            for t in range(NT):
                k = it % 2
                c0 = s * S + t * 128
                kb.dma("sp", mT[k][:], mTs.t[:, c0:c0 + 128].rearrange("(c p) t -> p c t", p=128), reads=[mTs], writes=[mT[k]], sembuf=mT[k])
                kb.dma("sp", ht[k][:], hin.t[c0:c0 + 128, :], reads=[hin], writes=[ht[k]], sembuf=ht[k])
                pys = py[k * 4:(k + 1) * 4]
                for cb in range(4):
                    cs = slice(cb * 512, (cb + 1) * 512)
                    kb.mm(pys[cb][:], [(mT[k][:, c, :], Wo[:, c, cs]) for c in range(16)], reads=[mT[k], Wo], writes=[pys[cb]])
                    kb.op("act", lambda e: e.activation(out=jk[:], in_=pys[cb][:], func=AF.Square, accum_out=sq[:, it, cb:cb + 1]),
                          reads=[pys[cb]], writes=[jk, sq])
                kb.op("dve", lambda e: e.tensor_reduce(out=sq[:, it, 4:5], in_=sq[:, it, 0:4], axis=AX.X, op=ALU.add), reads=[sq], writes=[sq])
                rstd_from_ss(kb, sq[:, it, 4:5], sq[:, it, 5:6], D, sq, sq)
                for cb in range(4):
                    cs = slice(cb * 512, (cb + 1) * 512)
                    tt = t1[cb % 2]
                    kb.op("dve", lambda e: e.scalar_tensor_tensor(out=tt[:], in0=pys[cb][:], scalar=sq[:, it, 5:6], in1=gp[:, cs], op0=ALU.mult, op1=ALU.mult),
                          reads=[pys[cb], sq, gp], writes=[tt])
                    kb.op("pool", lambda e: e.tensor_tensor(out=hn[k][:, cs], in0=tt[:], in1=ht[k][:, cs], op=ALU.add), reads=[tt, ht[k]], writes=[hn[k]])
                kb.dma("act", hout.t[c0:c0 + 128, :], hn[k][:], reads=[hn[k]], writes=[hout], sembuf=hn[k])
                it += 1
```
